# Optimizing a Trainium2 kernel written in Bass

```python
import math
import jax
import jax.numpy as jnp
from jax import lax
import numpy as np

D_MODEL = 2048
BATCH = 4
SEQ = 2048
DEPTH = 2
DEC_BATCH = 8
DEC_SEQ = 1
PAST_LEN = 16384
PAGE_SIZE = 128

HEAD_DIM = 128
MIX_WIDTH = D_MODEL
NSA_HEADS = MIX_WIDTH // (2 * HEAD_DIM)
NSA_KV = 2
NSA_HPG = NSA_HEADS // NSA_KV
GDN_HEADS = MIX_WIDTH // (2 * HEAD_DIM)
GDN_DK = HEAD_DIM
GDN_DV = HEAD_DIM
ROT_DIM = HEAD_DIM // 4
ROPE_THETA = 500000.0
BLOCK = 64
TOPN = 16
WINDOW = 512
CMP_HID = 256
NSA_QBLOCK = 64
CONV_W = 4
GDN_CHUNK = 64
D_FF = -(-8 * D_MODEL // (3 * 256)) * 256
EPS = 1e-6
NSA_Q_W = NSA_HEADS * HEAD_DIM
KV_W = NSA_KV * HEAD_DIM
GATE_W = 3 * NSA_HEADS
GDN_QK_W = GDN_HEADS * GDN_DK
GDN_V_W = GDN_HEADS * GDN_DV
CONV_CH = 2 * GDN_QK_W + GDN_V_W
IN_SPLITS = (NSA_Q_W, KV_W, KV_W, KV_W, KV_W, KV_W, KV_W, GATE_W, CONV_CH, GDN_HEADS, GDN_HEADS, GDN_V_W)
IN_W = sum(IN_SPLITS)

kernel_name = 'hymba_nsa_gdn_decode_step'


def rmsnorm(x, g):
    xf = x.astype(jnp.float32)
    y = xf * lax.rsqrt(jnp.mean(xf * xf, axis=-1, keepdims=True) + EPS) * g.astype(jnp.float32)
    return y.astype(x.dtype)


def l2norm(x):
    xf = x.astype(jnp.float32)
    return xf * lax.rsqrt(jnp.sum(xf * xf, axis=-1, keepdims=True) + EPS)


def rope(x, pos):
    half = ROT_DIM // 2
    inv = ROPE_THETA ** (-jnp.arange(half, dtype=jnp.float32) * 2.0 / ROT_DIM)
    ang = pos.astype(jnp.float32)[:, None] * inv
    cos, sin = jnp.cos(ang)[:, None, :], jnp.sin(ang)[:, None, :]
    xf = x.astype(jnp.float32)
    x1, x2, rest = xf[..., :half], xf[..., half:ROT_DIM], xf[..., ROT_DIM:]
    return jnp.concatenate([x1 * cos - x2 * sin, x2 * cos + x1 * sin, rest], -1).astype(x.dtype)


def masked_softmax(s, mask):
    s = jnp.where(mask, s.astype(jnp.float32), -jnp.inf)
    m = jnp.max(s, axis=-1, keepdims=True)
    m = jnp.where(jnp.isfinite(m), m, 0.0)
    e = jnp.where(mask, jnp.exp(s - m), 0.0)
    return e / jnp.maximum(jnp.sum(e, axis=-1, keepdims=True), 1e-30)


def kv_rows(k, v):
    return jnp.stack([k, v], axis=3).transpose(0, 2, 1, 3, 4)


def compress_blocks(rows, pe, w1, w2):
    B, L, G, D = rows.shape
    nb = L // BLOCK
    blk = rows.reshape(B, nb, BLOCK, G, D) + pe[None, None, :, None, :].astype(rows.dtype)
    blk = blk.transpose(0, 1, 3, 2, 4).reshape(B, nb, G, BLOCK * D)
    return jax.nn.gelu(blk @ w1) @ w2


def nsa_core(q, q_pos, gates, kc, vc, sel_gather, n_sel, kw, vw, kw_pos):
    B, Tq = q.shape[:2]
    scale = HEAD_DIM ** -0.5
    qf = q.reshape(B, Tq, NSA_KV, NSA_HPG, HEAD_DIM)
    nbc = kc.shape[1]
    j = jnp.arange(nbc)
    cmask = (j * BLOCK + (BLOCK - 1))[None, :] <= q_pos[:, None]
    s_c = jnp.einsum('btghd,bjgd->btghj', qf, kc).astype(jnp.float32) * scale
    p_c = masked_softmax(s_c, cmask[None, :, None, None, :])
    o_c = jnp.einsum('btghj,bjgd->btghd', p_c, vc)
    imp = jnp.sum(p_c, axis=3)
    cur = q_pos // BLOCK
    forced = (j[None] == 0) | (j[None] == cur[:, None]) | (j[None] == cur[:, None] - 1)
    future = j[None] > cur[:, None]
    score = jnp.where(forced[None, :, None, :], jnp.inf, jnp.where(future[None, :, None, :], -jnp.inf, imp))
    _, idx = lax.top_k(score, n_sel)
    ks, vs = sel_gather(idx)
    kpos = idx[..., None] * BLOCK + jnp.arange(BLOCK)
    smask = (kpos <= q_pos[None, :, None, None, None]).reshape(B, Tq, NSA_KV, 1, n_sel * BLOCK)
    s_s = jnp.einsum('btghd,btgnrd->btghnr', qf, ks).astype(jnp.float32) * scale
    p_s = masked_softmax(s_s.reshape(B, Tq, NSA_KV, NSA_HPG, n_sel * BLOCK), smask)
    o_s = jnp.einsum('btghm,btgmd->btghd', p_s, vs.reshape(B, Tq, NSA_KV, n_sel * BLOCK, HEAD_DIM))
    wmask = (kw_pos[None, :] <= q_pos[:, None]) & (kw_pos[None, :] >= q_pos[:, None] - WINDOW) & (kw_pos[None, :] >= 0)
    s_w = jnp.einsum('btghd,blgd->btghl', qf, kw).astype(jnp.float32) * scale
    p_w = masked_softmax(s_w, wmask[None, :, None, None, :])
    o_w = jnp.einsum('btghl,blgd->btghd', p_w, vw)
    g = gates.reshape(B, Tq, NSA_KV, NSA_HPG, 3)
    o = g[..., 0:1] * o_c + g[..., 1:2] * o_s + g[..., 2:3] * o_w
    return o.reshape(B, Tq, NSA_HEADS * HEAD_DIM).astype(q.dtype)


def gated_delta_chunked(q, k, v, g, beta, s0):
    B, T, H, _ = q.shape
    DV = v.shape[-1]
    C = GDN_CHUNK
    n = -(-T // C)
    pad = n * C - T

    def chunks(t):
        t = jnp.pad(t, ((0, 0), (0, pad)) + ((0, 0),) * (t.ndim - 2))
        t = t.reshape((B, n, C) + t.shape[2:])
        return jnp.moveaxis(t, 3, 1)

    q, k, v, g, beta = (chunks(t) for t in (q, k, v, g, beta))
    gc = jnp.cumsum(g, axis=-1)
    tri = jnp.tril(jnp.ones((C, C), bool))
    strict = jnp.tril(jnp.ones((C, C), bool), -1)
    decay = jnp.exp(jnp.where(tri, gc[..., :, None] - gc[..., None, :], -jnp.inf))
    kb = k * beta[..., None]
    L = jnp.where(strict, jnp.einsum('bhnid,bhnjd->bhnij', kb, k) * decay, 0.0)
    rhs = jnp.concatenate([v * beta[..., None], kb * jnp.exp(gc)[..., None]], axis=-1)
    sol = lax.linalg.triangular_solve(L + jnp.eye(C, dtype=L.dtype), rhs, left_side=True, lower=True, unit_diagonal=True)
    u, w = sol[..., :DV], sol[..., DV:]
    qk = jnp.where(tri, jnp.einsum('bhnid,bhnjd->bhnij', q, k) * decay, 0.0)

    def step(S, xs):
        q_c, k_c, u_c, w_c, g_c, qk_c = xs
        v_new = u_c - jnp.einsum('bhcd,bhde->bhce', w_c, S)
        o = jnp.einsum('bhcd,bhde->bhce', q_c * jnp.exp(g_c)[..., None], S) + jnp.einsum('bhij,bhje->bhie', qk_c, v_new)
        g_last = g_c[..., -1:]
        S = S * jnp.exp(g_last)[..., None] + jnp.einsum('bhcd,bhce->bhde', k_c * jnp.exp(g_last - g_c)[..., None], v_new)
        return S, o

    xs = tuple(jnp.moveaxis(t, 2, 0) for t in (q, k, u, w, gc, qk))
    S, o = lax.scan(step, s0, xs)
    o = o.transpose(1, 0, 3, 2, 4).reshape(B, n * C, H, DV)[:, :T]
    return o, S


def gdn_mixer(conv_in, a, b, z, conv_buf, s0, conv_w, a_log, dt_bias, norm_w):
    B, T, _ = conv_in.shape
    xc = jnp.concatenate([conv_buf.astype(conv_in.dtype), conv_in], axis=1)
    y = lax.conv_general_dilated(xc, conv_w.astype(xc.dtype)[:, None, :], window_strides=(1,), padding='VALID',
                                 dimension_numbers=('NWC', 'WIO', 'NWC'), feature_group_count=CONV_CH)
    y = jax.nn.silu(y)
    q, k, v = jnp.split(y, [GDN_QK_W, 2 * GDN_QK_W], axis=-1)
    q = l2norm(q.reshape(B, T, GDN_HEADS, GDN_DK)) * (GDN_DK ** -0.5)
    k = l2norm(k.reshape(B, T, GDN_HEADS, GDN_DK))
    v = v.reshape(B, T, GDN_HEADS, GDN_DV).astype(jnp.float32)
    beta = jax.nn.sigmoid(b.astype(jnp.float32))
    g = -jnp.exp(a_log.astype(jnp.float32)) * jax.nn.softplus(a.astype(jnp.float32) + dt_bias.astype(jnp.float32))
    o, S = gated_delta_chunked(q, k, v, g, beta, s0.astype(jnp.float32))
    o = rmsnorm(o, norm_w) * jax.nn.silu(z.reshape(B, T, GDN_HEADS, GDN_DV).astype(jnp.float32))
    return o.reshape(B, T, GDN_V_W).astype(conv_in.dtype), S.astype(conv_in.dtype), xc[:, -(CONV_W - 1):]


def run_trunk(x, c, pos, nsa_fn, gdn_fn, w_ada, b_ada, g_pre_mix, w_in, w_out, g_post_mix, g_pre_ffn, w_gate, w_up, w_down, g_post_ffn):
    B, T, _ = x.shape
    offs = np.cumsum(IN_SPLITS)[:-1].tolist()
    states = []
    for l in range(DEPTH):
        mod = jax.nn.silu(c) @ w_ada[l] + b_ada[l]
        sh1, sc1, ga1, sh2, sc2, ga2 = jnp.split(mod[:, None, :], 6, axis=-1)
        h = rmsnorm(x, g_pre_mix[l]) * (1 + sc1) + sh1
        q, kc, vc, ks, vs, kw, vw, gl, conv_in, a, bb, z = jnp.split(h @ w_in[l], offs, axis=-1)
        q = rope(q.reshape(B, T, NSA_HEADS, HEAD_DIM), pos)
        kc, ks, kw = (rope(t.reshape(B, T, NSA_KV, HEAD_DIM), pos) for t in (kc, ks, kw))
        vc, vs, vw = (t.reshape(B, T, NSA_KV, HEAD_DIM) for t in (vc, vs, vw))
        gates = jax.nn.sigmoid(gl.astype(jnp.float32)).reshape(B, T, NSA_HEADS, 3)
        o_nsa, nsa_state = nsa_fn(l, q, gates, kc, vc, ks, vs, kw, vw)
        o_gdn, gdn_state = gdn_fn(l, conv_in, a, bb, z)
        mix = jnp.concatenate([o_nsa, o_gdn], axis=-1) @ w_out[l]
        x = x + ga1 * rmsnorm(mix, g_post_mix[l])
        h = rmsnorm(x, g_pre_ffn[l]) * (1 + sc2) + sh2
        f = (jax.nn.silu(h @ w_gate[l]) * (h @ w_up[l])) @ w_down[l]
        x = x + ga2 * rmsnorm(f, g_post_ffn[l])
        states.append(nsa_state + gdn_state)
    stacked = [jnp.stack(s, axis=1) for s in zip(*states)]
    return x, stacked


def setup_inputs(seed: int = 0) -> dict:
    key = jax.random.key(seed)
    keys = iter(jax.random.split(key, 40))

    def nrm(shape, scale):
        return jax.random.normal(next(keys), shape, jnp.float32) * scale

    def gain(width=D_MODEL):
        return 1.0 + nrm((DEPTH, width), 0.05)

    n_pages = PAST_LEN // PAGE_SIZE
    n_pool = (5 * DEC_BATCH * n_pages + 3) // 4
    w_buf = min(WINDOW, PAST_LEN)
    perm = jax.random.permutation(next(keys), n_pool)
    page_table = perm[:DEC_BATCH * n_pages].reshape(DEC_BATCH, n_pages).astype(jnp.int32)
    dt = jnp.exp(jax.random.uniform(next(keys), (DEPTH, GDN_HEADS), jnp.float32, math.log(1e-3), math.log(1e-1)))
    return {
        'x_prompt': nrm((BATCH, SEQ, D_MODEL), 1.0),
        'x_sample': nrm((DEC_BATCH, DEC_SEQ, D_MODEL), 1.0),
        'cache_cmp_kv': nrm((n_pool, DEPTH, NSA_KV, PAGE_SIZE, 2, HEAD_DIM), 1.0),
        'cache_sel_kv': nrm((n_pool, DEPTH, NSA_KV, PAGE_SIZE, 2, HEAD_DIM), 1.0),
        'cache_win_kv': nrm((DEC_BATCH, DEPTH, NSA_KV, w_buf, 2, HEAD_DIM), 1.0),
        'state_gdn': nrm((DEC_BATCH, DEPTH, GDN_HEADS, GDN_DK, GDN_DV), GDN_DK ** -0.5),
        'state_conv': nrm((DEC_BATCH, DEPTH, CONV_W - 1, CONV_CH), 1.0),
        'page_table': page_table,
        'c_prompt': nrm((BATCH, D_MODEL), 1.0),
        'c_sample': nrm((DEC_BATCH, D_MODEL), 1.0),
        'w_ada': nrm((DEPTH, D_MODEL, 6 * D_MODEL), 0.5 * D_MODEL ** -0.5),
        'b_ada': nrm((DEPTH, 6 * D_MODEL), 0.02),
        'g_pre_mix': gain(),
        'w_in': nrm((DEPTH, D_MODEL, IN_W), D_MODEL ** -0.5),
        'cmp_pe': nrm((DEPTH, 2, BLOCK, HEAD_DIM), 0.1),
        'cmp_w1': nrm((DEPTH, 2, BLOCK * HEAD_DIM, CMP_HID), (BLOCK * HEAD_DIM) ** -0.5),
        'cmp_w2': nrm((DEPTH, 2, CMP_HID, HEAD_DIM), CMP_HID ** -0.5),
        'conv_w': nrm((DEPTH, CONV_W, CONV_CH), CONV_W ** -0.5),
        'gdn_a_log': jnp.log(jax.random.uniform(next(keys), (DEPTH, GDN_HEADS), jnp.float32, 1.0, 16.0)),
        'gdn_dt_bias': dt + jnp.log(-jnp.expm1(-dt)),
        'gdn_norm': gain(GDN_DV),
        'w_out': nrm((DEPTH, MIX_WIDTH, D_MODEL), MIX_WIDTH ** -0.5),
        'g_post_mix': gain(),
        'g_pre_ffn': gain(),
        'w_gate': nrm((DEPTH, D_MODEL, D_FF), D_MODEL ** -0.5),
        'w_up': nrm((DEPTH, D_MODEL, D_FF), D_MODEL ** -0.5),
        'w_down': nrm((DEPTH, D_FF, D_MODEL), D_FF ** -0.5),
        'g_post_ffn': gain(),
    }


def reference(x_prompt, x_sample, cache_cmp_kv, cache_sel_kv, cache_win_kv, state_gdn, state_conv, page_table,
              c_prompt, c_sample, w_ada, b_ada, g_pre_mix, w_in, cmp_pe, cmp_w1, cmp_w2, conv_w, gdn_a_log,
              gdn_dt_bias, gdn_norm, w_out, g_post_mix, g_pre_ffn, w_gate, w_up, w_down, g_post_ffn):
    bi_kv = jnp.arange(NSA_KV)[None, None, :, None]

    def comp(l, rows, i):
        return compress_blocks(rows, cmp_pe[l, i], cmp_w1[l, i], cmp_w2[l, i])

    def gdn_run(l, conv_in, a, b, z, buf0, s0):
        o, S, buf = gdn_mixer(conv_in, a, b, z, buf0, s0, conv_w[l], gdn_a_log[l], gdn_dt_bias[l], gdn_norm[l])
        return o, (S, buf)

    def nsa_prompt(l, q, gates, kc, vc, ks, vs, kw, vw):
        B, S = q.shape[:2]
        nb = S // BLOCK
        kcb, vcb = comp(l, kc, 0), comp(l, vc, 1)
        ks_store = ks.reshape(B, nb, BLOCK, NSA_KV, HEAD_DIM).transpose(0, 3, 1, 2, 4)
        vs_store = vs.reshape(B, nb, BLOCK, NSA_KV, HEAD_DIM).transpose(0, 3, 1, 2, 4)
        bi = jnp.arange(B)[:, None, None, None]

        def gather(idx):
            return ks_store[bi, bi_kv, idx], vs_store[bi, bi_kv, idx]

        kw_pad = jnp.pad(kw, ((0, 0), (WINDOW, 0), (0, 0), (0, 0)))
        vw_pad = jnp.pad(vw, ((0, 0), (WINDOW, 0), (0, 0), (0, 0)))
        n_sel = min(TOPN, nb)

        def step(i):
            q0 = i * NSA_QBLOCK
            qb = lax.dynamic_slice_in_dim(q, q0, NSA_QBLOCK, axis=1)
            gb = lax.dynamic_slice_in_dim(gates, q0, NSA_QBLOCK, axis=1)
            kwb = lax.dynamic_slice_in_dim(kw_pad, q0, WINDOW + NSA_QBLOCK, axis=1)
            vwb = lax.dynamic_slice_in_dim(vw_pad, q0, WINDOW + NSA_QBLOCK, axis=1)
            q_pos = q0 + jnp.arange(NSA_QBLOCK)
            kw_pos = q0 - WINDOW + jnp.arange(WINDOW + NSA_QBLOCK)
            return nsa_core(qb, q_pos, gb, kcb, vcb, gather, n_sel, kwb, vwb, kw_pos)

        o = lax.map(step, jnp.arange(S // NSA_QBLOCK))
        o = jnp.moveaxis(o, 0, 1).reshape(B, S, NSA_Q_W)
        wl = min(WINDOW, S)
        return o, (kv_rows(kc, vc), kv_rows(ks, vs), kv_rows(kw[:, S - wl:], vw[:, S - wl:]))

    def gdn_prompt(l, conv_in, a, b, z):
        B = conv_in.shape[0]
        buf0 = jnp.zeros((B, CONV_W - 1, CONV_CH), conv_in.dtype)
        s0 = jnp.zeros((B, GDN_HEADS, GDN_DK, GDN_DV), jnp.float32)
        return gdn_run(l, conv_in, a, b, z, buf0, s0)

    def nsa_sample(l, q, gates, kc, vc, ks, vs, kw, vw):
        B, T = q.shape[:2]
        npb = PAST_LEN // BLOCK
        nb_new = -(-T // BLOCK)
        pad = nb_new * BLOCK - T
        bpp = PAGE_SIZE // BLOCK
        bi = jnp.arange(B)[:, None, None, None]

        def pad_rows(t):
            return jnp.pad(t, ((0, 0), (0, pad), (0, 0), (0, 0)))

        pages = cache_cmp_kv.reshape(-1, NSA_KV, PAGE_SIZE, 2, HEAD_DIM)[page_table * DEPTH + l]
        past = pages.transpose(0, 1, 3, 4, 2, 5).reshape(B, PAST_LEN, 2, NSA_KV, HEAD_DIM).astype(kc.dtype)
        kcb = comp(l, jnp.concatenate([past[:, :, 0], pad_rows(kc)], axis=1), 0)
        vcb = comp(l, jnp.concatenate([past[:, :, 1], pad_rows(vc)], axis=1), 1)
        sel_flat = cache_sel_kv.reshape(-1, BLOCK, 2, HEAD_DIM)
        new_store = jnp.stack([pad_rows(ks), pad_rows(vs)], axis=3)
        new_store = new_store.reshape(B, nb_new, BLOCK, NSA_KV, 2, HEAD_DIM).transpose(0, 3, 1, 2, 4, 5)

        def gather(idx):
            in_past = idx < npb
            jp = jnp.minimum(idx, npb - 1)
            phys = page_table[bi, jp // bpp]
            lin = ((phys * DEPTH + l) * NSA_KV + bi_kv) * bpp + jp % bpp
            kv_past = sel_flat[lin].astype(new_store.dtype)
            kv_new = new_store[bi, bi_kv, jnp.clip(idx - npb, 0, nb_new - 1)]
            kv = jnp.where(in_past[..., None, None, None], kv_past, kv_new)
            return kv[..., 0, :], kv[..., 1, :]

        win = cache_win_kv[:, l]
        w_buf = win.shape[2]
        kw_all = jnp.concatenate([win[..., 0, :].transpose(0, 2, 1, 3).astype(kw.dtype), kw], axis=1)
        vw_all = jnp.concatenate([win[..., 1, :].transpose(0, 2, 1, 3).astype(vw.dtype), vw], axis=1)
        kw_pos = PAST_LEN - w_buf + jnp.arange(w_buf + T)
        q_pos = PAST_LEN + jnp.arange(T)
        o = nsa_core(q, q_pos, gates, kcb, vcb, gather, min(TOPN, npb + nb_new), kw_all, vw_all, kw_pos)
        return o, (kv_rows(kc, vc), kv_rows(ks, vs), kv_rows(kw_all[:, -w_buf:], vw_all[:, -w_buf:]))

    def gdn_sample(l, conv_in, a, b, z):
        return gdn_run(l, conv_in, a, b, z, state_conv[:, l], state_gdn[:, l])

    y_prompt, p_states = run_trunk(x_prompt, c_prompt, jnp.arange(x_prompt.shape[1]), nsa_prompt, gdn_prompt,
                                   w_ada, b_ada, g_pre_mix, w_in, w_out, g_post_mix, g_pre_ffn, w_gate, w_up, w_down, g_post_ffn)
    y_sample, s_states = run_trunk(x_sample, c_sample, PAST_LEN + jnp.arange(x_sample.shape[1]), nsa_sample, gdn_sample,
                                   w_ada, b_ada, g_pre_mix, w_in, w_out, g_post_mix, g_pre_ffn, w_gate, w_up, w_down, g_post_ffn)
    p_cmp, p_sel, p_win, p_gdn, p_conv = p_states
    s_cmp, s_sel, s_win, s_gdn, s_conv = s_states
    return (y_prompt, y_sample, p_cmp, p_sel, p_win, p_gdn, p_conv, s_cmp, s_sel, s_win, s_gdn, s_conv)
```

```python
import numpy as np
import concourse.bass as bass
import concourse.mybir as mybir
from concourse.bass_utils import run_bass_kernel_spmd

F32 = mybir.dt.float32
BF16 = mybir.dt.bfloat16
I32 = mybir.dt.int32
AF = mybir.ActivationFunctionType
ALU = mybir.AluOpType
AX = mybir.AxisListType

ENGS = ("pe", "act", "dve", "pool", "sp")

D = 2048
T = 2048
NT = T // 128
DEPTH = 2
DFF = 5632
INW = 6696
NCH = D // 128
EPS = 1e-6
TG = 512
NG = T // TG
O_Q, O_KC, O_VC, O_KS, O_VS, O_KW, O_VW, O_GL, O_CONV, O_A, O_B, O_Z = (
    0, 1024, 1280, 1536, 1792, 2048, 2304, 2560, 2584, 5656, 5664, 5672)


def _isz(dt):
    return mybir.dt.size(dt)


class Sched:
    def __init__(self, nc, n_dma=16):
        self.nc = nc
        self.prog = {e: [] for e in ENGS}
        self.esem = {e: nc.alloc_semaphore("es_" + e) for e in ENGS}
        self.ecnt = {e: 0 for e in ENGS}
        self.known = {e: {} for e in ENGS}
        self.pool = {q: [[nc.alloc_semaphore("ds_%s%d" % (q, i)), 0] for i in range(n_dma)]
                     for q in ("sp", "pool")}
        self.pidx = {q: 0 for q in self.pool}
        self.recs = {}
        self.readonly = set()
        self.bank_granular = set()
        self.fsize = {}
        self.n_inst = 0

    def sb(self, name, shape, dt):
        t = self.nc.alloc_sbuf_tensor(name, list(shape), dt)
        self.fsize[name] = int(np.prod(shape[1:])) * _isz(dt)
        return t

    def ps(self, name, shape, dt=F32):
        t = self.nc.alloc_psum_tensor(name, list(shape), dt)
        self.fsize[name] = int(np.prod(shape[1:])) * _isz(dt)
        self.bank_granular.add(name)
        return t

    def region(self, ap):
        name = ap.name
        steps = ap.ap
        es = _isz(ap.dtype)
        off = ap.offset * es
        if name in self.fsize:
            F = self.fsize[name]
            if name in self.bank_granular:
                return name, 0, 128, 0, F
            p0 = off // F
            f0 = off % F
            pc = steps[0][1]
            ext = es
            for s, c in steps[1:]:
                ext += (c - 1) * abs(s) * es
            return name, p0, p0 + pc, f0, f0 + ext
        ext = es
        for s, c in steps:
            ext += (c - 1) * abs(s) * es
        return name, 0, 1, off, off + ext

    def _deps_and_record(self, reads, writes, tag):
        deps = []
        for is_w, aps in ((False, reads), (True, writes)):
            for ap in aps:
                if ap.name in self.readonly:
                    continue
                name, p0, p1, f0, f1 = self.region(ap)
                lst = self.recs.get(name, ())
                keep = []
                psum = name in self.bank_granular
                for r in lst:
                    rp0, rp1, rf0, rf1, rw, rtag = r
                    ov = not (rp1 <= p0 or p1 <= rp0 or rf1 <= f0 or f1 <= rf0)
                    if ov and (rw or is_w or (psum and rtag[2] != tag[2])):
                        deps.append(rtag)
                    contained = rp0 >= p0 and rp1 <= p1 and rf0 >= f0 and rf1 <= f1
                    if contained and (is_w or ((not rw) and rtag[2] == tag[2] and not tag[2].startswith("dma"))):
                        continue
                    keep.append(r)
                keep.append((p0, p1, f0, f1, is_w, tag))
                self.recs[name] = keep
        return deps

    def _waits(self, eng, deps):
        waits = []
        kn = self.known[eng]
        for sem, val, src in deps:
            if src == eng and eng == "pe":
                continue
            key = sem.num
            if kn.get(key, 0) >= val:
                continue
            kn[key] = val
            waits.append((sem, val))
        return waits

    def op(self, eng, fn, reads=(), writes=()):
        self.ecnt[eng] += 1
        sem = self.esem[eng]
        tag = (sem, self.ecnt[eng], eng)
        deps = self._deps_and_record(reads, writes, tag)
        deps = [d for d in deps if d is not tag]
        waits = self._waits(eng, deps)
        import sys as _sys
        fr = _sys._getframe(1)
        ln = []
        while fr is not None and len(ln) < 4:
            ln.append(fr.f_lineno)
            fr = fr.f_back
        self.prog[eng].append((waits, fn, sem, 1, ln))
        self.n_inst += 1

    def dma(self, q, out, in_, **kw):
        pool = self.pool[q]
        slot = pool[self.pidx[q] % len(pool)]
        self.pidx[q] += 1
        deps = []
        if slot[1] > 0:
            deps.append((slot[0], slot[1], "dma:" + q))
        slot[1] += 16
        tag = (slot[0], slot[1], "dma:" + q)
        deps += self._deps_and_record([in_], [out], tag)
        deps = [d for d in deps if d is not tag]
        waits = self._waits(q, deps)
        self.prog[q].append((waits, (lambda e, o=out, i=in_, k=kw: e.dma_start(out=o, in_=i, **k)), slot[0], 16))
        self.n_inst += 1

    def mm(self, out, lhsT, rhs, start=True, stop=True):
        self.op("pe", lambda e: e.matmul(out, lhsT=lhsT, rhs=rhs, start=start, stop=stop),
                reads=[lhsT, rhs], writes=[out])

    def tr(self, out, in_, ident):
        self.op("pe", lambda e: e.matmul(out, lhsT=in_, rhs=ident, start=True, stop=True),
                reads=[in_, ident], writes=[out])

    def act(self, out, in_, func, scale=1.0, bias=None, accum=None):
        rd = [in_]
        wr = [out]
        kw = {}
        if bias is not None:
            kw["bias"] = bias
            if not isinstance(bias, float):
                rd.append(bias)
        if not isinstance(scale, float):
            rd.append(scale)
        if accum is not None:
            kw["accum_out"] = accum
            wr.append(accum)
        self.op("act", lambda e: e.activation(out=out, in_=in_, func=func, scale=scale, **kw), reads=rd, writes=wr)

    def ts(self, out, in0, s1, s2=None, op0=ALU.mult, op1=None, eng="dve"):
        rd = [in0] + [s for s in (s1, s2) if s is not None and not isinstance(s, float)]
        kw = {}
        if op1 is not None:
            kw["op1"] = op1
        self.op(eng, lambda e: e.tensor_scalar(out=out, in0=in0, scalar1=s1, scalar2=s2, op0=op0, **kw),
                reads=rd, writes=[out])

    def tt(self, out, in0, in1, op, eng="dve"):
        self.op(eng, lambda e: e.tensor_tensor(out=out, in0=in0, in1=in1, op=op), reads=[in0, in1], writes=[out])

    def stt(self, out, in0, scalar, in1, op0, op1, eng="dve"):
        rd = [in0, in1] + ([] if isinstance(scalar, float) else [scalar])
        self.op(eng, lambda e: e.scalar_tensor_tensor(out=out, in0=in0, scalar=scalar, in1=in1, op0=op0, op1=op1),
                reads=rd, writes=[out])

    def cp(self, out, in_, eng="dve"):
        self.op(eng, lambda e: e.tensor_copy(out=out, in_=in_), reads=[in_], writes=[out])

    def memset(self, out, val, eng="pool"):
        self.op(eng, lambda e: e.memset(out, val), writes=[out])

    def finish(self):
        fin = []
        for q in self.pool:
            for sem, val in self.pool[q]:
                if val > 0:
                    fin.append((sem, val))
        nc = self.nc
        prog = self.prog

        def run(lst, e, extra=()):
            for item in lst:
                waits, fn, sem, inc = item[:4]
                for s, v in waits:
                    e.wait_ge(s, v)
                try:
                    ins = fn(e)
                except Exception:
                    print("EMIT FAIL at lines", item[4] if len(item) > 4 else None, flush=True)
                    raise
                ins.then_inc(sem, inc)
            for s, v in extra:
                e.wait_ge(s, v)

        with nc.Block() as block:
            @block.tensor
            def _(e):
                run(prog["pe"], e)

            @block.scalar
            def _(e):
                run(prog["act"], e)

            @block.vector
            def _(e):
                run(prog["dve"], e)

            @block.gpsimd
            def _(e):
                run(prog["pool"], e)

            @block.sync
            def _(e):
                run(prog["sp"], e, fin)


class _Stop(Exception):
    pass


def build_program(cfg):
    nc = bass.Bass("TRN2", target_bir_lowering=False)
    S = Sched(nc)
    try:
        _build_body(nc, S, cfg)
    except _Stop:
        pass
    S.finish()
    return nc, S


def _build_body(nc, S, cfg):
    def chk(name):
        if cfg.get("stop") == name:
            raise _Stop()

    def din(name, shape, dt=F32):
        S.readonly.add(name)
        return nc.dram_tensor(name, list(shape), dt, kind="ExternalInput").ap()

    def dout(name, shape, dt=F32):
        return nc.dram_tensor(name, list(shape), dt, kind="ExternalOutput").ap()

    def dscr(name, shape, dt=F32):
        return nc.dram_tensor(name, list(shape), dt, kind="Internal").ap()

    x_in = din("x_p", [T, D])
    cT_in = din("cT", [128, NCH, 2])
    bada_in = din("badaT", [128, DEPTH, 96])
    gains_in = din("gainsT", [128, 4, DEPTH, NCH])
    w_ada = din("w_ada", [DEPTH, D, 6 * D])
    w_in = din("w_in", [DEPTH, D, INW])
    w_out = din("w_out", [DEPTH, D, D])
    w_gate = din("w_gate", [DEPTH, D, DFF])
    w_up = din("w_up", [DEPTH, D, DFF])
    w_down = din("w_down", [DEPTH, DFF, D])
    consts_in = din("consts", [128, 3, 128])
    ropeFM_in = din("ropeFM", [128, 2, T])
    xsT_in = din("xsT", [128, NCH])
    ptab_in = din("ptab", [1, 128], I32)
    iota_in = din("iota", [128, 1], I32)
    cache_cmp = din("cache_cmp", [1280 * 2 * 2 * 128, 256])
    cache_sel = din("cache_sel", [1280 * 2 * 2 * 128, 256])
    cache_win = din("cache_win", [DEPTH, 2, 512, 256])
    sgdn_in = din("sgdn", [DEPTH, 8, 128, 128])
    sconv_in = din("sconv", [DEPTH, 3, 3072])
    sconvT_in = din("sconvT", [DEPTH, 128, 24, 3])
    ropeS_in = din("ropeS", [1, 2, 16])
    dsel_in = din("dsel", [128, 2, 2, 128])
    hsel_in = din("hsel", [128, 2, 128])
    gconst_in = din("gconst", [64, 5, 128])
    convwT_in = din("convwT", [128, DEPTH, 24, 4])
    gvec_in = din("gvec", [64, DEPTH, 2, 8])
    gnorm_in = din("gnormb", [64, DEPTH, 128])
    cmp_peT_in = din("cmp_peT", [DEPTH, 128, 2, 64])
    cmp_w1 = din("cmp_w1", [DEPTH, 2, 8192, 256])
    cmp_w2 = din("cmp_w2", [DEPTH, 2, 256, 128])
    tabA_in = din("tabA", [128, 3, NT, 32])
    cm01T_in = din("cm01T", [32, T])
    eexp_in = din("eexp", [32, 16, 128], BF16)
    gsel_in = din("gsel", [24, 24, 128])
    caus_in = din("caus", [128, 4, 512], BF16)
    wmask_in = din("wmask", [128, 8, 512], BF16)

    y_p = dout("y_p", [T, D])
    p_cmp = dout("p_cmp", [DEPTH, 2, T, 2, 128])
    p_sel = dout("p_sel", [DEPTH, 2, T, 2, 128])
    p_win = dout("p_win", [DEPTH, 2, 512, 2, 128])
    p_conv = dout("p_conv", [DEPTH, 3, 3072])
    p_gdn = dout("p_gdn", [DEPTH, 8, 128, 128])
    y_s = dout("y_s", [D])
    s_cmp = dout("s_cmp", [DEPTH, 2, 1, 2, 128])
    s_sel = dout("s_sel", [DEPTH, 2, 1, 2, 128])
    s_win = dout("s_win", [DEPTH, 2, 512, 256])
    s_gdn = dout("s_gdn", [DEPTH, 8, 128, 128])
    s_conv = dout("s_conv", [DEPTH, 3, 3072])
    gscr = dscr("gscr", [24])
    dbg_s = dout("dbg_s", [128, 64]) if cfg.get("dbg_mix") else None

    xbuf = dscr("xbuf", [T, D])
    xbuf2 = dscr("xbuf2", [T, D])
    mixT = (dout if cfg.get("dbg_mix") else dscr)("mixT", [128, 16, T], BF16)
    cin = dscr("cin", [24, 128, T])
    abz = dscr("abz", [T, 16 + 1024])

    cst = S.sb("cst", [128, 3, 128], F32)
    ident = cst[:, 0, :]
    rotT = cst[:, 1, :]
    ones = cst[:, 2, :]
    small = S.sb("small", [128, 1024], F32)
    cs = small[:, 0:32].rearrange("p (c n) -> p c n", n=2)
    bada = small[:, 32:224].rearrange("p (l n) -> p l n", l=DEPTH)
    gains = small[:, 224:352].rearrange("p (k l c) -> p k l c", k=4, l=DEPTH)
    modT = small[:, 352:736].rearrange("p (l n s) -> p l n s", l=DEPTH, s=2)
    eff = small[:, 736:928].rearrange("p (k c s) -> p k c s", k=6, s=2)
    stat = small[:, 928:1024]
    epsb = stat[:, 0:1]
    gbc = S.sb("gbc", [128, D], F32)
    xsam = S.sb("xsam", [128, NCH], F32)
    sm2 = S.sb("sm2", [128, 256], F32)
    pti = S.sb("pti", [128, 132], I32)
    xt = S.sb("xt", [128, D], F32)
    xt2 = S.sb("xt2", [128, D], F32)
    actA = S.sb("actA", [128, NCH, TG], BF16)
    actB = S.sb("actB", [128, NCH, TG], BF16)
    wb = [S.sb("wb%d" % i, [128, 8192], BF16) for i in range(2)]
    stg = [S.sb("stg%d" % i, [128, 512], F32) for i in range(2)]
    ARENA = int(cfg.get('arena_kb', 104)) * 1024
    arena = S.sb("arena", [128, ARENA // 4], F32)
    PS = [S.ps("ps%d" % i, [128, 512]) for i in range(8)]
    psi = [0]

    def nps():
        psi[0] += 1
        return PS[psi[0] % 8]

    def carve(off_bytes, shape, dt):
        n = int(np.prod(shape[1:]))
        e0 = off_bytes // 4
        e1 = e0 + (n * _isz(dt) + 3) // 4
        v = arena[:, e0:e1]
        if dt != F32:
            v = v.bitcast(dt)
        if len(shape) == 3:
            v = v.rearrange("p (a b) -> p a b", a=shape[1])
        elif len(shape) == 4:
            v = v.rearrange("p (a b c) -> p a b c", a=shape[1], b=shape[2])
        return v

    KB = 1024
    qT = carve(0, [128, 8, T], BF16)
    kT = carve(32 * KB, [128, 6, T], BF16)
    vcT = carve(56 * KB, [128, 2, T], BF16)
    vtm = carve(64 * KB, [128, NT, 4, 128], BF16)
    gatT = carve(80 * KB, [128, T], F32)
    ropeT = carve((88 if ARENA >= 104 * KB else 72) * KB, [128, 2, T], F32)
    aT = carve(0, [128, 44, TG], BF16)
    fsb = carve(44 * KB, [128, 4, D], F32)

    wq = ["pool"]

    def wload(buf_view, dram_view, nsplit=2, q="pool"):
        kc = dram_view.shape[1]
        step = max(1, kc // nsplit)
        for k0 in range(0, kc, step):
            k1 = min(kc, k0 + step)
            S.dma(q, buf_view[:, k0:k1, :], dram_view[:, k0:k1, :])

    S.dma("sp", cst[:], consts_in)
    S.dma("sp", cs, cT_in)
    S.dma("sp", bada, bada_in)
    S.dma("sp", gains, gains_in)
    S.memset(epsb, EPS)

    S.act(cs, cs, AF.Silu)
    wi = 0
    for l in range(DEPTH):
        wsrc = w_ada[l].rearrange("(c p) n -> p c n", p=128)
        for n in range(96):
            buf = wb[wi % 2]
            wi += 1
            wv = buf[:, 0:4096].bitcast(F32).rearrange("p (c n) -> p c n", c=NCH)
            wload(wv, wsrc[:, :, n * 128:(n + 1) * 128], nsplit=2, q="sp")
            pt = nps()
            for kc in range(NCH):
                S.mm(pt[:, 0:2], lhsT=wv[:, kc, :], rhs=cs[:, kc, :], start=(kc == 0), stop=(kc == NCH - 1))
            S.act(modT[:, l, n, :], pt[:, 0:2], AF.Identity, bias=bada[:, l, n:n + 1])

    chk("ada")

    def layer_eff(l):
        for s in range(2):
            for (k, sc_off, gk) in ((0, 16, 0), (2, 64, 2)):
                S.ts(eff[:, k, :, s], modT[:, l, sc_off:sc_off + 16, s], 1.0, None, op0=ALU.add)
                S.tt(eff[:, k, :, s], eff[:, k, :, s], gains[:, gk, l, :], ALU.mult)
            S.cp(eff[:, 1, :, s], modT[:, l, 0:16, s])
            S.cp(eff[:, 3, :, s], modT[:, l, 48:64, s])
            S.tt(eff[:, 4, :, s], modT[:, l, 32:48, s], gains[:, 1, l, :], ALU.mult)
            S.tt(eff[:, 5, :, s], modT[:, l, 80:96, s], gains[:, 3, l, :], ALU.mult)

    def make_gbc(k, s):
        for c4 in range(4):
            pt = nps()
            for cc in range(4):
                c = c4 * 4 + cc
                tmp = stg[0][:, cc * 128:(cc + 1) * 128]
                S.ts(tmp, ident, eff[:, k, c, s:s + 1], None, op0=ALU.mult)
                S.mm(pt[:, cc * 128:(cc + 1) * 128], lhsT=ones, rhs=tmp)
            S.cp(gbc[:, c4 * 512:(c4 + 1) * 512], pt[:], eng="dve")

    def rstd_of(ssum, n):
        S.act(ssum, ssum, AF.Ln, scale=1.0 / n, bias=epsb)
        S.act(ssum, ssum, AF.Exp, scale=-0.5)

    def norm_to_fm(xtile, dst, col0, ka, kb, s):
        ss = stat[:, 1:2]
        junk = xt2
        S.memset(ss, 0.0)
        S.act(junk[:], xtile[:], AF.Square, accum=ss)
        rstd_of(ss, D)
        if cfg.get("nstop") == 1:
            raise _Stop()
        S.ts(junk[:], xtile[:], ss, None, op0=ALU.mult)
        if cfg.get("nstop") == 2:
            raise _Stop()
        for c4 in range(4):
            pt = nps()
            for cc in range(4):
                c = c4 * 4 + cc
                S.tr(pt[:, cc * 128:(cc + 1) * 128], junk[:, c * 128:(c + 1) * 128], ident)
            if cfg.get("nstop") == 3:
                raise _Stop()
            for cc in range(4):
                c = c4 * 4 + cc
                S.ts(dst[:, c, col0:col0 + 128], pt[:, cc * 128:(cc + 1) * 128],
                     eff[:, ka, c, s:s + 1], eff[:, kb, c, s:s + 1], op0=ALU.mult, op1=ALU.add)
            if cfg.get("nstop") == 4:
                raise _Stop()
        if cfg.get("nstop") == 5:
            raise _Stop()

    def post_norm_residual(banks, xtile, out_tile):
        ss4 = stat[:, 4:8]
        S.memset(ss4, 0.0)
        for j in range(4):
            S.act(stg[1][:], banks[j][:], AF.Square, accum=ss4[:, j:j + 1])
        ss = stat[:, 2:3]
        S.op("dve", lambda e: e.reduce_sum(out=ss, in_=ss4, axis=AX.X), reads=[ss4], writes=[ss])
        rstd_of(ss, D)
        for j in range(4):
            sl = slice(j * 512, (j + 1) * 512)
            S.stt(xt2[:, sl], banks[j][:], ss, gbc[:, sl], ALU.mult, ALU.mult)
            S.tt(out_tile[:, sl], xt2[:, sl], xtile[:, sl], ALU.add, eng="pool")


    def fview(base, dt_base, off_b, shape, dt):
        flat = base[:]
        if len(base.shape) == 3:
            flat = flat.rearrange("p a b -> p (a b)")
        es = _isz(dt_base)
        n = int(np.prod(shape[1:])) * _isz(dt)
        v = flat[:, off_b // es:(off_b + n) // es]
        if dt != dt_base:
            v = v.bitcast(dt)
        if len(shape) == 3:
            v = v.rearrange("p (a b) -> p a b", a=shape[1])
        elif len(shape) == 4:
            v = v.rearrange("p (a b c) -> p a b c", a=shape[1], b=shape[2])
        return v[0:shape[0]]

    ps4 = [0]

    def nps4():
        ps4[0] += 1
        return PS[ps4[0] % 4]

    SCALE = float(128 ** -0.5)
    tiny = stat[:, 3:4]
    S.memset(tiny, 1e-20)

    def rsum(out, in_):
        S.op("dve", lambda e: e.reduce_sum(out=out, in_=in_, axis=AX.X), reads=[in_], writes=[out])

    def nsa_phase(l):
        tabA = fview(xt, F32, 0, [128, 3, NT, 32], F32)
        sm = fview(xt, F32, 6144, [128, 512], F32)
        selT = fview(xt2, F32, 0, [32, 2, T], BF16)
        cmp3 = fview(gbc, F32, 0, [128, 32, 32], F32)
        e4 = fview(gbc, F32, 4096, [128, 4, 32], F32)
        p4 = fview(gbc, F32, 4608, [128, 4, 32], F32)
        Eexp = fview(wb[0], BF16, 0, [32, 16, 128], BF16)
        cm01T = fview(wb[0], BF16, 4096, [32, T], F32)
        gsel = fview(wb[1], BF16, 0, [24, 24, 128], F32)
        caus = fview(actA, BF16, 0, [128, 4, 512], BF16)
        wmask = fview(actA, BF16, 4096, [128, 8, 512], BF16)
        mexp = [fview(actA, BF16, 12288 + i * 1024, [128, 512], BF16) for i in range(2)]
        ebf = [fview(actA, BF16, 14336 + i * 1024, [128, 512], BF16) for i in range(2)]
        Xpe = fview(actB, BF16, 0, [128, 2, 32, 64], BF16)
        oacc = fview(actB, BF16, 0, [128, 512], F32)
        t1 = fview(actB, BF16, 2048, [128, 512], F32)
        rl = fview(actB, BF16, 4096, [128, 512], F32)
        t32 = fview(actB, BF16, 6144, [32, 512], F32)
        hid = fview(actB, BF16, 8192, [32, 256], F32)
        g1 = fview(actB, BF16, 9216, [32, 256], F32)
        g2 = fview(actB, BF16, 10240, [32, 256], F32)
        hidT = fview(actB, BF16, 11264, [128, 2, 32], F32)
        kcbT = fview(actB, BF16, 11520, [128, 2, 32], BF16)
        vcb = fview(actB, BF16, 11648, [32, 2, 128], BF16)
        ecT = fview(actB, BF16, 12288, [32, 512], BF16)
        pe_t = fview(actB, BF16, 13312, [128, 2, 64], F32)
        w2_t = fview(actB, BF16, 13824, [128, 2, 2, 128], F32)
        onesb = fview(actB, BF16, 15872, [128, 128], BF16)
        obf = stg[1][:].bitcast(BF16)[:, 0:512]

        S.memset(onesb, 1.0)
        S.dma("sp", pe_t, cmp_peT_in[l])
        S.dma("sp", w2_t, cmp_w2[l].rearrange("i (c p) d -> p i c d", p=128))
        for i in range(2):
            for g in range(2):
                src = (kT[:, g, :] if i == 0 else vcT[:, g, :]).rearrange("p (j r) -> p j r", r=64)
                S.tt(Xpe[:, g], src, pe_t[:, i, :].unsqueeze(1).broadcast_to([128, 32, 64]), ALU.add)
            accs = [PS[4], PS[5]]
            w1v = cmp_w1[l, i].rearrange("(r d) n -> d r n", d=128)
            for rq in range(4):
                buf = wb[rq % 2]
                wv = buf[:, 0:4096].rearrange("p (c n) -> p c n", c=16)
                wload(wv, w1v[:, rq * 16:(rq + 1) * 16, :])
                for g in range(2):
                    for rr in range(16):
                        r = rq * 16 + rr
                        S.mm(accs[g][0:32, 0:256], lhsT=Xpe[:, g, :, r], rhs=wv[:, rr, :],
                             start=(r == 0), stop=(r == 63))
            for g in range(2):
                S.act(hid, accs[g][0:32, 0:256], AF.Identity)
                S.tt(g1, hid, hid, ALU.mult)
                S.ts(g1, g1, 0.044715, 1.0, op0=ALU.mult, op1=ALU.add)
                S.tt(g1, g1, hid, ALU.mult)
                S.act(g2, g1, AF.Sigmoid, scale=1.5957691216057308)
                S.tt(hid, hid, g2, ALU.mult)
                pt = nps4()
                for c in range(2):
                    S.mm(pt[:, c * 32:(c + 1) * 32], lhsT=hid[:, c * 128:(c + 1) * 128], rhs=ident[0:32, 0:32])
                S.cp(hidT[:].rearrange("p c j -> p (c j)"), pt[:, 0:64])
                pt2 = nps4()
                if i == 0:
                    for c in range(2):
                        S.mm(pt2[:, 0:32], lhsT=w2_t[:, 0, c, :], rhs=hidT[:, c, :], start=(c == 0), stop=(c == 1))
                    S.cp(kcbT[:, g, :], pt2[:, 0:32])
                else:
                    for c in range(2):
                        S.mm(pt2[0:32, 0:128], lhsT=hidT[:, c, :], rhs=w2_t[:, 1, c, :], start=(c == 0), stop=(c == 1))
                    S.cp(vcb[:, g, :], pt2[0:32, 0:128])
        S.dma("sp", tabA, tabA_in)
        S.dma("sp", Eexp, eexp_in)
        S.dma("sp", cm01T, cm01T_in)
        S.dma("sp", gsel, gsel_in)
        S.dma("sp", caus, caus_in)
        S.dma("sp", wmask, wmask_in)
        for g in range(2):
            for tt_ in range(NT):
                tq = slice(tt_ * 128, (tt_ + 1) * 128)
                pt = nps4()
                for h in range(4):
                    S.mm(pt[:, h * 32:(h + 1) * 32], lhsT=qT[:, 4 * g + h, tq], rhs=kcbT[:, g, :])
                e4f = e4.rearrange("p h j -> p (h j)")
                S.act(e4f, pt[:, 0:128], AF.Exp, scale=SCALE)
                S.tt(e4, e4, tabA[:, 0, tt_, :].unsqueeze(1).broadcast_to([128, 4, 32]), ALU.mult)
                den = sm[:, 0:4]
                rsum(den, e4)
                S.act(den, den, AF.Ln, bias=tiny)
                S.act(den, den, AF.Exp, scale=-1.0)
                S.tt(p4, e4, den.unsqueeze(2).broadcast_to([128, 4, 32]), ALU.mult)
                imp = sm[:, 8:40]
                rsum(imp, p4.rearrange("p h j -> p j h"))
                score = sm[:, 40:72]
                S.tt(score, imp, tabA[:, 1, tt_, :], ALU.mult)
                S.tt(score, score, tabA[:, 2, tt_, :], ALU.add)
                S.tt(cmp3, score.unsqueeze(1).broadcast_to([128, 32, 32]),
                     score.unsqueeze(2).broadcast_to([128, 32, 32]), ALU.is_gt)
                rank = sm[:, 72:104]
                rsum(rank, cmp3)
                sel = sm[:, 104:136]
                S.ts(sel, rank, 15.5, None, op0=ALU.is_lt)
                pt2 = nps4()
                S.mm(pt2[0:32, 0:128], lhsT=sel, rhs=ident)
                S.cp(selT[:, g, tq], pt2[0:32, 0:128])

        def combine(num_ps, den_ps, gcol, tsl, first):
            S.act(rl, den_ps[:], AF.Ln, bias=tiny)
            S.act(rl, rl, AF.Exp, scale=-1.0)
            gps = nps4()
            S.mm(gps[:], lhsT=gsel[:, gcol, :], rhs=gatT[0:24, tsl])
            S.tt(t1, num_ps[:], rl, ALU.mult)
            if first:
                S.tt(oacc, t1, gps[:], ALU.mult)
            else:
                S.tt(t1, t1, gps[:], ALU.mult)
                S.tt(oacc, oacc, t1, ALU.add, eng="pool")

        hcount = 0
        for g in range(2):
            for tg in range(NG):
                tsl = slice(tg * TG, (tg + 1) * TG)
                for h in range(4):
                    hh = 4 * g + h
                    num, den_ = (PS[4], PS[5]) if hcount % 2 == 0 else (PS[6], PS[7])
                    hcount += 1
                    ps_s = nps4()
                    S.mm(ps_s[0:32, :], lhsT=kcbT[:, g, :], rhs=qT[:, hh, tsl])
                    S.act(t32, ps_s[0:32, :], AF.Exp, scale=SCALE)
                    S.tt(ecT, t32, cm01T[:, tsl], ALU.mult)
                    S.mm(den_[:], lhsT=onesb[0:32, :], rhs=ecT)
                    S.mm(num[:], lhsT=vcb[:, g, :], rhs=ecT)
                    combine(num, den_, hh * 3 + 0, tsl, True)
                    nk = 4 * tg + 4
                    for kt in range(nk):
                        pm = nps4()
                        S.mm(pm[:], lhsT=Eexp[:, kt, :], rhs=selT[:, g, tsl])
                        m = mexp[kt % 2]
                        if kt >= 4 * tg:
                            S.tt(m, pm[:], caus[:, kt - 4 * tg, :], ALU.mult)
                        else:
                            S.cp(m, pm[:])
                        pscore = nps4()
                        S.mm(pscore[:], lhsT=kT[:, 2 + g, kt * 128:(kt + 1) * 128], rhs=qT[:, hh, tsl])
                        e = ebf[kt % 2]
                        S.act(e, pscore[:], AF.Exp, scale=SCALE)
                        S.tt(e, e, m, ALU.mult)
                        S.mm(num[:], lhsT=vtm[:, kt, g, :], rhs=e, start=(kt == 0), stop=(kt == nk - 1))
                        S.mm(den_[:], lhsT=onesb, rhs=e, start=(kt == 0), stop=(kt == nk - 1))
                    combine(num, den_, hh * 3 + 1, tsl, False)
                    kts = [kt for kt in range(4 * tg - 4, 4 * tg + 4) if kt >= 0]
                    for ix, kt in enumerate(kts):
                        a = kt - (4 * tg - 4)
                        pscore = nps4()
                        S.mm(pscore[:], lhsT=kT[:, 4 + g, kt * 128:(kt + 1) * 128], rhs=qT[:, hh, tsl])
                        e = ebf[ix % 2]
                        S.act(e, pscore[:], AF.Exp, scale=SCALE)
                        S.tt(e, e, wmask[:, a, :], ALU.mult)
                        S.mm(num[:], lhsT=vtm[:, kt, 2 + g, :], rhs=e, start=(ix == 0), stop=(ix == len(kts) - 1))
                        S.mm(den_[:], lhsT=onesb, rhs=e, start=(ix == 0), stop=(ix == len(kts) - 1))
                    combine(num, den_, hh * 3 + 2, tsl, False)
                    S.act(obf, oacc, AF.Identity)
                    S.dma("sp", mixT[:, hh, tsl], obf)

    def bc(ap, shape, axis):
        return ap.unsqueeze(axis).broadcast_to(list(shape))

    def gdn_phase(l):
        ident64 = ident[0:64, 0:64]
        ones64 = ones[0:64, 0:64]
        gconst = fview(wb[0], BF16, 0, [64, 5, 128], F32)
        TriLE = gconst[:, 0, 0:64]
        Sel63 = gconst[:, 1, :]
        maskSL = gconst[:, 2, 0:64]
        maskUI = gconst[:, 3, 0:64]
        cw = fview(wb[0], BF16, 2560, [128, 24, 4], F32)
        gv = fview(wb[0], BF16, 3072, [64, 2, 8], F32)
        nexpA = fview(wb[0], BF16, 3200, [64, 8], F32)
        normw = fview(wb[0], BF16, 3584, [64, 128], F32)
        ab = fview(wb[0], BF16, 4096, [64, 32, 16], F32)
        beta = fview(wb[0], BF16, 6144, [64, 32, 8], F32)
        gg = fview(wb[0], BF16, 7168, [64, 32, 8], F32)
        gc = fview(wb[0], BF16, 8192, [64, 32, 8], F32)
        kd = fview(wb[0], BF16, 9216, [64, 32, 8], F32)
        egc = fview(wb[0], BF16, 10240, [64, 32, 8], F32)
        bgc = fview(wb[0], BF16, 11264, [64, 32, 8], F32)
        egl = fview(wb[0], BF16, 12288, [128, 32, 8], F32)
        kdec = fview(actB, BF16, 0, [64, 4, 128], F32)
        kbg = fview(actB, BF16, 2048, [64, 4, 128], F32)
        bv = fview(actB, BF16, 4096, [64, 4, 128], F32)
        uu = fview(actB, BF16, 6144, [64, 4, 128], F32)
        Lm = fview(actB, BF16, 8192, [64, 4, 64], F32)
        Um = fview(actB, BF16, 9216, [64, 4, 64], F32)
        Pm = fview(actB, BF16, 10240, [64, 4, 64], F32)
        qkT = fview(actB, BF16, 11264, [64, 4, 64], F32)
        tmp = fview(actB, BF16, 12288, [64, 4, 64], F32)
        E1 = fview(actB, BF16, 13312, [64, 4, 64], F32)
        E2 = fview(actB, BF16, 14336, [64, 4, 64], F32)
        gcd = fview(actB, BF16, 15360, [64, 4, 64], F32)
        LU = [fview(wb[1], BF16, i * 2048, [64, 2, 4, 64], F32) for i in range(2)]
        wT = fview(wb[1], BF16, 4096, [128, 4, 64], F32)
        zt = fview(wb[1], BF16, 6144, [64, 4, 128], F32)
        ssq = fview(wb[1], BF16, 8192, [64, 4], F32)
        osq = fview(wb[1], BF16, 10240, [64, 4, 128], F32)
        Sst = fview(gbc, F32, 0, [128, 4, 128], F32)
        vnew = fview(gbc, F32, 2048, [64, 4, 128], F32)
        oo = fview(gbc, F32, 4096, [64, 4, 128], F32)
        po2s = fview(gbc, F32, 6144, [64, 4, 128], F32)
        oT = actA
        oTv = actA[:].rearrange("p a b -> p (a b)").rearrange("p (h t) -> p h t", h=4)

        S.dma("sp", gconst, gconst_in)
        S.dma("sp", cw, convwT_in[:, l])
        S.dma("sp", gv, gvec_in[:, l])
        S.dma("sp", normw, gnorm_in[:, l])
        S.dma("sp", ab, abz[:, 0:16].rearrange("(n p) c -> p n c", p=64))
        S.act(beta, ab[:, :, 8:16], AF.Sigmoid)
        S.tt(gg, ab[:, :, 0:8], bc(gv[:, 1, :], [64, 32, 8], 1), ALU.add)
        S.act(gg, gg, AF.Exp)
        S.act(gg, gg, AF.Ln, bias=1.0)
        S.act(nexpA, gv[:, 0, :], AF.Exp)
        S.ts(nexpA, nexpA, -1.0, None, op0=ALU.mult)
        S.tt(gg, gg, bc(nexpA, [64, 32, 8], 1), ALU.mult)
        ggf = gg.rearrange("p n h -> p (n h)")
        gcf = gc.rearrange("p n h -> p (n h)")
        pt = nps4()
        S.mm(pt[0:64, 0:256], lhsT=TriLE, rhs=ggf)
        S.cp(gcf, pt[0:64, 0:256])
        pt = nps4()
        S.mm(pt[:, 0:256], lhsT=Sel63, rhs=gcf)
        S.act(egl.rearrange("p n h -> p (n h)"), pt[:, 0:256], AF.Exp)
        S.tt(kd.rearrange("p n h -> p (n h)"), pt[0:64, 0:256], gcf, ALU.subtract)
        S.act(kd, kd, AF.Exp)
        S.act(egc, gc, AF.Exp)
        S.tt(bgc, egc, beta, ALU.mult)

        if cfg.get("gstop") == 0:
            raise _Stop()
        qkv = [carve(i * 8 * KB, [128, T], F32) for i in range(12)]
        for ps_ in range(2):
            h0 = 4 * ps_
            for kind in range(3):
                for hp in range(4):
                    c = kind * 8 + h0 + hp
                    y = qkv[kind * 4 + hp]
                    eng = "dve"
                    S.dma("sp", xt[:], cin[c])
                    S.ts(y[:, 0:T], xt[:, 0:T], cw[:, c, 3:4], None, op0=ALU.mult, eng=eng)
                    for w_ in (2, 1, 0):
                        sh = 3 - w_
                        S.stt(y[:, sh:T], xt[:, 0:T - sh], cw[:, c, w_:w_ + 1], y[:, sh:T], ALU.mult, ALU.add, eng=eng)
                    S.act(y, y, AF.Silu)
                    if kind < 2:
                        S.act(xt2[:], y, AF.Square)
                        for j in range(4):
                            sl = slice(j * 512, (j + 1) * 512)
                            pt = nps4()
                            S.mm(pt[:], lhsT=ones, rhs=xt2[:, sl])
                            S.act(stg[j % 2][:], pt[:], AF.Ln, bias=epsb)
                            S.act(stg[j % 2][:], stg[j % 2][:], AF.Exp, scale=-0.5)
                            if kind == 0:
                                S.stt(y[:, sl], y[:, sl], float(128 ** -0.5), stg[j % 2][:], ALU.mult, ALU.mult)
                            else:
                                S.tt(y[:, sl], y[:, sl], stg[j % 2][:], ALU.mult)
            qs, ks_, vs_ = qkv[0:4], qkv[4:8], qkv[8:12]
            if cfg.get("gstop") == 1:
                raise _Stop()
            S.memset(Sst, 0.0)
            hs = slice(h0, h0 + 4)
            for ci in range(32):
                cs = slice(ci * 64, (ci + 1) * 64)
                b4 = beta[:, ci, hs]
                gc4 = gc[:, ci, hs]
                pk = nps4()
                for hp in range(4):
                    S.mm(pk[0:64, hp * 128:(hp + 1) * 128], lhsT=ks_[hp][:, cs], rhs=ident)
                pkv = pk[0:64, :].rearrange("p (h d) -> p h d", h=4)
                S.tt(kdec, pkv, bc(kd[:, ci, hs], [64, 4, 128], 2), ALU.mult)
                S.tt(kbg, pkv, bc(bgc[:, ci, hs], [64, 4, 128], 2), ALU.mult)
                pv = nps4()
                for hp in range(4):
                    S.mm(pv[0:64, hp * 128:(hp + 1) * 128], lhsT=vs_[hp][:, cs], rhs=ident)
                S.tt(bv, pv[0:64, :].rearrange("p (h d) -> p h d", h=4), bc(b4, [64, 4, 128], 2), ALU.mult)
                pab = nps4()
                for hp in range(4):
                    S.mm(pab[0:64, hp * 64:(hp + 1) * 64], lhsT=ks_[hp][:, cs], rhs=ks_[hp][:, cs])
                    S.mm(pab[0:64, 256 + hp * 64:256 + (hp + 1) * 64], lhsT=ks_[hp][:, cs], rhs=qs[hp][:, cs])
                S.tt(gcd, bc(ident64, [64, 4, 64], 1), bc(gc4, [64, 4, 64], 2), ALU.mult)
                pg = nps4()
                for hp in range(4):
                    S.mm(pg[0:64, hp * 64:(hp + 1) * 64], lhsT=ones64, rhs=gcd[:, hp, :])
                S.tt(tmp, pg[0:64, 0:256].rearrange("p (h j) -> p h j", h=4), bc(gc4, [64, 4, 64], 2), ALU.subtract)
                S.ts(E1, tmp, 0.0, None, op0=ALU.max)
                S.act(E1, E1, AF.Exp, scale=-1.0)
                S.ts(E2, tmp, 0.0, None, op0=ALU.min)
                S.act(E2, E2, AF.Exp)
                S.tt(Lm, pab[0:64, 0:256].rearrange("p (h j) -> p h j", h=4), E1, ALU.mult)
                S.tt(Lm, Lm, bc(maskSL, [64, 4, 64], 1), ALU.mult)
                S.tt(Lm, Lm, bc(b4, [64, 4, 64], 2), ALU.mult)
                S.tt(qkT, pab[0:64, 256:512].rearrange("p (h j) -> p h j", h=4), E2, ALU.mult)
                S.tt(qkT, qkT, bc(maskUI, [64, 4, 64], 1), ALU.mult)
                pu = nps4()
                for hp in range(4):
                    S.mm(pu[0:64, hp * 64:(hp + 1) * 64], lhsT=Lm[:, hp, :], rhs=ident64)
                puv = pu[0:64, 0:256].rearrange("p (h j) -> p h j", h=4)
                S.act(Um, puv, AF.Identity)
                S.tt(Pm, bc(ident64, [64, 4, 64], 1), puv, ALU.subtract)
                Lc, Uc = Lm, Um
                for lev in range(5):
                    last = lev == 4
                    pl = nps4()
                    for hp in range(4):
                        S.mm(pl[0:64, hp * 64:(hp + 1) * 64], lhsT=Uc[:, hp, :], rhs=Lc[:, hp, :])
                        if not last:
                            S.mm(pl[0:64, 256 + hp * 64:256 + (hp + 1) * 64], lhsT=Lc[:, hp, :], rhs=Uc[:, hp, :])
                    nxt = LU[lev % 2]
                    ncol = 256 if last else 512
                    S.act(nxt[:].rearrange("p a h j -> p (a h j)")[:, 0:ncol], pl[0:64, 0:ncol], AF.Identity)
                    Lc, Uc = nxt[:, 0], nxt[:, 1]
                    pp = nps4()
                    for hp in range(4):
                        S.mm(pp[0:64, hp * 64:(hp + 1) * 64], lhsT=Lc[:, hp, :], rhs=Pm[:, hp, :])
                    S.tt(Pm, Pm, pp[0:64, 0:256].rearrange("p (h j) -> p h j", h=4), ALU.add)
                pw = nps4()
                for hp in range(4):
                    S.mm(pw[:, hp * 64:(hp + 1) * 64], lhsT=kbg[:, hp, :], rhs=Pm[:, hp, :])
                S.act(wT[:].rearrange("p h c -> p (h c)"), pw[:, 0:256], AF.Identity)
                pu2 = nps4()
                for hp in range(4):
                    S.mm(pu2[0:64, hp * 128:(hp + 1) * 128], lhsT=Pm[:, hp, :], rhs=bv[:, hp, :])
                S.act(uu[:].rearrange("p h e -> p (h e)"), pu2[0:64, :], AF.Identity)
                if cfg.get("gstop") == 2:
                    raise _Stop()
                S.dma("sp", zt, abz[ci * 64:(ci + 1) * 64, 16 + h0 * 128:16 + (h0 + 4) * 128].rearrange("p (h e) -> p h e", h=4))
                pws = PS[4]
                for hp in range(4):
                    S.mm(pws[0:64, hp * 128:(hp + 1) * 128], lhsT=wT[:, hp, :], rhs=Sst[:, hp, :])
                S.tt(vnew, uu, pws[0:64, :].rearrange("p (h e) -> p h e", h=4), ALU.subtract)
                po1 = PS[5]
                for hp in range(4):
                    S.mm(po1[0:64, hp * 128:(hp + 1) * 128], lhsT=qs[hp][:, cs], rhs=Sst[:, hp, :])
                po2 = PS[6]
                for hp in range(4):
                    S.mm(po2[0:64, hp * 128:(hp + 1) * 128], lhsT=qkT[:, hp, :], rhs=vnew[:, hp, :])
                pS = PS[7]
                for hp in range(4):
                    S.mm(pS[:, hp * 128:(hp + 1) * 128], lhsT=kdec[:, hp, :], rhs=vnew[:, hp, :])
                S.act(po2s[:].rearrange("p h e -> p (h e)"), po2[0:64, :], AF.Identity)
                S.tt(oo, po1[0:64, :].rearrange("p (h e) -> p h e", h=4), bc(egc[:, ci, hs], [64, 4, 128], 2), ALU.mult)
                S.tt(oo, oo, po2s, ALU.add, eng="pool")
                S.tt(Sst, Sst, bc(egl[:, ci, hs], [128, 4, 128], 2), ALU.mult)
                S.tt(Sst, Sst, pS[:].rearrange("p (h e) -> p h e", h=4), ALU.add)
                S.tt(osq, oo, oo, ALU.mult, eng="pool")
                rsum(ssq, osq)
                S.act(ssq, ssq, AF.Ln, scale=1.0 / 128, bias=epsb[0:64])
                S.act(ssq, ssq, AF.Exp, scale=-0.5)
                S.act(zt, zt, AF.Silu)
                S.tt(oo, oo, bc(ssq, [64, 4, 128], 2), ALU.mult)
                S.tt(zt, zt, bc(normw, [64, 4, 128], 1), ALU.mult, eng="pool")
                S.tt(oo, oo, zt, ALU.mult)
                pot = nps4()
                for hp in range(4):
                    S.mm(pot[:, hp * 64:(hp + 1) * 64], lhsT=oo[:, hp, :], rhs=ident64)
                S.act(oTv[:, :, cs], pot[:, 0:256].rearrange("p (h c) -> p h c", h=4), AF.Identity)
                if cfg.get("gstop") == 3:
                    raise _Stop()
            for hp in range(4):
                S.dma("sp", mixT[:, 8 + h0 + hp, :], oTv[:, hp, :])
                S.dma("sp", p_gdn[l, h0 + hp], Sst[:, hp, :])

    def idma(out, in_, idx):
        pool = S.pool["pool"]
        slot = pool[S.pidx["pool"] % len(pool)]
        S.pidx["pool"] += 1
        deps = []
        if slot[1] > 0:
            deps.append((slot[0], slot[1], "dma:pool"))
        slot[1] += 16
        tag = (slot[0], slot[1], "dma:pool")
        deps += S._deps_and_record([idx], [out], tag)
        deps = [d for d in deps if d is not tag]
        waits = S._waits("pool", deps)
        S.prog["pool"].append((waits, (lambda e: e.indirect_dma_start(
            out=out, out_offset=None, in_=in_, in_offset=bass.IndirectOffsetOnAxis(ap=idx, axis=0))), slot[0], 16))
        S.n_inst += 1

    one11 = ones[0:1, 0:1]

    def row_to_fm(row_ap, n, dst):
        pt = nps4()
        for c in range(n):
            S.mm(pt[:, c:c + 1], lhsT=row_ap[0:1, c * 128:(c + 1) * 128], rhs=one11)
        S.cp(dst, pt[:, 0:n])

    def fm_stats(xfm):
        sq = sm2[:, 0:16]
        ss = sm2[:, 16:17]
        tot = sm2[:, 17:18]
        S.memset(ss, 0.0)
        S.act(sq, xfm, AF.Square, accum=ss)
        pt = nps4()
        S.mm(pt[:, 0:1], lhsT=ones, rhs=ss)
        S.act(tot, pt[:, 0:1], AF.Ln, scale=1.0 / D, bias=epsb)
        S.act(tot, tot, AF.Exp, scale=-0.5)
        return tot

    def fm_prenorm(ka, kb, out_bf):
        tot = fm_stats(xsam[:])
        tmp = sm2[:, 18:34]
        S.ts(tmp, xsam[:], tot, None, op0=ALU.mult)
        S.tt(tmp, tmp, eff[:, ka, :, 1], ALU.mult)
        S.tt(out_bf, tmp, eff[:, kb, :, 1], ALU.add)

    def fm_post(frow, kG):
        f_fm = sm2[:, 40:56]
        row_to_fm(frow, 16, f_fm)
        tot = fm_stats(f_fm)
        tmp = sm2[:, 18:34]
        S.ts(tmp, f_fm, tot, None, op0=ALU.mult)
        S.tt(tmp, tmp, eff[:, kG, :, 1], ALU.mult)
        S.tt(xsam[:], xsam[:], tmp, ALU.add)

    wsi = [0]

    def tm_proj(in_f, nk, wview, N, out_row):
        for c0 in range(0, N, 256):
            ncol = min(256, N - c0)
            pt = nps4()
            for k0 in range(0, nk, 16):
                k1 = min(nk, k0 + 16)
                buf = wb[wsi[0] % 2]
                wsi[0] += 1
                wv = buf[:, 0:2 * (k1 - k0) * ncol].bitcast(F32).rearrange("p (c n) -> p c n", c=k1 - k0)
                wload(wv, wview[:, k0:k1, c0:c0 + ncol], q="sp")
                for kc in range(k0, k1):
                    S.mm(pt[0:1, 0:ncol], lhsT=in_f[:, kc:kc + 1], rhs=wv[:, kc - k0, :],
                         start=(kc == 0), stop=(kc == nk - 1))
            S.act(out_row[0:1, c0:c0 + ncol], pt[0:1, 0:ncol], AF.Identity)

    def gelu_ps(dst, src_ps, t1_, t2_):
        S.act(dst, src_ps, AF.Identity)
        S.tt(t1_, dst, dst, ALU.mult)
        S.ts(t1_, t1_, 0.044715, 1.0, op0=ALU.mult, op1=ALU.add)
        S.tt(t1_, t1_, dst, ALU.mult)
        S.act(t2_, t1_, AF.Sigmoid, scale=1.5957691216057308)
        S.tt(dst, dst, t2_, ALU.mult)

    def sample_layer(l):
        projrow = carve(0, [128, 6720], F32)[0:1]
        xv = fview(xt, F32, 0, [128, 2048], F32)
        hid = [xv[:, 0:256], xv[:, 256:512], xv[0:1, 512:768]]
        gt1, gt2 = xv[:, 768:1024], xv[:, 1024:1280]
        hidT = xv[:, 1280:1280 + 514].rearrange("p (c j) -> p c j", c=2)
        x2v = fview(xt2, F32, 0, [128, 2048], F32)
        kcbT = x2v[:, 0:257]
        vcb = [x2v[:, 384:512], x2v[:, 512:640], x2v[0:1, 640:768]]
        pe_t = x2v[:, 768:896].rearrange("p (i r) -> p i r", i=2)
        w2_t = x2v[:, 896:1408].rearrange("p (i c d) -> p i c d", i=2, c=2)
        Xnew = x2v[:, 1408:1472].bitcast(BF16)
        Xnew = Xnew.rearrange("p (i r) -> p i r", i=2)
        qcol = x2v[:, 1472:1480]
        kvcol = x2v[:, 1480:1492]
        hsb = x2v[:, 1610:1626]
        mixcol = x2v[:, 1508:1524]
        mixb = mixcol
        h2b = x2v[:, 1626:1642]
        XnewF = x2v[:, 1642:1706]
        actcol = x2v[:, 1540:1584]
        actb = actcol
        gv_ = fview(gbc, F32, 0, [128, 2048], F32)
        scorebc = gv_[:, 0:256]
        cmpm = gv_[:, 256:512]
        selexp = gv_[:, 512:640]
        selcol = gv_[:, 640:642]
        rank2 = gv_[:, 642:644]
        scol = gv_[:, 644:646]
        e4 = gv_[0:4, 648:648 + 256]
        p4 = gv_[0:4, 904:904 + 256]
        pT = gv_[:, 1160:1168].rearrange("p (c h) -> p c h", c=2)
        dsel = gv_[:, 1168:1680].rearrange("p (jh a g) -> p jh a g", jh=2, a=2)
        hsel = gv_[:, 1680:1936].rearrange("p (a q) -> p a q", a=2)
        rowsA = fview(actA, BF16, 0, [128, 4096], F32)
        rowsB = fview(actB, BF16, 0, [128, 4096], F32)
        krow, vrow, kSr, qSr = (rowsA[0:1, i * 1024:(i + 1) * 1024] for i in range(4))
        vnew, orow, zrow, trow = (rowsB[0:1, i * 1024:(i + 1) * 1024] for i in range(4))
        scal = carve(92 * KB, [128, 2048], F32)
        pgt = [stg[0][:, 0:256], stg[1][:, 0:256]]
        pgx = [stg[0][:, 128:257], stg[1][:, 128:257]]
        kTt = [stg[0][:, 260:388], stg[1][:, 260:388]]
        for i_ in range(2):
            S.memset(stg[i_][:, 256:257], 1.0)

        fm_prenorm(0, 1, hsb)
        tm_proj(hsb, NCH, w_in[l].rearrange("(c p) n -> p c n", p=128), INW, projrow)
        rs = scal[0:1, 0:32].rearrange("p (a f) -> p a f", a=2)
        S.dma("sp", rs, ropeS_in)
        t16 = scal[0:1, 32:32 + 14 * 32]
        for (base, nseg, stride) in ((O_Q, 8, 128), (O_KC, 2, 128), (O_KS, 2, 128), (O_KW, 2, 128)):
            v = projrow[0:1, base:base + nseg * stride].rearrange("p (s d) -> p s d", s=nseg)
            x1, x2 = v[:, :, 0:16], v[:, :, 16:32]
            a_ = t16[0:1, 0:nseg * 16].rearrange("p (s f) -> p s f", s=nseg)
            b_ = t16[0:1, 128:128 + nseg * 16].rearrange("p (s f) -> p s f", s=nseg)
            c_ = t16[0:1, 256:256 + nseg * 16].rearrange("p (s f) -> p s f", s=nseg)
            cosb = bc(rs[:, 0, :], [1, nseg, 16], 1)
            sinb = bc(rs[:, 1, :], [1, nseg, 16], 1)
            S.tt(a_, x1, cosb, ALU.mult)
            S.tt(b_, x2, sinb, ALU.mult)
            S.tt(c_, x1, sinb, ALU.mult)
            S.tt(x1, a_, b_, ALU.subtract)
            S.tt(a_, x2, cosb, ALU.mult)
            S.tt(x2, a_, c_, ALU.add)
        for g in range(2):
            for (dst, ko, vo) in ((s_cmp, O_KC, O_VC), (s_sel, O_KS, O_VS)):
                S.dma("sp", dst[l, g, 0, 0:1, :], projrow[0:1, ko + g * 128:ko + (g + 1) * 128])
                S.dma("sp", dst[l, g, 0, 1:2, :], projrow[0:1, vo + g * 128:vo + (g + 1) * 128])
            S.dma("sp", s_win[l, g, 0:511, :], cache_win[l, g, 1:512, :])
            S.dma("sp", s_win[l, g, 511:512, 0:128], projrow[0:1, O_KW + g * 128:O_KW + (g + 1) * 128])
            S.dma("sp", s_win[l, g, 511:512, 128:256], projrow[0:1, O_VW + g * 128:O_VW + (g + 1) * 128])
        row_to_fm(projrow[0:1, O_Q:O_Q + 1024], 8, qcol)
        row_to_fm(projrow[0:1, O_KC:O_KC + 1536], 12, kvcol)
        S.act(scal[0:1, 512:536], projrow[0:1, O_GL:O_GL + 24], AF.Sigmoid)
        S.dma("sp", gscr.rearrange("(o n) -> o n", o=1), scal[0:1, 512:536])
        gat = scal[0:4, 544:550].rearrange("p (g i) -> p g i", g=2)
        for g in range(2):
            S.dma("sp", gat[:, g, :], gscr[g * 12:(g + 1) * 12].rearrange("(h i) -> h i", i=3))
        S.dma("sp", pti[:, 0:128], ptab_in.partition_broadcast(128))
        S.dma("sp", pti[:, 128:129], iota_in)
        S.op("pool", lambda e: e.tensor_scalar(out=pti[:, 0:128], in0=pti[:, 0:128], scalar1=512, scalar2=l * 256,
                                               op0=ALU.mult, op1=ALU.add), reads=[pti[:, 0:128]], writes=[pti[:, 0:128]])
        S.op("pool", lambda e: e.tensor_tensor(out=pti[:, 0:128], in0=pti[:, 0:128],
                                               in1=pti[:, 128:129].to_broadcast([128, 128]), op=ALU.add),
             reads=[pti[:, 0:129]], writes=[pti[:, 0:128]])
        S.dma("sp", pe_t, cmp_peT_in[l])
        S.dma("sp", w2_t, cmp_w2[l].rearrange("i (c p) d -> p i c d", p=128))
        S.dma("sp", dsel, dsel_in)
        S.dma("sp", hsel, hsel_in)
        osum = scal[0:4, 560:560 + 128]
        for g in range(2):
            idxg = pti[:, 130:131]
            for i in range(2):
                XTi = carve(28 * KB, [128, 16384], F32 if i == 0 else BF16)
                for pg in range(128):
                    tile_ = pgt[pg % 2]
                    S.op("pool", lambda e, pg=pg, g=g: e.tensor_scalar(out=idxg, in0=pti[:, pg:pg + 1], scalar1=g * 128, scalar2=None,
                                                                   op0=ALU.add), reads=[pti[:, pg:pg + 1]], writes=[idxg])
                    idma(tile_, cache_cmp, idxg)
                    pt = nps4()
                    S.mm(pt[:, 0:128], lhsT=tile_[:, i * 128:(i + 1) * 128], rhs=ident)
                    if pg % 2 == 0:
                        S.act(XTi[:, pg * 128:(pg + 1) * 128], pt[:, 0:128], AF.Identity)
                    else:
                        S.cp(XTi[:, pg * 128:(pg + 1) * 128], pt[:, 0:128])
                xv3 = XTi.rearrange("p (j r) -> p j r", r=64)
                for q4 in range(8):
                    S.tt(xv3[:, q4 * 32:(q4 + 1) * 32, :], xv3[:, q4 * 32:(q4 + 1) * 32, :],
                         bc(pe_t[:, i, :], [128, 32, 64], 1), ALU.add, eng=("dve" if q4 % 2 == 0 else "pool"))
                Xn = XnewF if i == 0 else Xnew[:, 1, :]
                S.cp(Xn, pe_t[:, i, :])
                S.tt(Xn[:, 0:1], pe_t[:, i, 0:1], kvcol[:, i * 2 + g:i * 2 + g + 1], ALU.add)
                accs = [PS[4], PS[5], PS[6]]
                w1v = cmp_w1[l, i].rearrange("(r d) n -> d r n", d=128)
                nq, nr = (8, 8) if i == 0 else (4, 16)
                for rq in range(nq):
                    buf = wb[rq % 2]
                    if i == 0:
                        wv = buf[:, 0:4096].bitcast(F32).rearrange("p (c n) -> p c n", c=8)
                        wload(wv, w1v[:, rq * 8:(rq + 1) * 8, :], q="sp")
                    else:
                        wv = buf[:, 0:4096].rearrange("p (c n) -> p c n", c=16)
                        wload(wv, w1v[:, rq * 16:(rq + 1) * 16, :])
                    for rr in range(nr):
                        r = rq * nr + rr
                        for mg in range(2):
                            S.mm(accs[mg][:, 0:256], lhsT=xv3[:, mg * 128:(mg + 1) * 128, r], rhs=wv[:, rr, :],
                                 start=(r == 0), stop=(r == 63))
                        S.mm(accs[2][0:1, 0:256], lhsT=Xn[:, r:r + 1], rhs=wv[:, rr, :], start=(r == 0), stop=(r == 63))
                for mg in range(3):
                    np_ = 1 if mg == 2 else 128
                    gelu_ps(hid[mg], accs[mg][0:np_, 0:256], gt1[0:np_], gt2[0:np_])
                    for c in range(2):
                        pt = nps4()
                        S.mm(pt[:, 0:np_], lhsT=hid[mg][:, c * 128:(c + 1) * 128], rhs=ident[0:np_, 0:np_])
                        S.cp(hidT[:, c, mg * 128:mg * 128 + np_], pt[:, 0:np_])
                if i == 0:
                    pt2 = nps4()
                    for c in range(2):
                        S.mm(pt2[:, 0:257], lhsT=w2_t[:, 0, c, :], rhs=hidT[:, c, :], start=(c == 0), stop=(c == 1))
                    S.cp(kcbT, pt2[:, 0:257])
                else:
                    for mg in range(3):
                        np_ = 1 if mg == 2 else 128
                        pt2 = nps4()
                        for c in range(2):
                            S.mm(pt2[0:np_, 0:128], lhsT=hidT[:, c, mg * 128:mg * 128 + np_], rhs=w2_t[:, 1, c, :],
                                 start=(c == 0), stop=(c == 1))
                        S.cp(vcb[mg], pt2[0:np_, 0:128])
            qg = qcol[:, 4 * g:4 * g + 4]
            ps_ = nps4()
            S.mm(ps_[0:4, 0:256], lhsT=qg, rhs=kcbT[:, 0:256])
            den4 = scal[0:4, 700:701]
            S.memset(den4, 0.0)
            S.act(e4, ps_[0:4, 0:256], AF.Exp, scale=SCALE, accum=den4)
            S.act(den4, den4, AF.Ln)
            S.act(den4, den4, AF.Exp, scale=-1.0)
            S.ts(p4, e4, den4, None, op0=ALU.mult)
            for c in range(2):
                pt = nps4()
                S.mm(pt[:, 0:4], lhsT=p4[:, c * 128:(c + 1) * 128], rhs=ident[0:4, 0:4])
                S.cp(pT[:, c, :], pt[:, 0:4])
            po = nps4()
            for c in range(2):
                S.mm(po[0:4, 0:128], lhsT=pT[:, c, :], rhs=vcb[c], start=(c == 0), stop=(c == 1))
            S.ts(osum, po[0:4, 0:128], gat[:, g, 0:1], None, op0=ALU.mult)
            pi = nps4()
            S.mm(pi[0:1, 0:256], lhsT=ones[0:4, 0:1], rhs=p4)
            imr = scal[0:1, 704:704 + 256]
            S.cp(imr, pi[0:1, 0:256])
            S.memset(imr[:, 0:1], 12.0)
            S.memset(imr[:, 255:256], 10.0)
            pb = nps4()
            S.mm(pb[:, 0:256], lhsT=ones[0:1, :], rhs=imr)
            S.cp(scorebc, pb[:, 0:256])
            row_to_fm(imr, 2, scol)
            for c in range(2):
                S.ts(cmpm, scorebc, scol[:, c:c + 1], None, op0=ALU.is_gt)
                rsum(rank2[:, c:c + 1], cmpm)
            S.ts(selcol, rank2, 14.5, None, op0=ALU.is_lt)
            pe_ = nps4()
            k_ = 0
            for jh in range(2):
                for a in range(2):
                    tmpd_ = cmpm[:, 0:128]
                    S.ts(tmpd_, dsel[:, jh, a, :], selcol[:, jh:jh + 1], None, op0=ALU.mult)
                    S.mm(pe_[:, 0:128], lhsT=hsel[:, a, :], rhs=tmpd_, start=(k_ == 0), stop=(k_ == 3))
                    k_ += 1
            S.cp(selexp, pe_[:, 0:128])
            if dbg_s is not None and l == 0:
                S.dma("sp", dbg_s[:, 16 + 2 * g:18 + 2 * g], selcol)
                S.dma("sp", dbg_s[0:4, 32 + 8 * g:32 + 8 * g + 3], gat[:, g, :])
            for br in range(2):
                acc = PS[4 + br]
                ntile = 128 if br == 0 else 4
                for tix in range(ntile):
                    tile_ = pgt[tix % 2]
                    if br == 0:
                        S.op("pool", lambda e, pg=tix, g=g: e.tensor_scalar(out=idxg, in0=pti[:, pg:pg + 1], scalar1=g * 128,
                                                                        scalar2=None, op0=ALU.add),
                             reads=[pti[:, tix:tix + 1]], writes=[idxg])
                        idma(tile_, cache_sel, idxg)
                    else:
                        S.dma("sp", tile_, cache_win[l, g, tix * 128:(tix + 1) * 128, :])
                    pt = nps4()
                    S.mm(pt[:, 0:128], lhsT=tile_[:, 0:128], rhs=ident)
                    kt_ = kTt[tix % 2]
                    S.act(kt_, pt[:, 0:128], AF.Identity)
                    pss = nps4()
                    S.mm(pss[:, 0:4], lhsT=kt_, rhs=qg)
                    eT = scal[:, 960 + (tix % 2) * 4:964 + (tix % 2) * 4]
                    S.act(eT, pss[:, 0:4], AF.Exp, scale=SCALE)
                    if br == 0:
                        S.ts(eT, eT, selexp[:, tix:tix + 1], None, op0=ALU.mult)
                    S.mm(acc[0:4, 0:129], lhsT=eT, rhs=pgx[tix % 2], start=(tix == 0), stop=False)
                kcol_new = kvcol[:, 4 + br * 4 + g:5 + br * 4 + g]
                vo_ = (O_VS if br == 0 else O_VW) + g * 128
                pss = nps4()
                S.mm(pss[0:1, 0:4], lhsT=kcol_new, rhs=qg)
                en = scal[0:1, 970:974]
                S.act(en, pss[0:1, 0:4], AF.Exp, scale=SCALE)
                vn1 = scal[0:1, 1200:1329]
                S.cp(vn1[:, 0:128], projrow[0:1, vo_:vo_ + 128])
                S.memset(vn1[:, 128:129], 1.0)
                S.mm(acc[0:4, 0:129], lhsT=en, rhs=vn1, start=False, stop=True)
                rd = scal[0:4, 976:977]
                S.act(rd, acc[0:4, 128:129], AF.Ln)
                S.act(rd, rd, AF.Exp, scale=-1.0)
                S.tt(rd, rd, gat[:, g, 1 + br:2 + br], ALU.mult)
                ob = scal[0:4, 980:980 + 128]
                S.ts(ob, acc[0:4, 0:128], rd, None, op0=ALU.mult)
                S.tt(osum, osum, ob, ALU.add)
            pt = nps4()
            S.mm(pt[:, 0:4], lhsT=osum, rhs=ident[0:4, 0:4])
            S.cp(mixcol[:, 4 * g:4 * g + 4], pt[:, 0:4])

        cw = fview(wb[0], BF16, 2560, [128, 24, 4], F32)
        S.dma("sp", cw, convwT_in[:, l])
        cst3 = scal[:, 256:328].rearrange("p (c w) -> p c w", w=3)
        S.dma("sp", cst3, sconvT_in[l])
        cfm = scal[:, 328:352]
        row_to_fm(projrow[0:1, O_CONV:O_CONV + 3072], 24, cfm)
        yfm = scal[:, 352:376]
        tfm = scal[:, 376:400]
        S.tt(yfm, cfm, cw[:, :, 3], ALU.mult)
        for w_ in range(3):
            S.tt(tfm, cst3[:, :, w_], cw[:, :, w_], ALU.mult)
            S.tt(yfm, yfm, tfm, ALU.add)
        S.act(yfm, yfm, AF.Silu)
        S.dma("sp", s_conv[l, 0:2, :], sconv_in[l, 1:3, :])
        S.dma("sp", s_conv[l, 2:3, :], projrow[0:1, O_CONV:O_CONV + 3072])
        sqf = scal[:, 400:416]
        S.tt(sqf, yfm[:, 0:16], yfm[:, 0:16], ALU.mult)
        pn = nps4()
        S.mm(pn[:, 0:16], lhsT=ones, rhs=sqf)
        rn = scal[:, 416:432]
        S.act(rn, pn[:, 0:16], AF.Ln, bias=epsb)
        S.act(rn, rn, AF.Exp, scale=-0.5)
        S.tt(yfm[:, 0:16], yfm[:, 0:16], rn, ALU.mult)
        S.ts(yfm[:, 0:8], yfm[:, 0:8], float(128 ** -0.5), None, op0=ALU.mult)
        qf, kf, vf = yfm[:, 0:8], yfm[:, 8:16], yfm[:, 16:24]
        Sg = carve(100 * KB, [128, 8, 128], F32)
        S.dma("sp", Sg, sgdn_in[l].rearrange("h k v -> k h v"))
        for (rowdst, colsrc) in ((krow, kf), (vrow, vf)):
            for half in range(2):
                pt = nps4()
                for hh in range(4):
                    h = half * 4 + hh
                    S.mm(pt[0:1, hh * 128:(hh + 1) * 128], lhsT=colsrc[:, h:h + 1], rhs=ident)
                S.cp(rowdst[0:1, half * 512:(half + 1) * 512], pt[0:1, :])
        for (rowdst, colsrc) in ((kSr, kf), (qSr, qf)):
            for half in range(2):
                pt = nps4()
                for hh in range(4):
                    h = half * 4 + hh
                    S.mm(pt[0:1, hh * 128:(hh + 1) * 128], lhsT=colsrc[:, h:h + 1], rhs=Sg[:, h, :])
                S.cp(rowdst[0:1, half * 512:(half + 1) * 512], pt[0:1, :])
        prod = scal[:, 432:440]
        S.tt(prod, qf, kf, ALU.mult)
        pq = nps4()
        S.mm(pq[0:1, 0:8], lhsT=ones[:, 0:1], rhs=prod)
        r8 = scal[0:1, 440:520].rearrange("p (k h) -> p k h", h=8)
        S.cp(r8[:, 0, :], pq[0:1, 0:8])
        gvr = scal[0:1, 520:536].rearrange("p (a h) -> p a h", a=2)
        S.dma("sp", gvr, gvec_in[0:1, l])
        S.act(r8[:, 1, :], projrow[0:1, O_B:O_B + 8], AF.Sigmoid)
        S.tt(r8[:, 2, :], projrow[0:1, O_A:O_A + 8], gvr[:, 1, :], ALU.add)
        S.act(r8[:, 2, :], r8[:, 2, :], AF.Exp)
        S.act(r8[:, 2, :], r8[:, 2, :], AF.Ln, bias=1.0)
        S.act(r8[:, 3, :], gvr[:, 0, :], AF.Exp)
        S.tt(r8[:, 2, :], r8[:, 2, :], r8[:, 3, :], ALU.mult)
        S.act(r8[:, 2, :], r8[:, 2, :], AF.Exp, scale=-1.0)
        qk8, be8, eg8 = r8[:, 0, :], r8[:, 1, :], r8[:, 2, :]
        v3 = lambda ap: ap.rearrange("p (h e) -> p h e", h=8)
        S.tt(v3(trow), v3(kSr), bc(eg8, [1, 8, 128], 2), ALU.mult)
        S.tt(v3(trow), v3(vrow), v3(trow), ALU.subtract)
        S.tt(v3(vnew), v3(trow), bc(be8, [1, 8, 128], 2), ALU.mult)
        S.tt(v3(orow), v3(qSr), bc(eg8, [1, 8, 128], 2), ALU.mult)
        S.tt(v3(trow), v3(vnew), bc(qk8, [1, 8, 128], 2), ALU.mult)
        S.tt(orow, orow, trow, ALU.add)
        pegb = nps4()
        S.mm(pegb[:, 0:8], lhsT=ones[0:1, :], rhs=eg8)
        egb = scal[:, 536:544]
        S.cp(egb, pegb[:, 0:8])
        for half in range(2):
            pt = PS[6 + half]
            for hh in range(4):
                h = half * 4 + hh
                S.mm(pt[:, hh * 128:(hh + 1) * 128], lhsT=krow[0:1, h * 128:(h + 1) * 128], rhs=vnew[0:1, h * 128:(h + 1) * 128])
            sl = Sg[:, half * 4:(half + 1) * 4, :]
            S.tt(sl, sl, bc(egb[:, half * 4:(half + 1) * 4], [128, 4, 128], 2), ALU.mult)
            S.tt(sl, sl, pt[:].rearrange("p (h e) -> p h e", h=4), ALU.add)
        S.dma("sp", s_gdn[l].rearrange("h k v -> k h v"), Sg)
        S.tt(trow, orow, orow, ALU.mult)
        rsum(r8[:, 4, :], v3(trow))
        S.act(r8[:, 4, :], r8[:, 4, :], AF.Ln, scale=1.0 / 128, bias=epsb[0:1])
        S.act(r8[:, 4, :], r8[:, 4, :], AF.Exp, scale=-0.5)
        S.tt(v3(orow), v3(orow), bc(r8[:, 4, :], [1, 8, 128], 2), ALU.mult)
        nw = scal[0:1, 544:672]
        S.dma("sp", nw, gnorm_in[0:1, l])
        S.tt(v3(orow), v3(orow), bc(nw, [1, 8, 128], 1), ALU.mult)
        S.act(zrow, projrow[0:1, O_Z:O_Z + 1024], AF.Silu)
        S.tt(orow, orow, zrow, ALU.mult)
        row_to_fm(orow, 8, mixcol[:, 8:16])
        if dbg_s is not None and l == 0:
            S.dma("sp", dbg_s[:, 0:16], mixcol)
        frow = rowsB[0:1, 0:2048]
        tm_proj(mixb, NCH, w_out[l].rearrange("(c p) n -> p c n", p=128), D, frow)
        fm_post(frow, 4)
        fm_prenorm(2, 3, h2b)
        grow = carve(0, [128, 6720], F32)[0:1, 0:DFF]
        urow = carve(28 * KB, [128, 6720], F32)[0:1, 0:DFF]
        tm_proj(h2b, NCH, w_gate[l].rearrange("(c p) n -> p c n", p=128), DFF, grow)
        tm_proj(h2b, NCH, w_up[l].rearrange("(c p) n -> p c n", p=128), DFF, urow)
        S.act(grow, grow, AF.Silu)
        S.tt(grow, grow, urow, ALU.mult)
        row_to_fm(grow, 44, actcol)
        tm_proj(actb, 44, w_down[l].rearrange("(c p) n -> p c n", p=128), D, frow)
        fm_post(frow, 5)

    nlayers = cfg.get("layers", DEPTH)
    S.dma("sp", xsam[:], xsT_in)
    for l in range(nlayers):
        layer_eff(l)
        S.dma("sp", ropeT, ropeFM_in)
        xsrc = x_in if l == 0 else xbuf2
        chk("eff")
        win_l = w_in[l].rearrange("(c p) n -> p c n", p=128)
        for tg in range(NG):
            hT = actA
            for ti in range(4):
                tt_ = tg * 4 + ti
                S.dma("sp", xt[:], xsrc[tt_ * 128:(tt_ + 1) * 128, :])
                norm_to_fm(xt, hT, ti * 128, 0, 1, 0)
            tsl = slice(tg * TG, (tg + 1) * TG)
            chk("norm")
            fm_cols = [(O_Q + h * 128, ("q", h)) for h in range(8)]
            fm_cols += [(O_KC + g * 128, ("k", 0 + g)) for g in range(2)]
            fm_cols += [(O_KS + g * 128, ("k", 2 + g)) for g in range(2)]
            fm_cols += [(O_KW + g * 128, ("k", 4 + g)) for g in range(2)]
            fm_cols += [(O_VC + g * 128, ("vc", g)) for g in range(2)]
            fm_cols += [(O_CONV + c * 128, ("cin", c)) for c in range(24)]
            fm_cols += [(O_GL, ("gate", 0))]
            for col0, (kind, idx) in fm_cols:
                ncol = 24 if kind == "gate" else 128
                buf = wb[wi % 2]
                wi += 1
                wv = buf[:, 0:NCH * ncol].rearrange("p (c n) -> p c n", c=NCH)
                wload(wv, win_l[:, :, col0:col0 + ncol])
                pt = nps()
                for kc in range(NCH):
                    S.mm(pt[0:ncol, :], lhsT=wv[:, kc, :], rhs=hT[:, kc, :], start=(kc == 0), stop=(kc == NCH - 1))
                if kind in ("q", "k"):
                    xs = stg[0]
                    S.act(xs[:], pt[:], AF.Identity)
                    pr = nps()
                    S.mm(pr[:], lhsT=rotT, rhs=xs[:])
                    t1 = stg[1]
                    S.tt(t1[:], xs[:], ropeT[:, 0, tsl], ALU.mult)
                    S.tt(xs[:], pr[:], ropeT[:, 1, tsl], ALU.mult)
                    S.tt(t1[:], t1[:], xs[:], ALU.add, eng="pool")
                    if kind == "q":
                        S.act(qT[:, idx, tsl], t1[:], AF.Identity)
                    else:
                        S.act(kT[:, idx, tsl], t1[:], AF.Identity)
                        pt2 = nps()
                        for ti in range(4):
                            S.tr(pt2[:, ti * 128:(ti + 1) * 128], t1[:, ti * 128:(ti + 1) * 128], ident)
                        S.cp(xs[:], pt2[:])
                        stream, g = idx // 2, idx % 2
                        dst = (p_cmp, p_sel, p_win)[stream]
                        if stream < 2:
                            S.dma("sp", dst[l, g, tsl, 0, :].rearrange("(a t) d -> t a d", a=4),
                                  xs[:].rearrange("p (a d) -> p a d", a=4))
                        elif tg == NG - 1:
                            S.dma("sp", dst[l, g, :, 0, :].rearrange("(a t) d -> t a d", a=4),
                                  xs[:].rearrange("p (a d) -> p a d", a=4))
                elif kind == "vc":
                    S.act(vcT[:, idx, tsl], pt[:], AF.Identity)
                elif kind == "gate":
                    S.act(gatT[0:24, tsl], pt[0:24, :], AF.Sigmoid)
                else:
                    xs = stg[idx % 2]
                    S.act(xs[:], pt[:], AF.Identity)
                    S.dma("sp", cin[idx, :, tsl], xs[:])
            chk("fm")
            tm_blocks = [(O_VC, 256, "v", 0), (O_VS, 256, "v", 1), (O_VW, 256, "v", 2),
                         (O_A, 16, "ab", 0), (O_Z, 512, "z", 0), (O_Z + 512, 512, "z", 1)]
            for col0, ncol, kind, idx in tm_blocks:
                buf = wb[wi % 2]
                wi += 1
                wv = buf[:, 0:NCH * ncol].rearrange("p (c n) -> p c n", c=NCH)
                wload(wv, win_l[:, :, col0:col0 + ncol])
                for ti in range(4):
                    tt_ = tg * 4 + ti
                    rows = slice(tt_ * 128, (tt_ + 1) * 128)
                    pt = nps()
                    for kc in range(NCH):
                        S.mm(pt[:, 0:ncol], lhsT=hT[:, kc, ti * 128:(ti + 1) * 128], rhs=wv[:, kc, :],
                             start=(kc == 0), stop=(kc == NCH - 1))
                    xs = stg[ti % 2]
                    S.act(xs[:, 0:ncol], pt[:, 0:ncol], AF.Identity)
                    if kind == "v":
                        dst = (p_cmp, p_sel, p_win)[idx]
                        if idx < 2:
                            S.dma("sp", dst[l, :, rows, 1, :].rearrange("g t d -> t g d"),
                                  xs[:, 0:256].rearrange("p (g d) -> p g d", g=2))
                        elif tt_ >= NT - 4:
                            r2 = slice((tt_ - (NT - 4)) * 128, (tt_ - (NT - 4) + 1) * 128)
                            S.dma("sp", dst[l, :, r2, 1, :].rearrange("g t d -> t g d"),
                                  xs[:, 0:256].rearrange("p (g d) -> p g d", g=2))
                        if idx >= 1:
                            S.cp(vtm[:, tt_, (idx - 1) * 2:(idx - 1) * 2 + 2, :],
                                 xs[:, 0:256].rearrange("p (g d) -> p g d", g=2), eng="pool")
                    elif kind == "ab":
                        S.dma("sp", abz[rows, 0:16], xs[:, 0:16])
                    else:
                        S.dma("sp", abz[rows, 16 + idx * 512:16 + (idx + 1) * 512], xs[:, 0:512])
            chk("tm")
        for c in range(24):
            S.dma("sp", p_conv[l, :, c * 128:(c + 1) * 128].rearrange("w p -> p w"), cin[c, :, T - 3:T],
                  allow_slow_non_contiguous=True)

        chk("A")
        z = stg[0][:].bitcast(BF16)
        S.memset(z, 0.0)
        zc = range(16)
        if cfg.get("nsa", True):
            nsa_phase(l)
            zc = range(8, 16)
        if cfg.get("gdn", True):
            zc = range(0, 8) if not cfg.get("nsa", True) else ()
        for c in zc:
            for j in range(2):
                S.dma("sp", mixT[:, c, j * 1024:(j + 1) * 1024], z)
        if cfg.get("gdn", True):
            gdn_phase(l)
        chk("B")

        wout_l = w_out[l].rearrange("(c p) n -> p c n", p=128)
        wg_l = w_gate[l].rearrange("(c p) n -> p c n", p=128)
        wu_l = w_up[l].rearrange("(c p) n -> p c n", p=128)
        wd_l = w_down[l].rearrange("(c p) n -> p c n", p=128)
        ydst = y_p if l == nlayers - 1 else xbuf2
        for tg in range(NG):
            tsl = slice(tg * TG, (tg + 1) * TG)
            mg = actA
            S.dma("sp", mg[:, 0:8, :], mixT[:, 0:8, tsl])
            S.dma("sp", mg[:, 8:16, :], mixT[:, 8:16, tsl])
            make_gbc(4, 0)
            h2T = actB
            for cb in range(4):
                buf = wb[wi % 2]
                wi += 1
                wv = buf[:, 0:8192].rearrange("p (c n) -> p c n", c=NCH)
                wload(wv, wout_l[:, :, cb * 512:(cb + 1) * 512])
                for ti in range(4):
                    pt = nps()
                    for kc in range(NCH):
                        S.mm(pt[:], lhsT=mg[:, kc, ti * 128:(ti + 1) * 128], rhs=wv[:, kc, :],
                             start=(kc == 0), stop=(kc == NCH - 1))
                    S.act(fsb[:, ti, cb * 512:(cb + 1) * 512], pt[:], AF.Identity)
            for ti in range(4):
                tt_ = tg * 4 + ti
                rows = slice(tt_ * 128, (tt_ + 1) * 128)
                S.dma("sp", xt[:], xsrc[rows, :])
                banks = [fsb[:, ti, j * 512:(j + 1) * 512] for j in range(4)]
                post_norm_residual(banks, xt, xt)
                S.dma("sp", xbuf[rows, :], xt[:])
                norm_to_fm(xt, h2T, ti * 128, 2, 3, 0)
            chk("wout")
            for n in range(DFF // 128):
                bg = wb[wi % 2]
                wi += 1
                wvg = bg[:, 0:2048].rearrange("p (c n) -> p c n", c=NCH)
                wvu = bg[:, 2048:4096].rearrange("p (c n) -> p c n", c=NCH)
                wload(wvg, wg_l[:, :, n * 128:(n + 1) * 128], nsplit=1)
                wload(wvu, wu_l[:, :, n * 128:(n + 1) * 128], nsplit=1)
                pg = nps()
                for kc in range(NCH):
                    S.mm(pg[:], lhsT=wvg[:, kc, :], rhs=h2T[:, kc, :], start=(kc == 0), stop=(kc == NCH - 1))
                pu = nps()
                for kc in range(NCH):
                    S.mm(pu[:], lhsT=wvu[:, kc, :], rhs=h2T[:, kc, :], start=(kc == 0), stop=(kc == NCH - 1))
                sg = stg[n % 2]
                S.act(sg[:], pg[:], AF.Silu)
                S.tt(aT[:, n, :], sg[:], pu[:], ALU.mult)
            chk("gateup")
            make_gbc(5, 0)
            for cb in range(4):
                halves = []
                for hh in range(2):
                    buf = wb[wi % 2]
                    wi += 1
                    wv = buf[:, 0:22 * 256].rearrange("p (c n) -> p c n", c=22)
                    halves.append(wv)
                for half in range(2):
                    c0 = cb * 512 + half * 256
                    for hh in range(2):
                        wload(halves[hh], wd_l[:, hh * 22:(hh + 1) * 22, c0:c0 + 256])
                    for ti in range(4):
                        pt = nps()
                        for kc in range(44):
                            S.mm(pt[:, 0:256], lhsT=aT[:, kc, ti * 128:(ti + 1) * 128],
                                 rhs=halves[kc // 22][:, kc % 22, :], start=(kc == 0), stop=(kc == 43))
                        S.act(fsb[:, ti, c0:c0 + 256], pt[:, 0:256], AF.Identity)
            for ti in range(4):
                tt_ = tg * 4 + ti
                rows = slice(tt_ * 128, (tt_ + 1) * 128)
                S.dma("sp", xt[:], xbuf[rows, :])
                banks = [fsb[:, ti, j * 512:(j + 1) * 512] for j in range(4)]
                post_norm_residual(banks, xt, xt)
                S.dma("sp", ydst[rows, :], xt[:])
            chk("D%d" % tg)
        if cfg.get("sample", True):
            sample_layer(l)
    if cfg.get("sample", True):
        S.dma("sp", y_s.rearrange("(c p) -> p c", p=128), xsam[:], allow_slow_non_contiguous=True)


def _host_consts():
    ident = np.eye(128, dtype=np.float32)
    R = np.zeros((128, 128), np.float32)
    for i in range(16):
        R[i, 16 + i] = -1.0
        R[16 + i, i] = 1.0
    rotT = np.ascontiguousarray(R.T)
    ones = np.ones((128, 128), np.float32)
    consts = np.ascontiguousarray(np.stack([ident, rotT, ones], axis=1))
    half = 16
    inv = (500000.0 ** (-np.arange(half, dtype=np.float32) * 2.0 / 32)).astype(np.float32)
    pos = np.arange(T, dtype=np.float32)
    ang = (pos[None, :] * inv[:, None]).astype(np.float32)
    cos, sin = np.cos(ang).astype(np.float32), np.sin(ang).astype(np.float32)
    C = np.ones((128, T), np.float32)
    Sn = np.zeros((128, T), np.float32)
    C[0:16], C[16:32] = cos, cos
    Sn[0:16], Sn[16:32] = sin, sin
    ropeFM = np.ascontiguousarray(np.stack([C, Sn], axis=1))
    return consts, ropeFM


def _host_tables():
    import ml_dtypes
    bf = ml_dtypes.bfloat16
    t = np.arange(T)
    j = np.arange(32)
    cm = ((j[None, :] * 64 + 63) <= t[:, None]).astype(np.float32)
    cur = t // 64
    forced_v = np.zeros((T, 32), np.float32)
    forced_v[np.arange(T)[cur >= 1], (cur - 1)[cur >= 1]] = 10.0
    forced_v[np.arange(T), cur] = 11.0
    forced_v[:, 0] = 12.0
    future = j[None, :] > cur[:, None]
    keep = ((forced_v == 0) & (~future)).astype(np.float32)
    add = np.where(future, -1.0, forced_v).astype(np.float32)
    tab = np.stack([cm, keep, add], 0).reshape(3, NT, 128, 32).transpose(2, 0, 1, 3)
    cm01T = np.ascontiguousarray(cm.T)
    key = np.arange(128)
    eexp = np.zeros((32, 16, 128), np.float32)
    for kt in range(16):
        eexp[2 * kt + key // 64, kt, key] = 1.0
    gsel = np.zeros((24, 24, 128), np.float32)
    for c in range(24):
        gsel[c, c, :] = 1.0
    tp = np.arange(512)
    caus = np.zeros((128, 4, 512), np.float32)
    for a in range(4):
        caus[:, a, :] = ((a * 128 + key)[:, None] <= tp[None, :])
    wmask = np.zeros((128, 8, 512), np.float32)
    for a in range(8):
        rel = ((a - 4) * 128 + key)[:, None]
        wmask[:, a, :] = (rel <= tp[None, :]) & (rel >= tp[None, :] - 512)
    return dict(tabA=np.ascontiguousarray(tab), cm01T=cm01T, eexp=eexp.astype(bf), gsel=gsel,
                caus=caus.astype(bf), wmask=wmask.astype(bf))


def kernel(x_prompt, x_sample, cache_cmp_kv, cache_sel_kv, cache_win_kv, state_gdn, state_conv, page_table,
           c_prompt, c_sample, w_ada, b_ada, g_pre_mix, w_in, cmp_pe, cmp_w1, cmp_w2, conv_w, gdn_a_log,
           gdn_dt_bias, gdn_norm, w_out, g_post_mix, g_pre_ffn, w_gate, w_up, w_down, g_post_ffn, _cfg=None):
    cfg = _cfg or {}
    f = lambda a: np.ascontiguousarray(np.asarray(a, dtype=np.float32))
    nc, S = build_program(cfg)
    consts, ropeFM = _host_consts()
    badaT = f(np.asarray(b_ada).reshape(DEPTH, 96, 128).transpose(2, 0, 1))
    gains = np.stack([np.asarray(g) for g in (g_pre_mix, g_post_mix, g_pre_ffn, g_post_ffn)], 0)
    gainsT = f(gains.reshape(4, DEPTH, NCH, 128).transpose(3, 0, 1, 2))
    shared = dict(w_ada=f(w_ada), w_in=f(w_in), w_out=f(w_out), w_gate=f(w_gate), w_up=f(w_up),
                  w_down=f(w_down), consts=consts, ropeFM=ropeFM, badaT=badaT, gainsT=gainsT,
                  cmp_peT=f(np.asarray(cmp_pe).transpose(0, 3, 1, 2)), cmp_w1=f(cmp_w1), cmp_w2=f(cmp_w2))
    shared.update(_host_tables())
    k_ = np.arange(64)
    gconst = np.zeros((64, 5, 128), np.float32)
    gconst[:, 0, 0:64] = (k_[:, None] <= k_[None, :])
    gconst[63, 1, :] = 1.0
    gconst[:, 2, 0:64] = (k_[None, :] < k_[:, None])
    gconst[:, 3, 0:64] = (k_[None, :] >= k_[:, None])
    shared["gconst"] = gconst
    shared["convwT"] = f(np.asarray(conv_w).reshape(DEPTH, 4, 24, 128).transpose(3, 0, 2, 1))
    gvec = np.stack([np.asarray(gdn_a_log), np.asarray(gdn_dt_bias)], 1)
    shared["gvec"] = f(np.broadcast_to(gvec[None], (64, DEPTH, 2, 8)))
    shared["gnormb"] = f(np.broadcast_to(np.asarray(gdn_norm)[None], (64, DEPTH, 128)))
    inv = (500000.0 ** (-np.arange(16, dtype=np.float32) * 2.0 / 32)).astype(np.float32)
    angS = (np.float32(16384.0) * inv).astype(np.float32)
    shared["ropeS"] = f(np.stack([np.cos(angS), np.sin(angS)], 0)[None])
    jj = np.arange(128)
    dsel = np.zeros((128, 2, 2, 128), np.float32)
    for jh in range(2):
        for a in range(2):
            for pg in range(128):
                j = 2 * pg + a
                if j // 128 == jh:
                    dsel[j % 128, jh, a, pg] = 1.0
    hsel = np.zeros((128, 2, 128), np.float32)
    hsel[:, 0, 0:64] = 1.0
    hsel[:, 1, 64:128] = 1.0
    shared["dsel"] = dsel
    shared["hsel"] = hsel
    shared["iota"] = np.arange(128, dtype=np.int32)[:, None].copy()
    shared["cache_cmp"] = f(cache_cmp_kv).reshape(-1, 256)
    shared["cache_sel"] = f(cache_sel_kv).reshape(-1, 256)
    cores = cfg.get("cores", list(range(8)))
    in_maps = []
    for i in cores:
        b = i % 4
        cT = np.stack([np.asarray(c_prompt)[b].reshape(NCH, 128).T, np.asarray(c_sample)[i].reshape(NCH, 128).T], -1)
        m = dict(shared)
        m["x_p"] = f(np.asarray(x_prompt)[b])
        m["cT"] = f(cT)
        m["xsT"] = f(np.asarray(x_sample)[i, 0].reshape(NCH, 128).T)
        m["ptab"] = np.ascontiguousarray(np.asarray(page_table)[i][None, :].astype(np.int32))
        m["cache_win"] = f(np.asarray(cache_win_kv)[i]).reshape(DEPTH, 2, 512, 256)
        m["sgdn"] = f(np.asarray(state_gdn)[i])
        m["sconv"] = f(np.asarray(state_conv)[i])
        m["sconvT"] = f(np.asarray(state_conv)[i].reshape(DEPTH, 3, 24, 128).transpose(0, 3, 2, 1))
        in_maps.append(m)
    res = run_bass_kernel_spmd(nc, in_maps, core_ids=list(range(len(cores))))
    if len(cores) < 8:
        return {c: res.results[k] for k, c in enumerate(cores)}
    R = res.results
    B = 4
    y_prompt = np.stack([R[b]["y_p"] for b in range(B)], 0)
    p_cmp = np.stack([R[b]["p_cmp"] for b in range(B)], 0)
    p_sel = np.stack([R[b]["p_sel"] for b in range(B)], 0)
    p_win = np.stack([R[b]["p_win"] for b in range(B)], 0)
    p_conv = np.stack([R[b]["p_conv"] for b in range(B)], 0)
    p_gdn = np.stack([R[b]["p_gdn"] for b in range(B)], 0)
    y_sample = np.stack([R[i]["y_s"].reshape(1, D) for i in range(8)], 0)
    s_cmp = np.stack([R[i]["s_cmp"] for i in range(8)], 0)
    s_sel = np.stack([R[i]["s_sel"] for i in range(8)], 0)
    s_win = np.stack([R[i]["s_win"].reshape(DEPTH, 2, 512, 2, 128) for i in range(8)], 0)
    s_gdn = np.stack([R[i]["s_gdn"] for i in range(8)], 0)
    s_conv = np.stack([R[i]["s_conv"] for i in range(8)], 0)
    return (y_prompt, y_sample, p_cmp, p_sel, p_win, p_gdn, p_conv, s_cmp, s_sel, s_win, s_gdn, s_conv)
```

```python
import numpy as np
import concourse.bass as bass
import concourse.mybir as mybir
from concourse.bass_utils import run_bass_kernel_spmd

F32 = mybir.dt.float32
BF16 = mybir.dt.bfloat16
I32 = mybir.dt.int32
AF = mybir.ActivationFunctionType
ALU = mybir.AluOpType
AX = mybir.AxisListType

ENGS = ("pe", "act", "dve", "pool", "sp")

D = 2048
T = 2048
NT = T // 128
DEPTH = 2
DFF = 5632
INW = 6696
NCH = D // 128
EPS = 1e-6
TG = 512
NG = T // TG
O_Q, O_KC, O_VC, O_KS, O_VS, O_KW, O_VW, O_GL, O_CONV, O_A, O_B, O_Z = (
    0, 1024, 1280, 1536, 1792, 2048, 2304, 2560, 2584, 5656, 5664, 5672)


def _isz(dt):
    return mybir.dt.size(dt)


class Sched:
    def __init__(self, nc, n_dma=16):
        self.nc = nc
        self.prog = {e: [] for e in ENGS}
        self.esem = {e: nc.alloc_semaphore("es_" + e) for e in ENGS}
        self.ecnt = {e: 0 for e in ENGS}
        self.known = {e: {} for e in ENGS}
        self.pool = {q: [[nc.alloc_semaphore("ds_%s%d" % (q, i)), 0] for i in range(n_dma)]
                     for q in ("sp", "pool")}
        self.pidx = {q: 0 for q in self.pool}
        self.recs = {}
        self.readonly = set()
        self.bank_granular = set()
        self.fsize = {}
        self.n_inst = 0

    def sb(self, name, shape, dt):
        t = self.nc.alloc_sbuf_tensor(name, list(shape), dt)
        self.fsize[name] = int(np.prod(shape[1:])) * _isz(dt)
        return t

    def ps(self, name, shape, dt=F32):
        t = self.nc.alloc_psum_tensor(name, list(shape), dt)
        self.fsize[name] = int(np.prod(shape[1:])) * _isz(dt)
        self.bank_granular.add(name)
        return t

    def region(self, ap):
        name = ap.name
        steps = ap.ap
        es = _isz(ap.dtype)
        off = ap.offset * es
        if name in self.fsize:
            F = self.fsize[name]
            if name in self.bank_granular:
                return name, 0, 128, 0, F
            p0 = off // F
            f0 = off % F
            pc = steps[0][1]
            ext = es
            for s, c in steps[1:]:
                ext += (c - 1) * abs(s) * es
            return name, p0, p0 + pc, f0, f0 + ext
        ext = es
        for s, c in steps:
            ext += (c - 1) * abs(s) * es
        return name, 0, 1, off, off + ext

    def _deps_and_record(self, reads, writes, tag):
        deps = []
        for is_w, aps in ((False, reads), (True, writes)):
            for ap in aps:
                if ap.name in self.readonly:
                    continue
                name, p0, p1, f0, f1 = self.region(ap)
                lst = self.recs.get(name, ())
                keep = []
                psum = name in self.bank_granular
                for r in lst:
                    rp0, rp1, rf0, rf1, rw, rtag = r
                    ov = not (rp1 <= p0 or p1 <= rp0 or rf1 <= f0 or f1 <= rf0)
                    if ov and (rw or is_w or (psum and rtag[2] != tag[2])):
                        deps.append(rtag)
                    contained = rp0 >= p0 and rp1 <= p1 and rf0 >= f0 and rf1 <= f1
                    if contained and (is_w or ((not rw) and rtag[2] == tag[2] and not tag[2].startswith("dma"))):
                        continue
                    keep.append(r)
                keep.append((p0, p1, f0, f1, is_w, tag))
                self.recs[name] = keep
        return deps

    def _waits(self, eng, deps):
        waits = []
        kn = self.known[eng]
        for sem, val, src in deps:
            if src == eng and eng == "pe":
                continue
            key = sem.num
            if kn.get(key, 0) >= val:
                continue
            kn[key] = val
            waits.append((sem, val))
        return waits

    def op(self, eng, fn, reads=(), writes=()):
        self.ecnt[eng] += 1
        sem = self.esem[eng]
        tag = (sem, self.ecnt[eng], eng)
        deps = self._deps_and_record(reads, writes, tag)
        deps = [d for d in deps if d is not tag]
        waits = self._waits(eng, deps)
        import sys as _sys
        fr = _sys._getframe(1)
        ln = []
        while fr is not None and len(ln) < 4:
            ln.append(fr.f_lineno)
            fr = fr.f_back
        self.prog[eng].append((waits, fn, sem, 1, ln))
        self.n_inst += 1

    def dma(self, q, out, in_, **kw):
        pool = self.pool[q]
        slot = pool[self.pidx[q] % len(pool)]
        self.pidx[q] += 1
        deps = []
        if slot[1] > 0:
            deps.append((slot[0], slot[1], "dma:" + q))
        slot[1] += 16
        tag = (slot[0], slot[1], "dma:" + q)
        deps += self._deps_and_record([in_], [out], tag)
        deps = [d for d in deps if d is not tag]
        waits = self._waits(q, deps)
        self.prog[q].append((waits, (lambda e, o=out, i=in_, k=kw: e.dma_start(out=o, in_=i, **k)), slot[0], 16))
        self.n_inst += 1

    def mm(self, out, lhsT, rhs, start=True, stop=True):
        self.op("pe", lambda e: e.matmul(out, lhsT=lhsT, rhs=rhs, start=start, stop=stop),
                reads=[lhsT, rhs], writes=[out])

    def tr(self, out, in_, ident):
        self.op("pe", lambda e: e.matmul(out, lhsT=in_, rhs=ident, start=True, stop=True),
                reads=[in_, ident], writes=[out])

    def act(self, out, in_, func, scale=1.0, bias=None, accum=None):
        rd = [in_]
        wr = [out]
        kw = {}
        if bias is not None:
            kw["bias"] = bias
            if not isinstance(bias, float):
                rd.append(bias)
        if not isinstance(scale, float):
            rd.append(scale)
        if accum is not None:
            kw["accum_out"] = accum
            wr.append(accum)
        self.op("act", lambda e: e.activation(out=out, in_=in_, func=func, scale=scale, **kw), reads=rd, writes=wr)

    def ts(self, out, in0, s1, s2=None, op0=ALU.mult, op1=None, eng="dve"):
        rd = [in0] + [s for s in (s1, s2) if s is not None and not isinstance(s, float)]
        kw = {}
        if op1 is not None:
            kw["op1"] = op1
        self.op(eng, lambda e: e.tensor_scalar(out=out, in0=in0, scalar1=s1, scalar2=s2, op0=op0, **kw),
                reads=rd, writes=[out])

    def tt(self, out, in0, in1, op, eng="dve"):
        self.op(eng, lambda e: e.tensor_tensor(out=out, in0=in0, in1=in1, op=op), reads=[in0, in1], writes=[out])

    def stt(self, out, in0, scalar, in1, op0, op1, eng="dve"):
        rd = [in0, in1] + ([] if isinstance(scalar, float) else [scalar])
        self.op(eng, lambda e: e.scalar_tensor_tensor(out=out, in0=in0, scalar=scalar, in1=in1, op0=op0, op1=op1),
                reads=rd, writes=[out])

    def cp(self, out, in_, eng="dve"):
        self.op(eng, lambda e: e.tensor_copy(out=out, in_=in_), reads=[in_], writes=[out])

    def memset(self, out, val, eng="pool"):
        self.op(eng, lambda e: e.memset(out, val), writes=[out])

    def finish(self):
        fin = []
        for q in self.pool:
            for sem, val in self.pool[q]:
                if val > 0:
                    fin.append((sem, val))
        nc = self.nc
        prog = self.prog

        def run(lst, e, extra=()):
            for item in lst:
                waits, fn, sem, inc = item[:4]
                for s, v in waits:
                    e.wait_ge(s, v)
                try:
                    ins = fn(e)
                except Exception:
                    print("EMIT FAIL at lines", item[4] if len(item) > 4 else None, flush=True)
                    raise
                ins.then_inc(sem, inc)
            for s, v in extra:
                e.wait_ge(s, v)

        with nc.Block() as block:
            @block.tensor
            def _(e):
                run(prog["pe"], e)

            @block.scalar
            def _(e):
                run(prog["act"], e)

            @block.vector
            def _(e):
                run(prog["dve"], e)

            @block.gpsimd
            def _(e):
                run(prog["pool"], e)

            @block.sync
            def _(e):
                run(prog["sp"], e, fin)


class _Stop(Exception):
    pass


def build_program(cfg):
    nc = bass.Bass("TRN2", target_bir_lowering=False)
    S = Sched(nc)
    try:
        _build_body(nc, S, cfg)
    except _Stop:
        pass
    S.finish()
    return nc, S


def _build_body(nc, S, cfg):
    def chk(name):
        if cfg.get("stop") == name:
            raise _Stop()

    def din(name, shape, dt=F32):
        S.readonly.add(name)
        return nc.dram_tensor(name, list(shape), dt, kind="ExternalInput").ap()

    def dout(name, shape, dt=F32):
        return nc.dram_tensor(name, list(shape), dt, kind="ExternalOutput").ap()

    def dscr(name, shape, dt=F32):
        return nc.dram_tensor(name, list(shape), dt, kind="Internal").ap()

    x_in = din("x_p", [T, D])
    cT_in = din("cT", [128, NCH, 2])
    bada_in = din("badaT", [128, DEPTH, 96])
    gains_in = din("gainsT", [128, 4, DEPTH, NCH])
    w_ada = din("w_ada", [DEPTH, D, 6 * D])
    w_in = din("w_in", [DEPTH, D, INW])
    w_out = din("w_out", [DEPTH, D, D])
    w_gate = din("w_gate", [DEPTH, D, DFF])
    w_up = din("w_up", [DEPTH, D, DFF])
    w_down = din("w_down", [DEPTH, DFF, D])
    consts_in = din("consts", [128, 3, 128])
    ropeFM_in = din("ropeFM", [128, 2, T])
    xsT_in = din("xsT", [128, NCH])
    ptab_in = din("ptab", [1, 128], I32)
    iota_in = din("iota", [128, 1], I32)
    cache_cmp = din("cache_cmp", [1280 * 2 * 2 * 128, 256])
    cache_sel = din("cache_sel", [1280 * 2 * 2 * 128, 256])
    cache_win = din("cache_win", [DEPTH, 2, 512, 256])
    sgdn_in = din("sgdn", [DEPTH, 8, 128, 128])
    sconv_in = din("sconv", [DEPTH, 3, 3072])
    sconvT_in = din("sconvT", [DEPTH, 128, 24, 3])
    ropeS_in = din("ropeS", [1, 2, 16])
    dsel_in = din("dsel", [128, 2, 2, 128])
    hsel_in = din("hsel", [128, 2, 128])
    gconst_in = din("gconst", [64, 5, 128])
    convwT_in = din("convwT", [128, DEPTH, 24, 4])
    gvec_in = din("gvec", [64, DEPTH, 2, 8])
    gnorm_in = din("gnormb", [64, DEPTH, 128])
    cmp_peT_in = din("cmp_peT", [DEPTH, 128, 2, 64])
    cmp_w1 = din("cmp_w1", [DEPTH, 2, 8192, 256])
    cmp_w2 = din("cmp_w2", [DEPTH, 2, 256, 128])
    tabA_in = din("tabA", [128, 3, NT, 32])
    cm01T_in = din("cm01T", [32, T])
    eexp_in = din("eexp", [32, 16, 128], BF16)
    gsel_in = din("gsel", [24, 24, 128])
    caus_in = din("caus", [128, 4, 512], BF16)
    wmask_in = din("wmask", [128, 8, 512], BF16)

    y_p = dout("y_p", [T, D])
    p_cmp = dout("p_cmp", [DEPTH, 2, T, 2, 128])
    p_sel = dout("p_sel", [DEPTH, 2, T, 2, 128])
    p_win = dout("p_win", [DEPTH, 2, 512, 2, 128])
    p_conv = dout("p_conv", [DEPTH, 3, 3072])
    p_gdn = dout("p_gdn", [DEPTH, 8, 128, 128])
    y_s = dout("y_s", [D])
    s_cmp = dout("s_cmp", [DEPTH, 2, 1, 2, 128])
    s_sel = dout("s_sel", [DEPTH, 2, 1, 2, 128])
    s_win = dout("s_win", [DEPTH, 2, 512, 256])
    s_gdn = dout("s_gdn", [DEPTH, 8, 128, 128])
    s_conv = dout("s_conv", [DEPTH, 3, 3072])
    gscr = dscr("gscr", [24])
    dbg_s = dout("dbg_s", [128, 64]) if cfg.get("dbg_mix") else None

    xbuf = dscr("xbuf", [T, D])
    xbuf2 = dscr("xbuf2", [T, D])
    mixT = (dout if cfg.get("dbg_mix") else dscr)("mixT", [128, 16, T], BF16)
    cin = dscr("cin", [24, 128, T])
    abz = dscr("abz", [T, 16 + 1024])

    cst = S.sb("cst", [128, 3, 128], F32)
    ident = cst[:, 0, :]
    rotT = cst[:, 1, :]
    ones = cst[:, 2, :]
    small = S.sb("small", [128, 1024], F32)
    cs = small[:, 0:32].rearrange("p (c n) -> p c n", n=2)
    bada = small[:, 32:224].rearrange("p (l n) -> p l n", l=DEPTH)
    gains = small[:, 224:352].rearrange("p (k l c) -> p k l c", k=4, l=DEPTH)
    modT = small[:, 352:736].rearrange("p (l n s) -> p l n s", l=DEPTH, s=2)
    eff = small[:, 736:928].rearrange("p (k c s) -> p k c s", k=6, s=2)
    stat = small[:, 928:1024]
    epsb = stat[:, 0:1]
    gbc = S.sb("gbc", [128, D], F32)
    xsam = S.sb("xsam", [128, NCH], F32)
    sm2 = S.sb("sm2", [128, 256], F32)
    pti = S.sb("pti", [128, 132], I32)
    xt = S.sb("xt", [128, D], F32)
    xt2 = S.sb("xt2", [128, D], F32)
    actA = S.sb("actA", [128, NCH, TG], BF16)
    actB = S.sb("actB", [128, NCH, TG], BF16)
    wb = [S.sb("wb%d" % i, [128, 8192], BF16) for i in range(2)]
    stg = [S.sb("stg%d" % i, [128, 512], F32) for i in range(2)]
    ARENA = int(cfg.get('arena_kb', 104)) * 1024
    arena = S.sb("arena", [128, ARENA // 4], F32)
    PS = [S.ps("ps%d" % i, [128, 512]) for i in range(8)]
    psi = [0]

    def nps():
        psi[0] += 1
        return PS[psi[0] % 8]

    def carve(off_bytes, shape, dt):
        n = int(np.prod(shape[1:]))
        e0 = off_bytes // 4
        e1 = e0 + (n * _isz(dt) + 3) // 4
        v = arena[:, e0:e1]
        if dt != F32:
            v = v.bitcast(dt)
        if len(shape) == 3:
            v = v.rearrange("p (a b) -> p a b", a=shape[1])
        elif len(shape) == 4:
            v = v.rearrange("p (a b c) -> p a b c", a=shape[1], b=shape[2])
        return v

    KB = 1024
    qT = carve(0, [128, 8, T], BF16)
    kT = carve(32 * KB, [128, 6, T], BF16)
    vcT = carve(56 * KB, [128, 2, T], BF16)
    vtm = carve(64 * KB, [128, NT, 4, 128], BF16)
    gatT = carve(80 * KB, [128, T], F32)
    ropeT = carve((88 if ARENA >= 104 * KB else 72) * KB, [128, 2, T], F32)
    aT = carve(0, [128, 44, TG], BF16)
    fsb = carve(44 * KB, [128, 4, D], F32)

    wq = ["pool"]

    def wload(buf_view, dram_view, nsplit=2, q="pool"):
        kc = dram_view.shape[1]
        step = max(1, kc // nsplit)
        for k0 in range(0, kc, step):
            k1 = min(kc, k0 + step)
            S.dma(q, buf_view[:, k0:k1, :], dram_view[:, k0:k1, :])

    S.dma("sp", cst[:], consts_in)
    S.dma("sp", cs, cT_in)
    S.dma("sp", bada, bada_in)
    S.dma("sp", gains, gains_in)
    S.memset(epsb, EPS)

    S.act(cs, cs, AF.Silu)
    wi = 0
    for l in range(DEPTH):
        wsrc = w_ada[l].rearrange("(c p) n -> p c n", p=128)
        for n in range(96):
            buf = wb[wi % 2]
            wi += 1
            wv = buf[:, 0:4096].bitcast(F32).rearrange("p (c n) -> p c n", c=NCH)
            wload(wv, wsrc[:, :, n * 128:(n + 1) * 128], nsplit=2, q="sp")
            pt = nps()
            for kc in range(NCH):
                S.mm(pt[:, 0:2], lhsT=wv[:, kc, :], rhs=cs[:, kc, :], start=(kc == 0), stop=(kc == NCH - 1))
            S.act(modT[:, l, n, :], pt[:, 0:2], AF.Identity, bias=bada[:, l, n:n + 1])

    chk("ada")

    def layer_eff(l):
        for s in range(2):
            for (k, sc_off, gk) in ((0, 16, 0), (2, 64, 2)):
                S.ts(eff[:, k, :, s], modT[:, l, sc_off:sc_off + 16, s], 1.0, None, op0=ALU.add)
                S.tt(eff[:, k, :, s], eff[:, k, :, s], gains[:, gk, l, :], ALU.mult)
            S.cp(eff[:, 1, :, s], modT[:, l, 0:16, s])
            S.cp(eff[:, 3, :, s], modT[:, l, 48:64, s])
            S.tt(eff[:, 4, :, s], modT[:, l, 32:48, s], gains[:, 1, l, :], ALU.mult)
            S.tt(eff[:, 5, :, s], modT[:, l, 80:96, s], gains[:, 3, l, :], ALU.mult)

    def make_gbc(k, s):
        for c4 in range(4):
            pt = nps()
            for cc in range(4):
                c = c4 * 4 + cc
                tmp = stg[0][:, cc * 128:(cc + 1) * 128]
                S.ts(tmp, ident, eff[:, k, c, s:s + 1], None, op0=ALU.mult)
                S.mm(pt[:, cc * 128:(cc + 1) * 128], lhsT=ones, rhs=tmp)
            S.cp(gbc[:, c4 * 512:(c4 + 1) * 512], pt[:], eng="dve")

    def rstd_of(ssum, n):
        S.act(ssum, ssum, AF.Ln, scale=1.0 / n, bias=epsb)
        S.act(ssum, ssum, AF.Exp, scale=-0.5)

    def norm_to_fm(xtile, dst, col0, ka, kb, s):
        ss = stat[:, 1:2]
        junk = xt2
        S.memset(ss, 0.0)
        S.act(junk[:], xtile[:], AF.Square, accum=ss)
        rstd_of(ss, D)
        if cfg.get("nstop") == 1:
            raise _Stop()
        S.ts(junk[:], xtile[:], ss, None, op0=ALU.mult)
        if cfg.get("nstop") == 2:
            raise _Stop()
        for c4 in range(4):
            pt = nps()
            for cc in range(4):
                c = c4 * 4 + cc
                S.tr(pt[:, cc * 128:(cc + 1) * 128], junk[:, c * 128:(c + 1) * 128], ident)
            if cfg.get("nstop") == 3:
                raise _Stop()
            for cc in range(4):
                c = c4 * 4 + cc
                S.ts(dst[:, c, col0:col0 + 128], pt[:, cc * 128:(cc + 1) * 128],
                     eff[:, ka, c, s:s + 1], eff[:, kb, c, s:s + 1], op0=ALU.mult, op1=ALU.add)
            if cfg.get("nstop") == 4:
                raise _Stop()
        if cfg.get("nstop") == 5:
            raise _Stop()

    def post_norm_residual(banks, xtile, out_tile):
        ss4 = stat[:, 4:8]
        S.memset(ss4, 0.0)
        for j in range(4):
            S.act(stg[1][:], banks[j][:], AF.Square, accum=ss4[:, j:j + 1])
        ss = stat[:, 2:3]
        S.op("dve", lambda e: e.reduce_sum(out=ss, in_=ss4, axis=AX.X), reads=[ss4], writes=[ss])
        rstd_of(ss, D)
        for j in range(4):
            sl = slice(j * 512, (j + 1) * 512)
            S.stt(xt2[:, sl], banks[j][:], ss, gbc[:, sl], ALU.mult, ALU.mult)
            S.tt(out_tile[:, sl], xt2[:, sl], xtile[:, sl], ALU.add, eng="pool")


    def fview(base, dt_base, off_b, shape, dt):
        flat = base[:]
        if len(base.shape) == 3:
            flat = flat.rearrange("p a b -> p (a b)")
        es = _isz(dt_base)
        n = int(np.prod(shape[1:])) * _isz(dt)
        v = flat[:, off_b // es:(off_b + n) // es]
        if dt != dt_base:
            v = v.bitcast(dt)
        if len(shape) == 3:
            v = v.rearrange("p (a b) -> p a b", a=shape[1])
        elif len(shape) == 4:
            v = v.rearrange("p (a b c) -> p a b c", a=shape[1], b=shape[2])
        return v[0:shape[0]]

    ps4 = [0]

    def nps4():
        ps4[0] += 1
        return PS[ps4[0] % 4]

    SCALE = float(128 ** -0.5)
    tiny = stat[:, 3:4]
    S.memset(tiny, 1e-20)

    def rsum(out, in_):
        S.op("dve", lambda e: e.reduce_sum(out=out, in_=in_, axis=AX.X), reads=[in_], writes=[out])

    def nsa_phase(l):
        tabA = fview(xt, F32, 0, [128, 3, NT, 32], F32)
        sm = fview(xt, F32, 6144, [128, 512], F32)
        selT = fview(xt2, F32, 0, [32, 2, T], BF16)
        cmp3 = fview(gbc, F32, 0, [128, 32, 32], F32)
        e4 = fview(gbc, F32, 4096, [128, 4, 32], F32)
        p4 = fview(gbc, F32, 4608, [128, 4, 32], F32)
        Eexp = fview(wb[0], BF16, 0, [32, 16, 128], BF16)
        cm01T = fview(wb[0], BF16, 4096, [32, T], F32)
        gsel = fview(wb[1], BF16, 0, [24, 24, 128], F32)
        caus = fview(actA, BF16, 0, [128, 4, 512], BF16)
        wmask = fview(actA, BF16, 4096, [128, 8, 512], BF16)
        mexp = [fview(actA, BF16, 12288 + i * 1024, [128, 512], BF16) for i in range(2)]
        ebf = [fview(actA, BF16, 14336 + i * 1024, [128, 512], BF16) for i in range(2)]
        Xpe = fview(actB, BF16, 0, [128, 2, 32, 64], BF16)
        oacc = fview(actB, BF16, 0, [128, 512], F32)
        t1 = fview(actB, BF16, 2048, [128, 512], F32)
        rl = fview(actB, BF16, 4096, [128, 512], F32)
        t32 = fview(actB, BF16, 6144, [32, 512], F32)
        hid = fview(actB, BF16, 8192, [32, 256], F32)
        g1 = fview(actB, BF16, 9216, [32, 256], F32)
        g2 = fview(actB, BF16, 10240, [32, 256], F32)
        hidT = fview(actB, BF16, 11264, [128, 2, 32], F32)
        kcbT = fview(actB, BF16, 11520, [128, 2, 32], BF16)
        vcb = fview(actB, BF16, 11648, [32, 2, 128], BF16)
        ecT = fview(actB, BF16, 12288, [32, 512], BF16)
        pe_t = fview(actB, BF16, 13312, [128, 2, 64], F32)
        w2_t = fview(actB, BF16, 13824, [128, 2, 2, 128], F32)
        onesb = fview(actB, BF16, 15872, [128, 128], BF16)
        obf = stg[1][:].bitcast(BF16)[:, 0:512]

        S.memset(onesb, 1.0)
        S.dma("sp", pe_t, cmp_peT_in[l])
        S.dma("sp", w2_t, cmp_w2[l].rearrange("i (c p) d -> p i c d", p=128))
        for i in range(2):
            for g in range(2):
                src = (kT[:, g, :] if i == 0 else vcT[:, g, :]).rearrange("p (j r) -> p j r", r=64)
                S.tt(Xpe[:, g], src, pe_t[:, i, :].unsqueeze(1).broadcast_to([128, 32, 64]), ALU.add)
            accs = [PS[4], PS[5]]
            w1v = cmp_w1[l, i].rearrange("(r d) n -> d r n", d=128)
            for rq in range(4):
                buf = wb[rq % 2]
                wv = buf[:, 0:4096].rearrange("p (c n) -> p c n", c=16)
                wload(wv, w1v[:, rq * 16:(rq + 1) * 16, :])
                for g in range(2):
                    for rr in range(16):
                        r = rq * 16 + rr
                        S.mm(accs[g][0:32, 0:256], lhsT=Xpe[:, g, :, r], rhs=wv[:, rr, :],
                             start=(r == 0), stop=(r == 63))
            for g in range(2):
                S.act(hid, accs[g][0:32, 0:256], AF.Identity)
                S.tt(g1, hid, hid, ALU.mult)
                S.ts(g1, g1, 0.044715, 1.0, op0=ALU.mult, op1=ALU.add)
                S.tt(g1, g1, hid, ALU.mult)
                S.act(g2, g1, AF.Sigmoid, scale=1.5957691216057308)
                S.tt(hid, hid, g2, ALU.mult)
                pt = nps4()
                for c in range(2):
                    S.mm(pt[:, c * 32:(c + 1) * 32], lhsT=hid[:, c * 128:(c + 1) * 128], rhs=ident[0:32, 0:32])
                S.cp(hidT[:].rearrange("p c j -> p (c j)"), pt[:, 0:64])
                pt2 = nps4()
                if i == 0:
                    for c in range(2):
                        S.mm(pt2[:, 0:32], lhsT=w2_t[:, 0, c, :], rhs=hidT[:, c, :], start=(c == 0), stop=(c == 1))
                    S.cp(kcbT[:, g, :], pt2[:, 0:32])
                else:
                    for c in range(2):
                        S.mm(pt2[0:32, 0:128], lhsT=hidT[:, c, :], rhs=w2_t[:, 1, c, :], start=(c == 0), stop=(c == 1))
                    S.cp(vcb[:, g, :], pt2[0:32, 0:128])
        S.dma("sp", tabA, tabA_in)
        S.dma("sp", Eexp, eexp_in)
        S.dma("sp", cm01T, cm01T_in)
        S.dma("sp", gsel, gsel_in)
        S.dma("sp", caus, caus_in)
        S.dma("sp", wmask, wmask_in)
        for g in range(2):
            for tt_ in range(NT):
                tq = slice(tt_ * 128, (tt_ + 1) * 128)
                pt = nps4()
                for h in range(4):
                    S.mm(pt[:, h * 32:(h + 1) * 32], lhsT=qT[:, 4 * g + h, tq], rhs=kcbT[:, g, :])
                e4f = e4.rearrange("p h j -> p (h j)")
                S.act(e4f, pt[:, 0:128], AF.Exp, scale=SCALE)
                S.tt(e4, e4, tabA[:, 0, tt_, :].unsqueeze(1).broadcast_to([128, 4, 32]), ALU.mult)
                den = sm[:, 0:4]
                rsum(den, e4)
                S.act(den, den, AF.Ln, bias=tiny)
                S.act(den, den, AF.Exp, scale=-1.0)
                S.tt(p4, e4, den.unsqueeze(2).broadcast_to([128, 4, 32]), ALU.mult)
                imp = sm[:, 8:40]
                rsum(imp, p4.rearrange("p h j -> p j h"))
                score = sm[:, 40:72]
                S.tt(score, imp, tabA[:, 1, tt_, :], ALU.mult)
                S.tt(score, score, tabA[:, 2, tt_, :], ALU.add)
                S.tt(cmp3, score.unsqueeze(1).broadcast_to([128, 32, 32]),
                     score.unsqueeze(2).broadcast_to([128, 32, 32]), ALU.is_gt)
                rank = sm[:, 72:104]
                rsum(rank, cmp3)
                sel = sm[:, 104:136]
                S.ts(sel, rank, 15.5, None, op0=ALU.is_lt)
                pt2 = nps4()
                S.mm(pt2[0:32, 0:128], lhsT=sel, rhs=ident)
                S.cp(selT[:, g, tq], pt2[0:32, 0:128])

        def combine(num_ps, den_ps, gcol, tsl, first):
            S.act(rl, den_ps[:], AF.Ln, bias=tiny)
            S.act(rl, rl, AF.Exp, scale=-1.0)
            gps = nps4()
            S.mm(gps[:], lhsT=gsel[:, gcol, :], rhs=gatT[0:24, tsl])
            S.tt(t1, num_ps[:], rl, ALU.mult)
            if first:
                S.tt(oacc, t1, gps[:], ALU.mult)
            else:
                S.tt(t1, t1, gps[:], ALU.mult)
                S.tt(oacc, oacc, t1, ALU.add, eng="pool")

        hcount = 0
        for g in range(2):
            for tg in range(NG):
                tsl = slice(tg * TG, (tg + 1) * TG)
                for h in range(4):
                    hh = 4 * g + h
                    num, den_ = (PS[4], PS[5]) if hcount % 2 == 0 else (PS[6], PS[7])
                    hcount += 1
                    ps_s = nps4()
                    S.mm(ps_s[0:32, :], lhsT=kcbT[:, g, :], rhs=qT[:, hh, tsl])
                    S.act(t32, ps_s[0:32, :], AF.Exp, scale=SCALE)
                    S.tt(ecT, t32, cm01T[:, tsl], ALU.mult)
                    S.mm(den_[:], lhsT=onesb[0:32, :], rhs=ecT)
                    S.mm(num[:], lhsT=vcb[:, g, :], rhs=ecT)
                    combine(num, den_, hh * 3 + 0, tsl, True)
                    nk = 4 * tg + 4
                    for kt in range(nk):
                        pm = nps4()
                        S.mm(pm[:], lhsT=Eexp[:, kt, :], rhs=selT[:, g, tsl])
                        m = mexp[kt % 2]
                        if kt >= 4 * tg:
                            S.tt(m, pm[:], caus[:, kt - 4 * tg, :], ALU.mult)
                        else:
                            S.cp(m, pm[:])
                        pscore = nps4()
                        S.mm(pscore[:], lhsT=kT[:, 2 + g, kt * 128:(kt + 1) * 128], rhs=qT[:, hh, tsl])
                        e = ebf[kt % 2]
                        S.act(e, pscore[:], AF.Exp, scale=SCALE)
                        S.tt(e, e, m, ALU.mult)
                        S.mm(num[:], lhsT=vtm[:, kt, g, :], rhs=e, start=(kt == 0), stop=(kt == nk - 1))
                        S.mm(den_[:], lhsT=onesb, rhs=e, start=(kt == 0), stop=(kt == nk - 1))
                    combine(num, den_, hh * 3 + 1, tsl, False)
                    kts = [kt for kt in range(4 * tg - 4, 4 * tg + 4) if kt >= 0]
                    for ix, kt in enumerate(kts):
                        a = kt - (4 * tg - 4)
                        pscore = nps4()
                        S.mm(pscore[:], lhsT=kT[:, 4 + g, kt * 128:(kt + 1) * 128], rhs=qT[:, hh, tsl])
                        e = ebf[ix % 2]
                        S.act(e, pscore[:], AF.Exp, scale=SCALE)
                        S.tt(e, e, wmask[:, a, :], ALU.mult)
                        S.mm(num[:], lhsT=vtm[:, kt, 2 + g, :], rhs=e, start=(ix == 0), stop=(ix == len(kts) - 1))
                        S.mm(den_[:], lhsT=onesb, rhs=e, start=(ix == 0), stop=(ix == len(kts) - 1))
                    combine(num, den_, hh * 3 + 2, tsl, False)
                    S.act(obf, oacc, AF.Identity)
                    S.dma("sp", mixT[:, hh, tsl], obf)

    def bc(ap, shape, axis):
        return ap.unsqueeze(axis).broadcast_to(list(shape))

    def gdn_phase(l):
        ident64 = ident[0:64, 0:64]
        ones64 = ones[0:64, 0:64]
        gconst = fview(wb[0], BF16, 0, [64, 5, 128], F32)
        TriLE = gconst[:, 0, 0:64]
        Sel63 = gconst[:, 1, :]
        maskSL = gconst[:, 2, 0:64]
        maskUI = gconst[:, 3, 0:64]
        cw = fview(wb[0], BF16, 2560, [128, 24, 4], F32)
        gv = fview(wb[0], BF16, 3072, [64, 2, 8], F32)
        nexpA = fview(wb[0], BF16, 3200, [64, 8], F32)
        normw = fview(wb[0], BF16, 3584, [64, 128], F32)
        ab = fview(wb[0], BF16, 4096, [64, 32, 16], F32)
        beta = fview(wb[0], BF16, 6144, [64, 32, 8], F32)
        gg = fview(wb[0], BF16, 7168, [64, 32, 8], F32)
        gc = fview(wb[0], BF16, 8192, [64, 32, 8], F32)
        kd = fview(wb[0], BF16, 9216, [64, 32, 8], F32)
        egc = fview(wb[0], BF16, 10240, [64, 32, 8], F32)
        bgc = fview(wb[0], BF16, 11264, [64, 32, 8], F32)
        egl = fview(wb[0], BF16, 12288, [128, 32, 8], F32)
        kdec = fview(actB, BF16, 0, [64, 4, 128], F32)
        kbg = fview(actB, BF16, 2048, [64, 4, 128], F32)
        bv = fview(actB, BF16, 4096, [64, 4, 128], F32)
        uu = fview(actB, BF16, 6144, [64, 4, 128], F32)
        Lm = fview(actB, BF16, 8192, [64, 4, 64], F32)
        Um = fview(actB, BF16, 9216, [64, 4, 64], F32)
        Pm = fview(actB, BF16, 10240, [64, 4, 64], F32)
        qkT = fview(actB, BF16, 11264, [64, 4, 64], F32)
        tmp = fview(actB, BF16, 12288, [64, 4, 64], F32)
        E1 = fview(actB, BF16, 13312, [64, 4, 64], F32)
        E2 = fview(actB, BF16, 14336, [64, 4, 64], F32)
        gcd = fview(actB, BF16, 15360, [64, 4, 64], F32)
        LU = [fview(wb[1], BF16, i * 2048, [64, 2, 4, 64], F32) for i in range(2)]
        wT = fview(wb[1], BF16, 4096, [128, 4, 64], F32)
        zt = fview(wb[1], BF16, 6144, [64, 4, 128], F32)
        ssq = fview(wb[1], BF16, 8192, [64, 4], F32)
        osq = fview(wb[1], BF16, 10240, [64, 4, 128], F32)
        Sst = fview(gbc, F32, 0, [128, 4, 128], F32)
        vnew = fview(gbc, F32, 2048, [64, 4, 128], F32)
        oo = fview(gbc, F32, 4096, [64, 4, 128], F32)
        po2s = fview(gbc, F32, 6144, [64, 4, 128], F32)
        oT = actA
        oTv = actA[:].rearrange("p a b -> p (a b)").rearrange("p (h t) -> p h t", h=4)

        S.dma("sp", gconst, gconst_in)
        S.dma("sp", cw, convwT_in[:, l])
        S.dma("sp", gv, gvec_in[:, l])
        S.dma("sp", normw, gnorm_in[:, l])
        S.dma("sp", ab, abz[:, 0:16].rearrange("(n p) c -> p n c", p=64))
        S.act(beta, ab[:, :, 8:16], AF.Sigmoid)
        S.tt(gg, ab[:, :, 0:8], bc(gv[:, 1, :], [64, 32, 8], 1), ALU.add)
        S.act(gg, gg, AF.Exp)
        S.act(gg, gg, AF.Ln, bias=1.0)
        S.act(nexpA, gv[:, 0, :], AF.Exp)
        S.ts(nexpA, nexpA, -1.0, None, op0=ALU.mult)
        S.tt(gg, gg, bc(nexpA, [64, 32, 8], 1), ALU.mult)
        ggf = gg.rearrange("p n h -> p (n h)")
        gcf = gc.rearrange("p n h -> p (n h)")
        pt = nps4()
        S.mm(pt[0:64, 0:256], lhsT=TriLE, rhs=ggf)
        S.cp(gcf, pt[0:64, 0:256])
        pt = nps4()
        S.mm(pt[:, 0:256], lhsT=Sel63, rhs=gcf)
        S.act(egl.rearrange("p n h -> p (n h)"), pt[:, 0:256], AF.Exp)
        S.tt(kd.rearrange("p n h -> p (n h)"), pt[0:64, 0:256], gcf, ALU.subtract)
        S.act(kd, kd, AF.Exp)
        S.act(egc, gc, AF.Exp)
        S.tt(bgc, egc, beta, ALU.mult)

        if cfg.get("gstop") == 0:
            raise _Stop()
        qkv = [carve(i * 8 * KB, [128, T], F32) for i in range(12)]
        for ps_ in range(2):
            h0 = 4 * ps_
            for kind in range(3):
                for hp in range(4):
                    c = kind * 8 + h0 + hp
                    y = qkv[kind * 4 + hp]
                    eng = "dve"
                    S.dma("sp", xt[:], cin[c])
                    S.ts(y[:, 0:T], xt[:, 0:T], cw[:, c, 3:4], None, op0=ALU.mult, eng=eng)
                    for w_ in (2, 1, 0):
                        sh = 3 - w_
                        S.stt(y[:, sh:T], xt[:, 0:T - sh], cw[:, c, w_:w_ + 1], y[:, sh:T], ALU.mult, ALU.add, eng=eng)
                    S.act(y, y, AF.Silu)
                    if kind < 2:
                        S.act(xt2[:], y, AF.Square)
                        for j in range(4):
                            sl = slice(j * 512, (j + 1) * 512)
                            pt = nps4()
                            S.mm(pt[:], lhsT=ones, rhs=xt2[:, sl])
                            S.act(stg[j % 2][:], pt[:], AF.Ln, bias=epsb)
                            S.act(stg[j % 2][:], stg[j % 2][:], AF.Exp, scale=-0.5)
                            if kind == 0:
                                S.stt(y[:, sl], y[:, sl], float(128 ** -0.5), stg[j % 2][:], ALU.mult, ALU.mult)
                            else:
                                S.tt(y[:, sl], y[:, sl], stg[j % 2][:], ALU.mult)
            qs, ks_, vs_ = qkv[0:4], qkv[4:8], qkv[8:12]
            if cfg.get("gstop") == 1:
                raise _Stop()
            S.memset(Sst, 0.0)
            hs = slice(h0, h0 + 4)
            for ci in range(32):
                cs = slice(ci * 64, (ci + 1) * 64)
                b4 = beta[:, ci, hs]
                gc4 = gc[:, ci, hs]
                pk = nps4()
                for hp in range(4):
                    S.mm(pk[0:64, hp * 128:(hp + 1) * 128], lhsT=ks_[hp][:, cs], rhs=ident)
                pkv = pk[0:64, :].rearrange("p (h d) -> p h d", h=4)
                S.tt(kdec, pkv, bc(kd[:, ci, hs], [64, 4, 128], 2), ALU.mult)
                S.tt(kbg, pkv, bc(bgc[:, ci, hs], [64, 4, 128], 2), ALU.mult)
                pv = nps4()
                for hp in range(4):
                    S.mm(pv[0:64, hp * 128:(hp + 1) * 128], lhsT=vs_[hp][:, cs], rhs=ident)
                S.tt(bv, pv[0:64, :].rearrange("p (h d) -> p h d", h=4), bc(b4, [64, 4, 128], 2), ALU.mult)
                pab = nps4()
                for hp in range(4):
                    S.mm(pab[0:64, hp * 64:(hp + 1) * 64], lhsT=ks_[hp][:, cs], rhs=ks_[hp][:, cs])
                    S.mm(pab[0:64, 256 + hp * 64:256 + (hp + 1) * 64], lhsT=ks_[hp][:, cs], rhs=qs[hp][:, cs])
                S.tt(gcd, bc(ident64, [64, 4, 64], 1), bc(gc4, [64, 4, 64], 2), ALU.mult)
                pg = nps4()
                for hp in range(4):
                    S.mm(pg[0:64, hp * 64:(hp + 1) * 64], lhsT=ones64, rhs=gcd[:, hp, :])
                S.tt(tmp, pg[0:64, 0:256].rearrange("p (h j) -> p h j", h=4), bc(gc4, [64, 4, 64], 2), ALU.subtract)
                S.ts(E1, tmp, 0.0, None, op0=ALU.max)
                S.act(E1, E1, AF.Exp, scale=-1.0)
                S.ts(E2, tmp, 0.0, None, op0=ALU.min)
                S.act(E2, E2, AF.Exp)
                S.tt(Lm, pab[0:64, 0:256].rearrange("p (h j) -> p h j", h=4), E1, ALU.mult)
                S.tt(Lm, Lm, bc(maskSL, [64, 4, 64], 1), ALU.mult)
                S.tt(Lm, Lm, bc(b4, [64, 4, 64], 2), ALU.mult)
                S.tt(qkT, pab[0:64, 256:512].rearrange("p (h j) -> p h j", h=4), E2, ALU.mult)
                S.tt(qkT, qkT, bc(maskUI, [64, 4, 64], 1), ALU.mult)
                pu = nps4()
                for hp in range(4):
                    S.mm(pu[0:64, hp * 64:(hp + 1) * 64], lhsT=Lm[:, hp, :], rhs=ident64)
                puv = pu[0:64, 0:256].rearrange("p (h j) -> p h j", h=4)
                S.act(Um, puv, AF.Identity)
                S.tt(Pm, bc(ident64, [64, 4, 64], 1), puv, ALU.subtract)
                Lc, Uc = Lm, Um
                for lev in range(5):
                    last = lev == 4
                    pl = nps4()
                    for hp in range(4):
                        S.mm(pl[0:64, hp * 64:(hp + 1) * 64], lhsT=Uc[:, hp, :], rhs=Lc[:, hp, :])
                        if not last:
                            S.mm(pl[0:64, 256 + hp * 64:256 + (hp + 1) * 64], lhsT=Lc[:, hp, :], rhs=Uc[:, hp, :])
                    nxt = LU[lev % 2]
                    ncol = 256 if last else 512
                    S.act(nxt[:].rearrange("p a h j -> p (a h j)")[:, 0:ncol], pl[0:64, 0:ncol], AF.Identity)
                    Lc, Uc = nxt[:, 0], nxt[:, 1]
                    pp = nps4()
                    for hp in range(4):
                        S.mm(pp[0:64, hp * 64:(hp + 1) * 64], lhsT=Lc[:, hp, :], rhs=Pm[:, hp, :])
                    S.tt(Pm, Pm, pp[0:64, 0:256].rearrange("p (h j) -> p h j", h=4), ALU.add)
                pw = nps4()
                for hp in range(4):
                    S.mm(pw[:, hp * 64:(hp + 1) * 64], lhsT=kbg[:, hp, :], rhs=Pm[:, hp, :])
                S.act(wT[:].rearrange("p h c -> p (h c)"), pw[:, 0:256], AF.Identity)
                pu2 = nps4()
                for hp in range(4):
                    S.mm(pu2[0:64, hp * 128:(hp + 1) * 128], lhsT=Pm[:, hp, :], rhs=bv[:, hp, :])
                S.act(uu[:].rearrange("p h e -> p (h e)"), pu2[0:64, :], AF.Identity)
                if cfg.get("gstop") == 2:
                    raise _Stop()
                S.dma("sp", zt, abz[ci * 64:(ci + 1) * 64, 16 + h0 * 128:16 + (h0 + 4) * 128].rearrange("p (h e) -> p h e", h=4))
                pws = PS[4]
                for hp in range(4):
                    S.mm(pws[0:64, hp * 128:(hp + 1) * 128], lhsT=wT[:, hp, :], rhs=Sst[:, hp, :])
                S.tt(vnew, uu, pws[0:64, :].rearrange("p (h e) -> p h e", h=4), ALU.subtract)
                po1 = PS[5]
                for hp in range(4):
                    S.mm(po1[0:64, hp * 128:(hp + 1) * 128], lhsT=qs[hp][:, cs], rhs=Sst[:, hp, :])
                po2 = PS[6]
                for hp in range(4):
                    S.mm(po2[0:64, hp * 128:(hp + 1) * 128], lhsT=qkT[:, hp, :], rhs=vnew[:, hp, :])
                pS = PS[7]
                for hp in range(4):
                    S.mm(pS[:, hp * 128:(hp + 1) * 128], lhsT=kdec[:, hp, :], rhs=vnew[:, hp, :])
                S.act(po2s[:].rearrange("p h e -> p (h e)"), po2[0:64, :], AF.Identity)
                S.tt(oo, po1[0:64, :].rearrange("p (h e) -> p h e", h=4), bc(egc[:, ci, hs], [64, 4, 128], 2), ALU.mult)
                S.tt(oo, oo, po2s, ALU.add, eng="pool")
                S.tt(Sst, Sst, bc(egl[:, ci, hs], [128, 4, 128], 2), ALU.mult)
                S.tt(Sst, Sst, pS[:].rearrange("p (h e) -> p h e", h=4), ALU.add)
                S.tt(osq, oo, oo, ALU.mult, eng="pool")
                rsum(ssq, osq)
                S.act(ssq, ssq, AF.Ln, scale=1.0 / 128, bias=epsb[0:64])
                S.act(ssq, ssq, AF.Exp, scale=-0.5)
                S.act(zt, zt, AF.Silu)
                S.tt(oo, oo, bc(ssq, [64, 4, 128], 2), ALU.mult)
                S.tt(zt, zt, bc(normw, [64, 4, 128], 1), ALU.mult, eng="pool")
                S.tt(oo, oo, zt, ALU.mult)
                pot = nps4()
                for hp in range(4):
                    S.mm(pot[:, hp * 64:(hp + 1) * 64], lhsT=oo[:, hp, :], rhs=ident64)
                S.act(oTv[:, :, cs], pot[:, 0:256].rearrange("p (h c) -> p h c", h=4), AF.Identity)
                if cfg.get("gstop") == 3:
                    raise _Stop()
            for hp in range(4):
                S.dma("sp", mixT[:, 8 + h0 + hp, :], oTv[:, hp, :])
                S.dma("sp", p_gdn[l, h0 + hp], Sst[:, hp, :])

    def idma(out, in_, idx):
        pool = S.pool["pool"]
        slot = pool[S.pidx["pool"] % len(pool)]
        S.pidx["pool"] += 1
        deps = []
        if slot[1] > 0:
            deps.append((slot[0], slot[1], "dma:pool"))
        slot[1] += 16
        tag = (slot[0], slot[1], "dma:pool")
        deps += S._deps_and_record([idx], [out], tag)
        deps = [d for d in deps if d is not tag]
        waits = S._waits("pool", deps)
        S.prog["pool"].append((waits, (lambda e: e.indirect_dma_start(
            out=out, out_offset=None, in_=in_, in_offset=bass.IndirectOffsetOnAxis(ap=idx, axis=0))), slot[0], 16))
        S.n_inst += 1

    one11 = ones[0:1, 0:1]

    def row_to_fm(row_ap, n, dst):
        pt = nps4()
        for c in range(n):
            S.mm(pt[:, c:c + 1], lhsT=row_ap[0:1, c * 128:(c + 1) * 128], rhs=one11)
        S.cp(dst, pt[:, 0:n])

    def fm_stats(xfm):
        sq = sm2[:, 0:16]
        ss = sm2[:, 16:17]
        tot = sm2[:, 17:18]
        S.memset(ss, 0.0)
        S.act(sq, xfm, AF.Square, accum=ss)
        pt = nps4()
        S.mm(pt[:, 0:1], lhsT=ones, rhs=ss)
        S.act(tot, pt[:, 0:1], AF.Ln, scale=1.0 / D, bias=epsb)
        S.act(tot, tot, AF.Exp, scale=-0.5)
        return tot

    def fm_prenorm(ka, kb, out_bf):
        tot = fm_stats(xsam[:])
        tmp = sm2[:, 18:34]
        S.ts(tmp, xsam[:], tot, None, op0=ALU.mult)
        S.tt(tmp, tmp, eff[:, ka, :, 1], ALU.mult)
        S.tt(out_bf, tmp, eff[:, kb, :, 1], ALU.add)

    def fm_post(frow, kG):
        f_fm = sm2[:, 40:56]
        row_to_fm(frow, 16, f_fm)
        tot = fm_stats(f_fm)
        tmp = sm2[:, 18:34]
        S.ts(tmp, f_fm, tot, None, op0=ALU.mult)
        S.tt(tmp, tmp, eff[:, kG, :, 1], ALU.mult)
        S.tt(xsam[:], xsam[:], tmp, ALU.add)

    wsi = [0]

    def tm_proj(in_f, nk, wview, N, out_row):
        for c0 in range(0, N, 256):
            ncol = min(256, N - c0)
            pt = nps4()
            for k0 in range(0, nk, 16):
                k1 = min(nk, k0 + 16)
                buf = wb[wsi[0] % 2]
                wsi[0] += 1
                wv = buf[:, 0:2 * (k1 - k0) * ncol].bitcast(F32).rearrange("p (c n) -> p c n", c=k1 - k0)
                wload(wv, wview[:, k0:k1, c0:c0 + ncol], q="sp")
                for kc in range(k0, k1):
                    S.mm(pt[0:1, 0:ncol], lhsT=in_f[:, kc:kc + 1], rhs=wv[:, kc - k0, :],
                         start=(kc == 0), stop=(kc == nk - 1))
            S.act(out_row[0:1, c0:c0 + ncol], pt[0:1, 0:ncol], AF.Identity)

    def gelu_ps(dst, src_ps, t1_, t2_):
        S.act(dst, src_ps, AF.Identity)
        S.tt(t1_, dst, dst, ALU.mult)
        S.ts(t1_, t1_, 0.044715, 1.0, op0=ALU.mult, op1=ALU.add)
        S.tt(t1_, t1_, dst, ALU.mult)
        S.act(t2_, t1_, AF.Sigmoid, scale=1.5957691216057308)
        S.tt(dst, dst, t2_, ALU.mult)

    def sample_layer(l):
        projrow = carve(0, [128, 6720], F32)[0:1]
        xv = fview(xt, F32, 0, [128, 2048], F32)
        hid = [xv[:, 0:256], xv[:, 256:512], xv[0:1, 512:768]]
        gt1, gt2 = xv[:, 768:1024], xv[:, 1024:1280]
        hidT = xv[:, 1280:1280 + 514].rearrange("p (c j) -> p c j", c=2)
        x2v = fview(xt2, F32, 0, [128, 2048], F32)
        kcbT = x2v[:, 0:257]
        vcb = [x2v[:, 384:512], x2v[:, 512:640], x2v[0:1, 640:768]]
        pe_t = x2v[:, 768:896].rearrange("p (i r) -> p i r", i=2)
        w2_t = x2v[:, 896:1408].rearrange("p (i c d) -> p i c d", i=2, c=2)
        Xnew = x2v[:, 1408:1472].bitcast(BF16)
        Xnew = Xnew.rearrange("p (i r) -> p i r", i=2)
        qcol = x2v[:, 1472:1480]
        kvcol = x2v[:, 1480:1492]
        hsb = x2v[:, 1610:1626]
        mixcol = x2v[:, 1508:1524]
        mixb = mixcol
        h2b = x2v[:, 1626:1642]
        XnewF = x2v[:, 1642:1706]
        actcol = x2v[:, 1540:1584]
        actb = actcol
        gv_ = fview(gbc, F32, 0, [128, 2048], F32)
        scorebc = gv_[:, 0:256]
        cmpm = gv_[:, 256:512]
        selexp = gv_[:, 512:640]
        selcol = gv_[:, 640:642]
        rank2 = gv_[:, 642:644]
        scol = gv_[:, 644:646]
        e4 = gv_[0:4, 648:648 + 256]
        p4 = gv_[0:4, 904:904 + 256]
        pT = gv_[:, 1160:1168].rearrange("p (c h) -> p c h", c=2)
        dsel = gv_[:, 1168:1680].rearrange("p (jh a g) -> p jh a g", jh=2, a=2)
        hsel = gv_[:, 1680:1936].rearrange("p (a q) -> p a q", a=2)
        rowsA = fview(actA, BF16, 0, [128, 4096], F32)
        rowsB = fview(actB, BF16, 0, [128, 4096], F32)
        krow, vrow, kSr, qSr = (rowsA[0:1, i * 1024:(i + 1) * 1024] for i in range(4))
        vnew, orow, zrow, trow = (rowsB[0:1, i * 1024:(i + 1) * 1024] for i in range(4))
        scal = carve(92 * KB, [128, 2048], F32)
        pgt = [stg[0][:, 0:256], stg[1][:, 0:256]]
        pgx = [stg[0][:, 128:257], stg[1][:, 128:257]]
        kTt = [stg[0][:, 260:388], stg[1][:, 260:388]]
        for i_ in range(2):
            S.memset(stg[i_][:, 256:257], 1.0)

        fm_prenorm(0, 1, hsb)
        tm_proj(hsb, NCH, w_in[l].rearrange("(c p) n -> p c n", p=128), INW, projrow)
        rs = scal[0:1, 0:32].rearrange("p (a f) -> p a f", a=2)
        S.dma("sp", rs, ropeS_in)
        t16 = scal[0:1, 32:32 + 14 * 32]
        for (base, nseg, stride) in ((O_Q, 8, 128), (O_KC, 2, 128), (O_KS, 2, 128), (O_KW, 2, 128)):
            v = projrow[0:1, base:base + nseg * stride].rearrange("p (s d) -> p s d", s=nseg)
            x1, x2 = v[:, :, 0:16], v[:, :, 16:32]
            a_ = t16[0:1, 0:nseg * 16].rearrange("p (s f) -> p s f", s=nseg)
            b_ = t16[0:1, 128:128 + nseg * 16].rearrange("p (s f) -> p s f", s=nseg)
            c_ = t16[0:1, 256:256 + nseg * 16].rearrange("p (s f) -> p s f", s=nseg)
            cosb = bc(rs[:, 0, :], [1, nseg, 16], 1)
            sinb = bc(rs[:, 1, :], [1, nseg, 16], 1)
            S.tt(a_, x1, cosb, ALU.mult)
            S.tt(b_, x2, sinb, ALU.mult)
            S.tt(c_, x1, sinb, ALU.mult)
            S.tt(x1, a_, b_, ALU.subtract)
            S.tt(a_, x2, cosb, ALU.mult)
            S.tt(x2, a_, c_, ALU.add)
        for g in range(2):
            for (dst, ko, vo) in ((s_cmp, O_KC, O_VC), (s_sel, O_KS, O_VS)):
                S.dma("sp", dst[l, g, 0, 0:1, :], projrow[0:1, ko + g * 128:ko + (g + 1) * 128])
                S.dma("sp", dst[l, g, 0, 1:2, :], projrow[0:1, vo + g * 128:vo + (g + 1) * 128])
            S.dma("sp", s_win[l, g, 0:511, :], cache_win[l, g, 1:512, :])
            S.dma("sp", s_win[l, g, 511:512, 0:128], projrow[0:1, O_KW + g * 128:O_KW + (g + 1) * 128])
            S.dma("sp", s_win[l, g, 511:512, 128:256], projrow[0:1, O_VW + g * 128:O_VW + (g + 1) * 128])
        row_to_fm(projrow[0:1, O_Q:O_Q + 1024], 8, qcol)
        row_to_fm(projrow[0:1, O_KC:O_KC + 1536], 12, kvcol)
        S.act(scal[0:1, 512:536], projrow[0:1, O_GL:O_GL + 24], AF.Sigmoid)
        S.dma("sp", gscr.rearrange("(o n) -> o n", o=1), scal[0:1, 512:536])
        gat = scal[0:4, 544:550].rearrange("p (g i) -> p g i", g=2)
        for g in range(2):
            S.dma("sp", gat[:, g, :], gscr[g * 12:(g + 1) * 12].rearrange("(h i) -> h i", i=3))
        S.dma("sp", pti[:, 0:128], ptab_in.partition_broadcast(128))
        S.dma("sp", pti[:, 128:129], iota_in)
        S.op("pool", lambda e: e.tensor_scalar(out=pti[:, 0:128], in0=pti[:, 0:128], scalar1=512, scalar2=l * 256,
                                               op0=ALU.mult, op1=ALU.add), reads=[pti[:, 0:128]], writes=[pti[:, 0:128]])
        S.op("pool", lambda e: e.tensor_tensor(out=pti[:, 0:128], in0=pti[:, 0:128],
                                               in1=pti[:, 128:129].to_broadcast([128, 128]), op=ALU.add),
             reads=[pti[:, 0:129]], writes=[pti[:, 0:128]])
        S.dma("sp", pe_t, cmp_peT_in[l])
        S.dma("sp", w2_t, cmp_w2[l].rearrange("i (c p) d -> p i c d", p=128))
        S.dma("sp", dsel, dsel_in)
        S.dma("sp", hsel, hsel_in)
        osum = scal[0:4, 560:560 + 128]
        for g in range(2):
            idxg = pti[:, 130:131]
            for i in range(2):
                XTi = carve(28 * KB, [128, 16384], F32 if i == 0 else BF16)
                for pg in range(128):
                    tile_ = pgt[pg % 2]
                    S.op("pool", lambda e, pg=pg, g=g: e.tensor_scalar(out=idxg, in0=pti[:, pg:pg + 1], scalar1=g * 128, scalar2=None,
                                                                   op0=ALU.add), reads=[pti[:, pg:pg + 1]], writes=[idxg])
                    idma(tile_, cache_cmp, idxg)
                    pt = nps4()
                    S.mm(pt[:, 0:128], lhsT=tile_[:, i * 128:(i + 1) * 128], rhs=ident)
                    if pg % 2 == 0:
                        S.act(XTi[:, pg * 128:(pg + 1) * 128], pt[:, 0:128], AF.Identity)
                    else:
                        S.cp(XTi[:, pg * 128:(pg + 1) * 128], pt[:, 0:128])
                xv3 = XTi.rearrange("p (j r) -> p j r", r=64)
                for q4 in range(8):
                    S.tt(xv3[:, q4 * 32:(q4 + 1) * 32, :], xv3[:, q4 * 32:(q4 + 1) * 32, :],
                         bc(pe_t[:, i, :], [128, 32, 64], 1), ALU.add, eng=("dve" if q4 % 2 == 0 else "pool"))
                Xn = XnewF if i == 0 else Xnew[:, 1, :]
                S.cp(Xn, pe_t[:, i, :])
                S.tt(Xn[:, 0:1], pe_t[:, i, 0:1], kvcol[:, i * 2 + g:i * 2 + g + 1], ALU.add)
                accs = [PS[4], PS[5], PS[6]]
                w1v = cmp_w1[l, i].rearrange("(r d) n -> d r n", d=128)
                nq, nr = (8, 8) if i == 0 else (4, 16)
                for rq in range(nq):
                    buf = wb[rq % 2]
                    if i == 0:
                        wv = buf[:, 0:4096].bitcast(F32).rearrange("p (c n) -> p c n", c=8)
                        wload(wv, w1v[:, rq * 8:(rq + 1) * 8, :], q="sp")
                    else:
                        wv = buf[:, 0:4096].rearrange("p (c n) -> p c n", c=16)
                        wload(wv, w1v[:, rq * 16:(rq + 1) * 16, :])
                    for rr in range(nr):
                        r = rq * nr + rr
                        for mg in range(2):
                            S.mm(accs[mg][:, 0:256], lhsT=xv3[:, mg * 128:(mg + 1) * 128, r], rhs=wv[:, rr, :],
                                 start=(r == 0), stop=(r == 63))
                        S.mm(accs[2][0:1, 0:256], lhsT=Xn[:, r:r + 1], rhs=wv[:, rr, :], start=(r == 0), stop=(r == 63))
                for mg in range(3):
                    np_ = 1 if mg == 2 else 128
                    gelu_ps(hid[mg], accs[mg][0:np_, 0:256], gt1[0:np_], gt2[0:np_])
                    for c in range(2):
                        pt = nps4()
                        S.mm(pt[:, 0:np_], lhsT=hid[mg][:, c * 128:(c + 1) * 128], rhs=ident[0:np_, 0:np_])
                        S.cp(hidT[:, c, mg * 128:mg * 128 + np_], pt[:, 0:np_])
                if i == 0:
                    pt2 = nps4()
                    for c in range(2):
                        S.mm(pt2[:, 0:257], lhsT=w2_t[:, 0, c, :], rhs=hidT[:, c, :], start=(c == 0), stop=(c == 1))
                    S.cp(kcbT, pt2[:, 0:257])
                else:
                    for mg in range(3):
                        np_ = 1 if mg == 2 else 128
                        pt2 = nps4()
                        for c in range(2):
                            S.mm(pt2[0:np_, 0:128], lhsT=hidT[:, c, mg * 128:mg * 128 + np_], rhs=w2_t[:, 1, c, :],
                                 start=(c == 0), stop=(c == 1))
                        S.cp(vcb[mg], pt2[0:np_, 0:128])
            qg = qcol[:, 4 * g:4 * g + 4]
            ps_ = nps4()
            S.mm(ps_[0:4, 0:256], lhsT=qg, rhs=kcbT[:, 0:256])
            den4 = scal[0:4, 700:701]
            S.memset(den4, 0.0)
            S.act(e4, ps_[0:4, 0:256], AF.Exp, scale=SCALE, accum=den4)
            S.act(den4, den4, AF.Ln)
            S.act(den4, den4, AF.Exp, scale=-1.0)
            S.ts(p4, e4, den4, None, op0=ALU.mult)
            for c in range(2):
                pt = nps4()
                S.mm(pt[:, 0:4], lhsT=p4[:, c * 128:(c + 1) * 128], rhs=ident[0:4, 0:4])
                S.cp(pT[:, c, :], pt[:, 0:4])
            po = nps4()
            for c in range(2):
                S.mm(po[0:4, 0:128], lhsT=pT[:, c, :], rhs=vcb[c], start=(c == 0), stop=(c == 1))
            S.ts(osum, po[0:4, 0:128], gat[:, g, 0:1], None, op0=ALU.mult)
            pi = nps4()
            S.mm(pi[0:1, 0:256], lhsT=ones[0:4, 0:1], rhs=p4)
            imr = scal[0:1, 704:704 + 256]
            S.cp(imr, pi[0:1, 0:256])
            S.memset(imr[:, 0:1], 12.0)
            S.memset(imr[:, 255:256], 10.0)
            pb = nps4()
            S.mm(pb[:, 0:256], lhsT=ones[0:1, :], rhs=imr)
            S.cp(scorebc, pb[:, 0:256])
            row_to_fm(imr, 2, scol)
            for c in range(2):
                S.ts(cmpm, scorebc, scol[:, c:c + 1], None, op0=ALU.is_gt)
                rsum(rank2[:, c:c + 1], cmpm)
            S.ts(selcol, rank2, 14.5, None, op0=ALU.is_lt)
            pe_ = nps4()
            k_ = 0
            for jh in range(2):
                for a in range(2):
                    tmpd_ = cmpm[:, 0:128]
                    S.ts(tmpd_, dsel[:, jh, a, :], selcol[:, jh:jh + 1], None, op0=ALU.mult)
                    S.mm(pe_[:, 0:128], lhsT=hsel[:, a, :], rhs=tmpd_, start=(k_ == 0), stop=(k_ == 3))
                    k_ += 1
            S.cp(selexp, pe_[:, 0:128])
            if dbg_s is not None and l == 0:
                S.dma("sp", dbg_s[:, 16 + 2 * g:18 + 2 * g], selcol)
                S.dma("sp", dbg_s[0:4, 32 + 8 * g:32 + 8 * g + 3], gat[:, g, :])
            for br in range(2):
                acc = PS[4 + br]
                ntile = 128 if br == 0 else 4
                for tix in range(ntile):
                    tile_ = pgt[tix % 2]
                    if br == 0:
                        S.op("pool", lambda e, pg=tix, g=g: e.tensor_scalar(out=idxg, in0=pti[:, pg:pg + 1], scalar1=g * 128,
                                                                        scalar2=None, op0=ALU.add),
                             reads=[pti[:, tix:tix + 1]], writes=[idxg])
                        idma(tile_, cache_sel, idxg)
                    else:
                        S.dma("sp", tile_, cache_win[l, g, tix * 128:(tix + 1) * 128, :])
                    pt = nps4()
                    S.mm(pt[:, 0:128], lhsT=tile_[:, 0:128], rhs=ident)
                    kt_ = kTt[tix % 2]
                    S.act(kt_, pt[:, 0:128], AF.Identity)
                    pss = nps4()
                    S.mm(pss[:, 0:4], lhsT=kt_, rhs=qg)
                    eT = scal[:, 960 + (tix % 2) * 4:964 + (tix % 2) * 4]
                    S.act(eT, pss[:, 0:4], AF.Exp, scale=SCALE)
                    if br == 0:
                        S.ts(eT, eT, selexp[:, tix:tix + 1], None, op0=ALU.mult)
                    S.mm(acc[0:4, 0:129], lhsT=eT, rhs=pgx[tix % 2], start=(tix == 0), stop=False)
                kcol_new = kvcol[:, 4 + br * 4 + g:5 + br * 4 + g]
                vo_ = (O_VS if br == 0 else O_VW) + g * 128
                pss = nps4()
                S.mm(pss[0:1, 0:4], lhsT=kcol_new, rhs=qg)
                en = scal[0:1, 970:974]
                S.act(en, pss[0:1, 0:4], AF.Exp, scale=SCALE)
                vn1 = scal[0:1, 1200:1329]
                S.cp(vn1[:, 0:128], projrow[0:1, vo_:vo_ + 128])
                S.memset(vn1[:, 128:129], 1.0)
                S.mm(acc[0:4, 0:129], lhsT=en, rhs=vn1, start=False, stop=True)
                rd = scal[0:4, 976:977]
                S.act(rd, acc[0:4, 128:129], AF.Ln)
                S.act(rd, rd, AF.Exp, scale=-1.0)
                S.tt(rd, rd, gat[:, g, 1 + br:2 + br], ALU.mult)
                ob = scal[0:4, 980:980 + 128]
                S.ts(ob, acc[0:4, 0:128], rd, None, op0=ALU.mult)
                S.tt(osum, osum, ob, ALU.add)
            pt = nps4()
            S.mm(pt[:, 0:4], lhsT=osum, rhs=ident[0:4, 0:4])
            S.cp(mixcol[:, 4 * g:4 * g + 4], pt[:, 0:4])

        cw = fview(wb[0], BF16, 2560, [128, 24, 4], F32)
        S.dma("sp", cw, convwT_in[:, l])
        cst3 = scal[:, 256:328].rearrange("p (c w) -> p c w", w=3)
        S.dma("sp", cst3, sconvT_in[l])
        cfm = scal[:, 328:352]
        row_to_fm(projrow[0:1, O_CONV:O_CONV + 3072], 24, cfm)
        yfm = scal[:, 352:376]
        tfm = scal[:, 376:400]
        S.tt(yfm, cfm, cw[:, :, 3], ALU.mult)
        for w_ in range(3):
            S.tt(tfm, cst3[:, :, w_], cw[:, :, w_], ALU.mult)
            S.tt(yfm, yfm, tfm, ALU.add)
        S.act(yfm, yfm, AF.Silu)
        S.dma("sp", s_conv[l, 0:2, :], sconv_in[l, 1:3, :])
        S.dma("sp", s_conv[l, 2:3, :], projrow[0:1, O_CONV:O_CONV + 3072])
        sqf = scal[:, 400:416]
        S.tt(sqf, yfm[:, 0:16], yfm[:, 0:16], ALU.mult)
        pn = nps4()
        S.mm(pn[:, 0:16], lhsT=ones, rhs=sqf)
        rn = scal[:, 416:432]
        S.act(rn, pn[:, 0:16], AF.Ln, bias=epsb)
        S.act(rn, rn, AF.Exp, scale=-0.5)
        S.tt(yfm[:, 0:16], yfm[:, 0:16], rn, ALU.mult)
        S.ts(yfm[:, 0:8], yfm[:, 0:8], float(128 ** -0.5), None, op0=ALU.mult)
        qf, kf, vf = yfm[:, 0:8], yfm[:, 8:16], yfm[:, 16:24]
        Sg = carve(100 * KB, [128, 8, 128], F32)
        S.dma("sp", Sg, sgdn_in[l].rearrange("h k v -> k h v"))
        for (rowdst, colsrc) in ((krow, kf), (vrow, vf)):
            for half in range(2):
                pt = nps4()
                for hh in range(4):
                    h = half * 4 + hh
                    S.mm(pt[0:1, hh * 128:(hh + 1) * 128], lhsT=colsrc[:, h:h + 1], rhs=ident)
                S.cp(rowdst[0:1, half * 512:(half + 1) * 512], pt[0:1, :])
        for (rowdst, colsrc) in ((kSr, kf), (qSr, qf)):
            for half in range(2):
                pt = nps4()
                for hh in range(4):
                    h = half * 4 + hh
                    S.mm(pt[0:1, hh * 128:(hh + 1) * 128], lhsT=colsrc[:, h:h + 1], rhs=Sg[:, h, :])
                S.cp(rowdst[0:1, half * 512:(half + 1) * 512], pt[0:1, :])
        prod = scal[:, 432:440]
        S.tt(prod, qf, kf, ALU.mult)
        pq = nps4()
        S.mm(pq[0:1, 0:8], lhsT=ones[:, 0:1], rhs=prod)
        r8 = scal[0:1, 440:520].rearrange("p (k h) -> p k h", h=8)
        S.cp(r8[:, 0, :], pq[0:1, 0:8])
        gvr = scal[0:1, 520:536].rearrange("p (a h) -> p a h", a=2)
        S.dma("sp", gvr, gvec_in[0:1, l])
        S.act(r8[:, 1, :], projrow[0:1, O_B:O_B + 8], AF.Sigmoid)
        S.tt(r8[:, 2, :], projrow[0:1, O_A:O_A + 8], gvr[:, 1, :], ALU.add)
        S.act(r8[:, 2, :], r8[:, 2, :], AF.Exp)
        S.act(r8[:, 2, :], r8[:, 2, :], AF.Ln, bias=1.0)
        S.act(r8[:, 3, :], gvr[:, 0, :], AF.Exp)
        S.tt(r8[:, 2, :], r8[:, 2, :], r8[:, 3, :], ALU.mult)
        S.act(r8[:, 2, :], r8[:, 2, :], AF.Exp, scale=-1.0)
        qk8, be8, eg8 = r8[:, 0, :], r8[:, 1, :], r8[:, 2, :]
        v3 = lambda ap: ap.rearrange("p (h e) -> p h e", h=8)
        S.tt(v3(trow), v3(kSr), bc(eg8, [1, 8, 128], 2), ALU.mult)
        S.tt(v3(trow), v3(vrow), v3(trow), ALU.subtract)
        S.tt(v3(vnew), v3(trow), bc(be8, [1, 8, 128], 2), ALU.mult)
        S.tt(v3(orow), v3(qSr), bc(eg8, [1, 8, 128], 2), ALU.mult)
        S.tt(v3(trow), v3(vnew), bc(qk8, [1, 8, 128], 2), ALU.mult)
        S.tt(orow, orow, trow, ALU.add)
        pegb = nps4()
        S.mm(pegb[:, 0:8], lhsT=ones[0:1, :], rhs=eg8)
        egb = scal[:, 536:544]
        S.cp(egb, pegb[:, 0:8])
        for half in range(2):
            pt = PS[6 + half]
            for hh in range(4):
                h = half * 4 + hh
                S.mm(pt[:, hh * 128:(hh + 1) * 128], lhsT=krow[0:1, h * 128:(h + 1) * 128], rhs=vnew[0:1, h * 128:(h + 1) * 128])
            sl = Sg[:, half * 4:(half + 1) * 4, :]
            S.tt(sl, sl, bc(egb[:, half * 4:(half + 1) * 4], [128, 4, 128], 2), ALU.mult)
            S.tt(sl, sl, pt[:].rearrange("p (h e) -> p h e", h=4), ALU.add)
        S.dma("sp", s_gdn[l].rearrange("h k v -> k h v"), Sg)
        S.tt(trow, orow, orow, ALU.mult)
        rsum(r8[:, 4, :], v3(trow))
        S.act(r8[:, 4, :], r8[:, 4, :], AF.Ln, scale=1.0 / 128, bias=epsb[0:1])
        S.act(r8[:, 4, :], r8[:, 4, :], AF.Exp, scale=-0.5)
        S.tt(v3(orow), v3(orow), bc(r8[:, 4, :], [1, 8, 128], 2), ALU.mult)
        nw = scal[0:1, 544:672]
        S.dma("sp", nw, gnorm_in[0:1, l])
        S.tt(v3(orow), v3(orow), bc(nw, [1, 8, 128], 1), ALU.mult)
        S.act(zrow, projrow[0:1, O_Z:O_Z + 1024], AF.Silu)
        S.tt(orow, orow, zrow, ALU.mult)
        row_to_fm(orow, 8, mixcol[:, 8:16])
        if dbg_s is not None and l == 0:
            S.dma("sp", dbg_s[:, 0:16], mixcol)
        frow = rowsB[0:1, 0:2048]
        tm_proj(mixb, NCH, w_out[l].rearrange("(c p) n -> p c n", p=128), D, frow)
        fm_post(frow, 4)
        fm_prenorm(2, 3, h2b)
        grow = carve(0, [128, 6720], F32)[0:1, 0:DFF]
        urow = carve(28 * KB, [128, 6720], F32)[0:1, 0:DFF]
        tm_proj(h2b, NCH, w_gate[l].rearrange("(c p) n -> p c n", p=128), DFF, grow)
        tm_proj(h2b, NCH, w_up[l].rearrange("(c p) n -> p c n", p=128), DFF, urow)
        S.act(grow, grow, AF.Silu)
        S.tt(grow, grow, urow, ALU.mult)
        row_to_fm(grow, 44, actcol)
        tm_proj(actb, 44, w_down[l].rearrange("(c p) n -> p c n", p=128), D, frow)
        fm_post(frow, 5)

    nlayers = cfg.get("layers", DEPTH)
    S.dma("sp", xsam[:], xsT_in)
    for l in range(nlayers):
        layer_eff(l)
        S.dma("sp", ropeT, ropeFM_in)
        xsrc = x_in if l == 0 else xbuf2
        chk("eff")
        win_l = w_in[l].rearrange("(c p) n -> p c n", p=128)
        for tg in range(NG):
            hT = actA
            for ti in range(4):
                tt_ = tg * 4 + ti
                S.dma("sp", xt[:], xsrc[tt_ * 128:(tt_ + 1) * 128, :])
                norm_to_fm(xt, hT, ti * 128, 0, 1, 0)
            tsl = slice(tg * TG, (tg + 1) * TG)
            chk("norm")
            fm_cols = [(O_Q + h * 128, ("q", h)) for h in range(8)]
            fm_cols += [(O_KC + g * 128, ("k", 0 + g)) for g in range(2)]
            fm_cols += [(O_KS + g * 128, ("k", 2 + g)) for g in range(2)]
            fm_cols += [(O_KW + g * 128, ("k", 4 + g)) for g in range(2)]
            fm_cols += [(O_VC + g * 128, ("vc", g)) for g in range(2)]
            fm_cols += [(O_CONV + c * 128, ("cin", c)) for c in range(24)]
            fm_cols += [(O_GL, ("gate", 0))]
            fm_cols.sort(key=lambda t: t[0])
            fm_plan = []
            i_ = 0
            while i_ < len(fm_cols):
                j_ = i_
                while (j_ + 1 < len(fm_cols) and j_ + 1 - i_ < 4 and fm_cols[j_ + 1][1][0] != "gate"
                       and fm_cols[j_][1][0] != "gate" and fm_cols[j_ + 1][0] == fm_cols[j_][0] + 128):
                    j_ += 1
                gcols = 24 if fm_cols[i_][1][0] == "gate" else 128 * (j_ - i_ + 1)
                for k_ in range(i_, j_ + 1):
                    fm_plan.append((fm_cols[k_][0], fm_cols[k_][1], k_ == i_, gcols, 128 * (k_ - i_)))
                i_ = j_ + 1
            for col0, (kind, idx), lead, gcols, off in fm_plan:
                ncol = 24 if kind == "gate" else 128
                if lead:
                    buf = wb[wi % 2]
                    wi += 1
                    wv_full = buf[:, 0:NCH * gcols].rearrange("p (c n) -> p c n", c=NCH)
                    wload(wv_full, win_l[:, :, col0:col0 + gcols])
                wv = wv_full[:, :, off:off + ncol]
                pt = nps()
                for kc in range(NCH):
                    S.mm(pt[0:ncol, :], lhsT=wv[:, kc, :], rhs=hT[:, kc, :], start=(kc == 0), stop=(kc == NCH - 1))
                if kind in ("q", "k"):
                    xs = stg[0]
                    S.act(xs[:], pt[:], AF.Identity)
                    pr = nps()
                    S.mm(pr[:], lhsT=rotT, rhs=xs[:])
                    t1 = stg[1]
                    S.tt(t1[:], xs[:], ropeT[:, 0, tsl], ALU.mult)
                    S.tt(xs[:], pr[:], ropeT[:, 1, tsl], ALU.mult)
                    S.tt(t1[:], t1[:], xs[:], ALU.add, eng="pool")
                    if kind == "q":
                        S.act(qT[:, idx, tsl], t1[:], AF.Identity)
                    else:
                        S.act(kT[:, idx, tsl], t1[:], AF.Identity)
                        pt2 = nps()
                        for ti in range(4):
                            S.tr(pt2[:, ti * 128:(ti + 1) * 128], t1[:, ti * 128:(ti + 1) * 128], ident)
                        S.cp(xs[:], pt2[:])
                        stream, g = idx // 2, idx % 2
                        dst = (p_cmp, p_sel, p_win)[stream]
                        if stream < 2:
                            S.dma("sp", dst[l, g, tsl, 0, :].rearrange("(a t) d -> t a d", a=4),
                                  xs[:].rearrange("p (a d) -> p a d", a=4))
                        elif tg == NG - 1:
                            S.dma("sp", dst[l, g, :, 0, :].rearrange("(a t) d -> t a d", a=4),
                                  xs[:].rearrange("p (a d) -> p a d", a=4))
                elif kind == "vc":
                    S.act(vcT[:, idx, tsl], pt[:], AF.Identity)
                elif kind == "gate":
                    S.act(gatT[0:24, tsl], pt[0:24, :], AF.Sigmoid)
                else:
                    xs = stg[idx % 2]
                    S.act(xs[:], pt[:], AF.Identity)
                    S.dma("sp", cin[idx, :, tsl], xs[:])
            chk("fm")
            tm_blocks = [(O_VC, 256, "v", 0), (O_VS, 256, "v", 1), (O_VW, 256, "v", 2),
                         (O_A, 16, "ab", 0), (O_Z, 512, "z", 0), (O_Z + 512, 512, "z", 1)]
            for col0, ncol, kind, idx in tm_blocks:
                buf = wb[wi % 2]
                wi += 1
                wv = buf[:, 0:NCH * ncol].rearrange("p (c n) -> p c n", c=NCH)
                wload(wv, win_l[:, :, col0:col0 + ncol])
                for ti in range(4):
                    tt_ = tg * 4 + ti
                    rows = slice(tt_ * 128, (tt_ + 1) * 128)
                    pt = nps()
                    for kc in range(NCH):
                        S.mm(pt[:, 0:ncol], lhsT=hT[:, kc, ti * 128:(ti + 1) * 128], rhs=wv[:, kc, :],
                             start=(kc == 0), stop=(kc == NCH - 1))
                    xs = stg[ti % 2]
                    S.act(xs[:, 0:ncol], pt[:, 0:ncol], AF.Identity)
                    if kind == "v":
                        dst = (p_cmp, p_sel, p_win)[idx]
                        if idx < 2:
                            S.dma("sp", dst[l, :, rows, 1, :].rearrange("g t d -> t g d"),
                                  xs[:, 0:256].rearrange("p (g d) -> p g d", g=2))
                        elif tt_ >= NT - 4:
                            r2 = slice((tt_ - (NT - 4)) * 128, (tt_ - (NT - 4) + 1) * 128)
                            S.dma("sp", dst[l, :, r2, 1, :].rearrange("g t d -> t g d"),
                                  xs[:, 0:256].rearrange("p (g d) -> p g d", g=2))
                        if idx >= 1:
                            S.cp(vtm[:, tt_, (idx - 1) * 2:(idx - 1) * 2 + 2, :],
                                 xs[:, 0:256].rearrange("p (g d) -> p g d", g=2), eng="pool")
                    elif kind == "ab":
                        S.dma("sp", abz[rows, 0:16], xs[:, 0:16])
                    else:
                        S.dma("sp", abz[rows, 16 + idx * 512:16 + (idx + 1) * 512], xs[:, 0:512])
            chk("tm")
        for c in range(24):
            S.dma("sp", p_conv[l, :, c * 128:(c + 1) * 128].rearrange("w p -> p w"), cin[c, :, T - 3:T],
                  allow_slow_non_contiguous=True)

        chk("A")
        z = stg[0][:].bitcast(BF16)
        S.memset(z, 0.0)
        zc = range(16)
        if cfg.get("nsa", True):
            nsa_phase(l)
            zc = range(8, 16)
        if cfg.get("gdn", True):
            zc = range(0, 8) if not cfg.get("nsa", True) else ()
        for c in zc:
            for j in range(2):
                S.dma("sp", mixT[:, c, j * 1024:(j + 1) * 1024], z)
        if cfg.get("gdn", True):
            gdn_phase(l)
        chk("B")

        wout_l = w_out[l].rearrange("(c p) n -> p c n", p=128)
        wg_l = w_gate[l].rearrange("(c p) n -> p c n", p=128)
        wu_l = w_up[l].rearrange("(c p) n -> p c n", p=128)
        wd_l = w_down[l].rearrange("(c p) n -> p c n", p=128)
        ydst = y_p if l == nlayers - 1 else xbuf2
        for tg in range(NG):
            tsl = slice(tg * TG, (tg + 1) * TG)
            mg = actA
            S.dma("sp", mg[:, 0:8, :], mixT[:, 0:8, tsl])
            S.dma("sp", mg[:, 8:16, :], mixT[:, 8:16, tsl])
            make_gbc(4, 0)
            h2T = actB
            for cb in range(4):
                buf = wb[wi % 2]
                wi += 1
                wv = buf[:, 0:8192].rearrange("p (c n) -> p c n", c=NCH)
                wload(wv, wout_l[:, :, cb * 512:(cb + 1) * 512])
                for ti in range(4):
                    pt = nps()
                    for kc in range(NCH):
                        S.mm(pt[:], lhsT=mg[:, kc, ti * 128:(ti + 1) * 128], rhs=wv[:, kc, :],
                             start=(kc == 0), stop=(kc == NCH - 1))
                    S.act(fsb[:, ti, cb * 512:(cb + 1) * 512], pt[:], AF.Identity)
            for ti in range(4):
                tt_ = tg * 4 + ti
                rows = slice(tt_ * 128, (tt_ + 1) * 128)
                S.dma("sp", xt[:], xsrc[rows, :])
                banks = [fsb[:, ti, j * 512:(j + 1) * 512] for j in range(4)]
                post_norm_residual(banks, xt, xt)
                S.dma("sp", xbuf[rows, :], xt[:])
                norm_to_fm(xt, h2T, ti * 128, 2, 3, 0)
            chk("wout")
            for n in range(DFF // 128):
                if n % 2 == 0:
                    bg = wb[wi % 2]
                    wi += 1
                    wvg2 = bg[:, 0:4096].rearrange("p (c n) -> p c n", c=NCH)
                    wvu2 = bg[:, 4096:8192].rearrange("p (c n) -> p c n", c=NCH)
                    wload(wvg2, wg_l[:, :, n * 128:(n + 2) * 128], nsplit=1)
                    wload(wvu2, wu_l[:, :, n * 128:(n + 2) * 128], nsplit=1)
                wvg = wvg2[:, :, (n % 2) * 128:(n % 2 + 1) * 128]
                wvu = wvu2[:, :, (n % 2) * 128:(n % 2 + 1) * 128]
                pg = nps()
                for kc in range(NCH):
                    S.mm(pg[:], lhsT=wvg[:, kc, :], rhs=h2T[:, kc, :], start=(kc == 0), stop=(kc == NCH - 1))
                pu = nps()
                for kc in range(NCH):
                    S.mm(pu[:], lhsT=wvu[:, kc, :], rhs=h2T[:, kc, :], start=(kc == 0), stop=(kc == NCH - 1))
                sg = stg[n % 2]
                S.act(sg[:], pg[:], AF.Silu)
                S.tt(aT[:, n, :], sg[:], pu[:], ALU.mult)
            chk("gateup")
            make_gbc(5, 0)
            for cb in range(4):
                halves = []
                for hh in range(2):
                    buf = wb[wi % 2]
                    wi += 1
                    wv = buf[:, 0:22 * 256].rearrange("p (c n) -> p c n", c=22)
                    halves.append(wv)
                for half in range(2):
                    c0 = cb * 512 + half * 256
                    for hh in range(2):
                        wload(halves[hh], wd_l[:, hh * 22:(hh + 1) * 22, c0:c0 + 256])
                    for ti in range(4):
                        pt = nps()
                        for kc in range(44):
                            S.mm(pt[:, 0:256], lhsT=aT[:, kc, ti * 128:(ti + 1) * 128],
                                 rhs=halves[kc // 22][:, kc % 22, :], start=(kc == 0), stop=(kc == 43))
                        S.act(fsb[:, ti, c0:c0 + 256], pt[:, 0:256], AF.Identity)
            for ti in range(4):
                tt_ = tg * 4 + ti
                rows = slice(tt_ * 128, (tt_ + 1) * 128)
                S.dma("sp", xt[:], xbuf[rows, :])
                banks = [fsb[:, ti, j * 512:(j + 1) * 512] for j in range(4)]
                post_norm_residual(banks, xt, xt)
                S.dma("sp", ydst[rows, :], xt[:])
            chk("D%d" % tg)
        if cfg.get("sample", True):
            sample_layer(l)
    if cfg.get("sample", True):
        S.dma("sp", y_s.rearrange("(c p) -> p c", p=128), xsam[:], allow_slow_non_contiguous=True)


def _host_consts():
    ident = np.eye(128, dtype=np.float32)
    R = np.zeros((128, 128), np.float32)
    for i in range(16):
        R[i, 16 + i] = -1.0
        R[16 + i, i] = 1.0
    rotT = np.ascontiguousarray(R.T)
    ones = np.ones((128, 128), np.float32)
    consts = np.ascontiguousarray(np.stack([ident, rotT, ones], axis=1))
    half = 16
    inv = (500000.0 ** (-np.arange(half, dtype=np.float32) * 2.0 / 32)).astype(np.float32)
    pos = np.arange(T, dtype=np.float32)
    ang = (pos[None, :] * inv[:, None]).astype(np.float32)
    cos, sin = np.cos(ang).astype(np.float32), np.sin(ang).astype(np.float32)
    C = np.ones((128, T), np.float32)
    Sn = np.zeros((128, T), np.float32)
    C[0:16], C[16:32] = cos, cos
    Sn[0:16], Sn[16:32] = sin, sin
    ropeFM = np.ascontiguousarray(np.stack([C, Sn], axis=1))
    return consts, ropeFM


def _host_tables():
    import ml_dtypes
    bf = ml_dtypes.bfloat16
    t = np.arange(T)
    j = np.arange(32)
    cm = ((j[None, :] * 64 + 63) <= t[:, None]).astype(np.float32)
    cur = t // 64
    forced_v = np.zeros((T, 32), np.float32)
    forced_v[np.arange(T)[cur >= 1], (cur - 1)[cur >= 1]] = 10.0
    forced_v[np.arange(T), cur] = 11.0
    forced_v[:, 0] = 12.0
    future = j[None, :] > cur[:, None]
    keep = ((forced_v == 0) & (~future)).astype(np.float32)
    add = np.where(future, -1.0, forced_v).astype(np.float32)
    tab = np.stack([cm, keep, add], 0).reshape(3, NT, 128, 32).transpose(2, 0, 1, 3)
    cm01T = np.ascontiguousarray(cm.T)
    key = np.arange(128)
    eexp = np.zeros((32, 16, 128), np.float32)
    for kt in range(16):
        eexp[2 * kt + key // 64, kt, key] = 1.0
    gsel = np.zeros((24, 24, 128), np.float32)
    for c in range(24):
        gsel[c, c, :] = 1.0
    tp = np.arange(512)
    caus = np.zeros((128, 4, 512), np.float32)
    for a in range(4):
        caus[:, a, :] = ((a * 128 + key)[:, None] <= tp[None, :])
    wmask = np.zeros((128, 8, 512), np.float32)
    for a in range(8):
        rel = ((a - 4) * 128 + key)[:, None]
        wmask[:, a, :] = (rel <= tp[None, :]) & (rel >= tp[None, :] - 512)
    return dict(tabA=np.ascontiguousarray(tab), cm01T=cm01T, eexp=eexp.astype(bf), gsel=gsel,
                caus=caus.astype(bf), wmask=wmask.astype(bf))


def kernel(x_prompt, x_sample, cache_cmp_kv, cache_sel_kv, cache_win_kv, state_gdn, state_conv, page_table,
           c_prompt, c_sample, w_ada, b_ada, g_pre_mix, w_in, cmp_pe, cmp_w1, cmp_w2, conv_w, gdn_a_log,
           gdn_dt_bias, gdn_norm, w_out, g_post_mix, g_pre_ffn, w_gate, w_up, w_down, g_post_ffn, _cfg=None):
    cfg = _cfg or {}
    f = lambda a: np.ascontiguousarray(np.asarray(a, dtype=np.float32))
    nc, S = build_program(cfg)
    consts, ropeFM = _host_consts()
    badaT = f(np.asarray(b_ada).reshape(DEPTH, 96, 128).transpose(2, 0, 1))
    gains = np.stack([np.asarray(g) for g in (g_pre_mix, g_post_mix, g_pre_ffn, g_post_ffn)], 0)
    gainsT = f(gains.reshape(4, DEPTH, NCH, 128).transpose(3, 0, 1, 2))
    shared = dict(w_ada=f(w_ada), w_in=f(w_in), w_out=f(w_out), w_gate=f(w_gate), w_up=f(w_up),
                  w_down=f(w_down), consts=consts, ropeFM=ropeFM, badaT=badaT, gainsT=gainsT,
                  cmp_peT=f(np.asarray(cmp_pe).transpose(0, 3, 1, 2)), cmp_w1=f(cmp_w1), cmp_w2=f(cmp_w2))
    shared.update(_host_tables())
    k_ = np.arange(64)
    gconst = np.zeros((64, 5, 128), np.float32)
    gconst[:, 0, 0:64] = (k_[:, None] <= k_[None, :])
    gconst[63, 1, :] = 1.0
    gconst[:, 2, 0:64] = (k_[None, :] < k_[:, None])
    gconst[:, 3, 0:64] = (k_[None, :] >= k_[:, None])
    shared["gconst"] = gconst
    shared["convwT"] = f(np.asarray(conv_w).reshape(DEPTH, 4, 24, 128).transpose(3, 0, 2, 1))
    gvec = np.stack([np.asarray(gdn_a_log), np.asarray(gdn_dt_bias)], 1)
    shared["gvec"] = f(np.broadcast_to(gvec[None], (64, DEPTH, 2, 8)))
    shared["gnormb"] = f(np.broadcast_to(np.asarray(gdn_norm)[None], (64, DEPTH, 128)))
    inv = (500000.0 ** (-np.arange(16, dtype=np.float32) * 2.0 / 32)).astype(np.float32)
    angS = (np.float32(16384.0) * inv).astype(np.float32)
    shared["ropeS"] = f(np.stack([np.cos(angS), np.sin(angS)], 0)[None])
    jj = np.arange(128)
    dsel = np.zeros((128, 2, 2, 128), np.float32)
    for jh in range(2):
        for a in range(2):
            for pg in range(128):
                j = 2 * pg + a
                if j // 128 == jh:
                    dsel[j % 128, jh, a, pg] = 1.0
    hsel = np.zeros((128, 2, 128), np.float32)
    hsel[:, 0, 0:64] = 1.0
    hsel[:, 1, 64:128] = 1.0
    shared["dsel"] = dsel
    shared["hsel"] = hsel
    shared["iota"] = np.arange(128, dtype=np.int32)[:, None].copy()
    shared["cache_cmp"] = f(cache_cmp_kv).reshape(-1, 256)
    shared["cache_sel"] = f(cache_sel_kv).reshape(-1, 256)
    cores = cfg.get("cores", list(range(8)))
    in_maps = []
    for i in cores:
        b = i % 4
        cT = np.stack([np.asarray(c_prompt)[b].reshape(NCH, 128).T, np.asarray(c_sample)[i].reshape(NCH, 128).T], -1)
        m = dict(shared)
        m["x_p"] = f(np.asarray(x_prompt)[b])
        m["cT"] = f(cT)
        m["xsT"] = f(np.asarray(x_sample)[i, 0].reshape(NCH, 128).T)
        m["ptab"] = np.ascontiguousarray(np.asarray(page_table)[i][None, :].astype(np.int32))
        m["cache_win"] = f(np.asarray(cache_win_kv)[i]).reshape(DEPTH, 2, 512, 256)
        m["sgdn"] = f(np.asarray(state_gdn)[i])
        m["sconv"] = f(np.asarray(state_conv)[i])
        m["sconvT"] = f(np.asarray(state_conv)[i].reshape(DEPTH, 3, 24, 128).transpose(0, 3, 2, 1))
        in_maps.append(m)
    res = run_bass_kernel_spmd(nc, in_maps, core_ids=list(range(len(cores))))
    if len(cores) < 8:
        return {c: res.results[k] for k, c in enumerate(cores)}
    R = res.results
    B = 4
    y_prompt = np.stack([R[b]["y_p"] for b in range(B)], 0)
    p_cmp = np.stack([R[b]["p_cmp"] for b in range(B)], 0)
    p_sel = np.stack([R[b]["p_sel"] for b in range(B)], 0)
    p_win = np.stack([R[b]["p_win"] for b in range(B)], 0)
    p_conv = np.stack([R[b]["p_conv"] for b in range(B)], 0)
    p_gdn = np.stack([R[b]["p_gdn"] for b in range(B)], 0)
    y_sample = np.stack([R[i]["y_s"].reshape(1, D) for i in range(8)], 0)
    s_cmp = np.stack([R[i]["s_cmp"] for i in range(8)], 0)
    s_sel = np.stack([R[i]["s_sel"] for i in range(8)], 0)
    s_win = np.stack([R[i]["s_win"].reshape(DEPTH, 2, 512, 2, 128) for i in range(8)], 0)
    s_gdn = np.stack([R[i]["s_gdn"] for i in range(8)], 0)
    s_conv = np.stack([R[i]["s_conv"] for i in range(8)], 0)
    return (y_prompt, y_sample, p_cmp, p_sel, p_win, p_gdn, p_conv, s_cmp, s_sel, s_win, s_gdn, s_conv)
```
